# Optimizing a Trainium2 kernel written in Bass

```python
import jax, jax.numpy as jnp
from jax import lax
import numpy as np

D_MODEL = 1024
BATCH = 4
SEQ = 8192
DEPTH = 4

N_EVEN = (DEPTH + 1) // 2
N_ODD = DEPTH // 2
D_A = D_MODEL // 2
A_HEADS = 8
CONV_WIDTH_A = 3
POOL_WINDOWS = (2, 4, 8, 16)
D_B = D_MODEL // 2
B_GROUP = D_B // len(POOL_WINDOWS)
D_IN_AB = 3 * D_A + D_B
CONF_KERNEL = 31
D_FF = 256 * (-(-(8 * D_MODEL) // (3 * 256)))
N_MOD = 6
RMS_EPS = 1e-6
LN_EPS = 1e-5

kernel_name = "hybrid_conv_pool_conformer_encoder"


def rms_norm(x, g):
    xf = x.astype(jnp.float32)
    y = xf * lax.rsqrt(jnp.mean(xf * xf, axis=-1, keepdims=True) + RMS_EPS)
    return (y * g.astype(jnp.float32)).astype(x.dtype)


def layer_norm(x, g, b):
    xf = x.astype(jnp.float32)
    mu = jnp.mean(xf, axis=-1, keepdims=True)
    var = jnp.mean(jnp.square(xf - mu), axis=-1, keepdims=True)
    y = (xf - mu) * lax.rsqrt(var + LN_EPS)
    return (y * g.astype(jnp.float32) + b.astype(jnp.float32)).astype(x.dtype)


def modulate(h, shift, scale):
    return h * (1 + scale[:, None, :]) + shift[:, None, :]


def depthwise_conv(x, w):
    k = w.shape[0]
    left = (k - 1) // 2
    return lax.conv_general_dilated(
        x, w[:, None, :].astype(x.dtype), window_strides=(1,),
        padding=[(left, k - 1 - left)],
        dimension_numbers=('NWC', 'WIO', 'NWC'),
        feature_group_count=x.shape[-1])


def centred_window_mean(x, w):
    L = x.shape[1]
    left = w // 2
    right = w - 1 - left
    xp = jnp.pad(x.astype(jnp.float32), ((0, 0), (left + 1, right), (0, 0)))
    cs = jnp.cumsum(xp, axis=1)
    s = cs[:, w:w + L] - cs[:, :L]
    t = jnp.arange(L)
    cnt = (jnp.minimum(t + right, L - 1) - jnp.maximum(t - left, 0) + 1).astype(jnp.float32)
    return (s / cnt[None, :, None]).astype(x.dtype)


def conv_pool_mixer(h, w_in, conv_a, w_pool, pool_scale, w_out):
    u = jnp.einsum('bsd,de->bse', h, w_in)
    b_gate, c_gate, v, p = jnp.split(u, [D_A, 2 * D_A, 3 * D_A], axis=-1)
    y_a = b_gate * depthwise_conv(c_gate * v, conv_a)
    groups = jnp.split(p, len(POOL_WINDOWS), axis=-1)
    pooled = jnp.stack([centred_window_mean(g, w) - g for g, w in zip(groups, POOL_WINDOWS)],
                       axis=2)
    y_b = jnp.einsum('bsgc,gce->bsge', pooled, w_pool)
    y_b = y_b.reshape(h.shape[0], h.shape[1], D_B) * pool_scale
    return jnp.einsum('bse,ed->bsd', jnp.concatenate([y_a, y_b], axis=-1), w_out)


def conformer_conv(h, w_pw1, b_pw1, w_dw, b_dw, ln_g, ln_b, w_pw2, b_pw2):
    u = jnp.einsum('bsd,de->bse', h, w_pw1) + b_pw1
    a, g = jnp.split(u, 2, axis=-1)
    z = a * jax.nn.sigmoid(g)
    z = depthwise_conv(z, w_dw) + b_dw
    z = jax.nn.silu(layer_norm(z, ln_g, ln_b))
    return jnp.einsum('bsd,de->bse', z, w_pw2) + b_pw2


def swiglu(h, w_gate, w_up, w_down):
    a = jnp.einsum('bsd,df->bsf', h, w_gate)
    b = jnp.einsum('bsd,df->bsf', h, w_up)
    return jnp.einsum('bsf,fd->bsd', jax.nn.silu(a) * b, w_down)


def setup_inputs(seed: int = 0) -> dict:
    key = jax.random.key(seed)
    ks = iter(jax.random.split(key, 32))
    D = D_MODEL
    f32 = jnp.float32

    def nrm(shape, scale):
        return jax.random.normal(next(ks), shape, f32) * scale

    return {
        "x": nrm((BATCH, SEQ, D), 1.0),
        "c": nrm((BATCH, D), 1.0),
        "norm_mix_g": 1.0 + nrm((DEPTH, D), 0.05),
        "norm_ffn_g": 1.0 + nrm((DEPTH, D), 0.05),
        "w_mod": nrm((DEPTH, D, N_MOD * D), 0.5 * D ** -0.5),
        "b_mod": nrm((DEPTH, N_MOD * D), 0.02),
        "ab_w_in": nrm((N_EVEN, D, D_IN_AB), D ** -0.5),
        "ab_conv": nrm((N_EVEN, CONV_WIDTH_A, D_A), CONV_WIDTH_A ** -0.5),
        "ab_w_pool": nrm((N_EVEN, len(POOL_WINDOWS), B_GROUP, B_GROUP), B_GROUP ** -0.5),
        "ab_pool_scale": 1.0 + nrm((N_EVEN, D_B), 0.1),
        "ab_w_out": nrm((N_EVEN, D_A + D_B, D), (D_A + D_B) ** -0.5),
        "cf_w_pw1": nrm((N_ODD, D, 2 * D), D ** -0.5),
        "cf_b_pw1": nrm((N_ODD, 2 * D), 0.02),
        "cf_w_dw": nrm((N_ODD, CONF_KERNEL, D), CONF_KERNEL ** -0.5),
        "cf_b_dw": nrm((N_ODD, D), 0.02),
        "cf_ln_g": 1.0 + nrm((N_ODD, D), 0.05),
        "cf_ln_b": nrm((N_ODD, D), 0.02),
        "cf_w_pw2": nrm((N_ODD, D, D), D ** -0.5),
        "cf_b_pw2": nrm((N_ODD, D), 0.02),
        "ffn_w_gate": nrm((DEPTH, D, D_FF), D ** -0.5),
        "ffn_w_up": nrm((DEPTH, D, D_FF), D ** -0.5),
        "ffn_w_down": nrm((DEPTH, D_FF, D), D_FF ** -0.5),
        "final_norm_g": 1.0 + nrm((D,), 0.05),
    }


def reference(x, c, norm_mix_g, norm_ffn_g, w_mod, b_mod,
              ab_w_in, ab_conv, ab_w_pool, ab_pool_scale, ab_w_out,
              cf_w_pw1, cf_b_pw1, cf_w_dw, cf_b_dw, cf_ln_g, cf_ln_b, cf_w_pw2, cf_b_pw2,
              ffn_w_gate, ffn_w_up, ffn_w_down, final_norm_g):
    c_act = jax.nn.silu(c)
    for layer in range(DEPTH):
        mod = jnp.einsum('bd,de->be', c_act, w_mod[layer]) + b_mod[layer]
        sh1, sc1, g1, sh2, sc2, g2 = jnp.split(mod, N_MOD, axis=-1)
        h = modulate(rms_norm(x, norm_mix_g[layer]), sh1, sc1)
        i = layer // 2
        if layer % 2 == 0:
            y = conv_pool_mixer(h, ab_w_in[i], ab_conv[i], ab_w_pool[i],
                                ab_pool_scale[i], ab_w_out[i])
        else:
            y = conformer_conv(h, cf_w_pw1[i], cf_b_pw1[i], cf_w_dw[i], cf_b_dw[i],
                               cf_ln_g[i], cf_ln_b[i], cf_w_pw2[i], cf_b_pw2[i])
        x = x + g1[:, None, :] * y
        h = modulate(rms_norm(x, norm_ffn_g[layer]), sh2, sc2)
        x = x + g2[:, None, :] * swiglu(h, ffn_w_gate[layer], ffn_w_up[layer], ffn_w_down[layer])
    return rms_norm(x, final_norm_g)
```

```python
import numpy as np
import concourse.bass as bass
import concourse.mybir as mybir
from concourse.bass_utils import run_bass_kernel_spmd

F32 = mybir.dt.float32
BF16 = mybir.dt.bfloat16
AF = mybir.ActivationFunctionType
ALU = mybir.AluOpType

D = 1024
S = 8192
BATCH = 4
DEPTH = 4
DFF = 2816
NCORES = 8
NB = 2
TOK = 2048
HALO = 46
TB = TOK + 2 * HALO
NS = 5
N = TB // NS
KC = 8
NJ = DFF // 128
NG = 2
JG = NJ // NG
RING = 5
EDGE = 48
PEDGE = 32
POFF = 32
ZP = 15
PP = 16

CV = {}
_off = 0


def _cv(name, n):
    global _off
    CV[name] = _off
    _off += n


for _l in range(DEPTH):
    _cv(("gmix", _l), 8)
    _cv(("gffn", _l), 8)
    _cv(("bmod", _l), 48)
for _i in range(2):
    _cv(("conv", _i), 12)
    _cv(("pscale", _i), 4)
for _i in range(2):
    _cv(("bpw1", _i), 16)
    _cv(("wdw", _i), 31 * 8)
    _cv(("bdw", _i), 8)
    _cv(("lng", _i), 8)
    _cv(("lnb", _i), 8)
    _cv(("bpw2", _i), 8)
_cv("gfin", 8)
_cv("c", 8)
NV = _off
EB = 2 * EDGE + 4 * 2 * PEDGE
NE = NB * EB


class Prog:
    ENG = ("pe", "act", "dve", "pool", "sp")

    def __init__(self):
        self.q = {e: [] for e in self.ENG}
        self.cnt = {e: 0 for e in self.ENG}
        self.waited = {e: {} for e in self.ENG}
        self.res = {}
        self.dmacnt = {}
        self.epoch = {}
        self.arena_prefixes = set()

    def _res(self, k):
        r = self.res.get(k)
        if r is None:
            if k[0] in self.arena_prefixes:
                r = [dict(self.epoch), {}]
            else:
                r = [{}, {}]
            self.res[k] = r
        return r

    def new_epoch(self):
        ep = {e: c for e, c in self.cnt.items() if c > 0}
        for s, c in self.dmacnt.items():
            if c > 0 and not s.startswith("sg") and not s.startswith("x"):
                ep[s] = c
        self.epoch = ep
        for k in [k for k in self.res if k[0] in self.arena_prefixes]:
            del self.res[k]

    def _deps(self, reads, writes):
        deps = {}

        def add(d):
            for s, c in d.items():
                if deps.get(s, 0) < c:
                    deps[s] = c
        for k in reads:
            add(self._res(k)[0])
        for k in writes:
            r = self._res(k)
            add(r[0])
            add(r[1])
        return deps

    def _emit_waits(self, eng, deps):
        w = self.waited[eng]
        for s, c in deps.items():
            if w.get(s, 0) < c:
                self.q[eng].append(("wait", s, c))
                w[s] = c

    def _commit(self, tok, reads, writes):
        s, c = tok
        for k in reads:
            r = self._res(k)
            if r[1].get(s, 0) < c:
                r[1][s] = c
        for k in writes:
            r = self._res(k)
            r[0] = {s: c}
            r[1] = {}

    def op(self, eng, fn, reads=(), writes=()):
        deps = self._deps(reads, writes)
        if eng == "pe":
            deps.pop("pe", None)
        self._emit_waits(eng, deps)
        self.cnt[eng] += 1
        tok = (eng, self.cnt[eng])
        self.q[eng].append(("op", fn, eng))
        self._commit(tok, reads, writes)
        return tok

    def dma(self, eng, sem, fn, reads=(), writes=()):
        self._emit_waits(eng, self._deps(reads, writes))
        self.dmacnt[sem] = self.dmacnt.get(sem, 0) + 16
        tok = (sem, self.dmacnt[sem])
        self.q[eng].append(("dma", fn, sem))
        self._commit(tok, reads, writes)
        return tok


def build_program(layers=(0, 1, 2, 3), final_norm=True, nblocks=NB, stop=None):
    nc = bass.Bass("TRN2", target_bir_lowering=False)
    dr = {}

    def din(name, shape):
        dr[name] = nc.dram_tensor(name, list(shape), F32, kind="ExternalInput").ap()
        return dr[name]

    xT = din("xT", [NB, D, TB])
    cvec_d = din("cvec", [128, NV])
    edge_d = din("edge", [128, NE])
    ident_d = din("ident", [128, 128])
    w_mod = din("w_mod", [DEPTH, D, 6 * D])
    ab_w_in = din("ab_w_in", [2, D, 2048])
    ab_w_pool = din("ab_w_pool", [2, 4, 128, 128])
    ab_w_out = din("ab_w_out", [2, D, D])
    cf_w_pw1 = din("cf_w_pw1", [2, D, 2 * D])
    cf_w_pw2 = din("cf_w_pw2", [2, D, D])
    ffn_w_gate = din("ffn_w_gate", [DEPTH, D, DFF])
    ffn_w_up = din("ffn_w_up", [DEPTH, D, DFF])
    ffn_w_down = din("ffn_w_down", [DEPTH, DFF, D])
    yT = nc.dram_tensor("yT", [NB, D, TOK], F32, kind="ExternalOutput").ap()
    dbg_d = nc.dram_tensor("dbg", [128, 8 * N + 192 + 192], F32, kind="ExternalOutput").ap() if stop else None

    import contextlib
    st = contextlib.ExitStack()
    with st:
        def sb(name, shape, dt):
            return st.enter_context(nc.sbuf_tensor(name, list(shape), dt))

        xs = sb("xs", [128, KC, TB], F32)
        hs = sb("hs", [128, KC, TB], BF16)
        ARENA_E = 28928
        arena = sb("arena", [128, ARENA_E], BF16)
        wring = [sb(f"wr{r}", [128, JG, 128], BF16) for r in range(RING)]
        sqr = [sb(f"sq{r}", [128, N], BF16) for r in range(4)]
        cvec = sb("cvecs", [128, NV], F32)
        edge = sb("edges", [128, NE], F32)
        identf = sb("identf", [128, 128], F32)
        identb = sb("identb", [128, 128], BF16)
        onesD = sb("onesD", [128, 128], BF16)
        stg = [sb(f"stg{r}", [128, KC, 128], F32) for r in range(2)]
        gbt = sb("gbt", [128, 8], F32)
        modv = sb("modv", [128, DEPTH, 48], F32)
        dv = sb("dv", [128, DEPTH, 6, 8], F32)
        cact = sb("cact", [128, KC, 2], F32)
        eps_r = sb("eps_r", [128, 1], F32)
        eps_l = sb("eps_l", [128, 1], F32)
        tmpf = [sb(f"tmpf{r}", [128, N], F32) for r in range(4)]
        wmst = [sb(f"wmst{r}", [128, KC, 128], F32) for r in range(2)]
        mtmp = sb("mtmp", [128, 128], F32)
        macc = [sb(f"macc{r}", [128, 128], F32) for r in range(2)]
        onescol = sb("onescol", [128, 2], F32)
        ps = [st.enter_context(nc.psum_tensor(f"ps{b}", [128, 512], F32)) for b in range(8)]

        sems = {}
        for e in Prog.ENG:
            sems[e] = st.enter_context(nc.semaphore(f"s_{e}"))
        dma_sem_names = ["sg0", "sg1"] + [f"x{m}" for m in range(KC)] + \
            ["o0", "o1", "o2", "o3", "wm0", "wm1", "cst"]
        for s_ in dma_sem_names:
            sems[s_] = st.enter_context(nc.semaphore(f"d_{s_}"))

        class Arena:
            def __init__(self):
                self.off = 0

            def reset(self):
                self.off = 0

            def take(self, nelem, dt):
                if dt == F32:
                    self.off = (self.off + 1) // 2 * 2
                    v = arena[:, self.off:self.off + 2 * nelem].bitcast(F32)
                    self.off += 2 * nelem
                else:
                    v = arena[:, self.off:self.off + nelem]
                    self.off += nelem
                assert self.off <= ARENA_E, (self.off, ARENA_E)
                return v
        AR = Arena()

        def run(P, wplan):
            P.arena_prefixes = {"ar"}
            wstate = {"next_load": 0, "next_use": 0, "specs": []}
            tf_i = [0]
            sq_i = [0]
            ring_i = [0]

            def tmp_tile():
                i = tf_i[0] % 4
                tf_i[0] += 1
                return tmpf[i], ("tmpf", i)

            def sq_tile():
                i = sq_i[0] % 4
                sq_i[0] += 1
                return sqr[i], ("sq", i)

            excl4 = [False]

            def bank():
                b = 4 + ring_i[0] % 4
                ring_i[0] += 1
                if excl4[0] and b == 4:
                    b = 4 + ring_i[0] % 4
                    ring_i[0] += 1
                return ps[b], ("ps", b)

            def cv_ap(name, j=0, n=1):
                o = CV[name] + j
                return cvec[:, o:o + n]

            import collections as _c
            bgq = _c.deque()
            BG_RATE = 2

            def drain_bg(n=None):
                k = 0
                while bgq and (n is None or k < n):
                    bgq.popleft()()
                    k += 1

            stg_i = [0]

            def issue_load(j):
                src, kcn = wplan[j]
                slot = j % RING
                k0 = 0
                while k0 < kcn:
                    n_ = min(KC, kcn - k0)
                    si = stg_i[0] % 2
                    stg_i[0] += 1
                    P.dma("sp", f"sg{si}",
                          lambda e, si=si, n_=n_, k0=k0, src=src: e.dma_start(out=stg[si][:, 0:n_, :], in_=src[:, k0:k0 + n_, :]),
                          writes=[("stg", si)])
                    P.op("pool", lambda e, si=si, n_=n_, k0=k0, slot=slot: e.tensor_copy(
                        out=wring[slot][:, k0:k0 + n_, :], in_=stg[si][:, 0:n_, :]),
                        reads=[("stg", si)], writes=[("w", slot)])
                    k0 += n_

            def get_tiles(specs):
                assert len(specs) <= RING
                j0 = wstate["next_use"]
                outs = []
                for (src2d, kcn) in specs:
                    j = wstate["next_use"]
                    wstate["next_use"] += 1
                    src = src2d.rearrange("(kc p) n -> p kc n", p=128)
                    wstate["specs"].append((src, kcn))
                    slot = j % RING
                    outs.append((wring[slot], ("w", slot)))
                if wplan is not None:
                    lim = min(len(wplan), j0 + RING)
                    while wstate["next_load"] < lim:
                        issue_load(wstate["next_load"])
                        wstate["next_load"] += 1
                return outs

            def get_tile(src2d, kcn):
                return get_tiles([(src2d, kcn)])[0]

            def mm_group(pst, pkey, pairs, reads, extra_first=None):
                def fn(e, pairs=pairs, pst=pst):
                    ins = None
                    n_ = len(pairs)
                    for i_, (l_, r_) in enumerate(pairs):
                        ins = e.matmul(pst, l_, r_, start=(i_ == 0), stop=(i_ == n_ - 1))
                    return ins
                tok = P.op("pe", fn, reads=reads, writes=[pkey])
                drain_bg(BG_RATE)
                return tok

            P.dma("sp", "cst", lambda e: e.dma_start(out=cvec[:, :], in_=cvec_d[:, :]), writes=[("cvec",)])
            P.dma("sp", "cst", lambda e: e.dma_start(out=edge[:, :], in_=edge_d[:, :]), writes=[("edge",)])
            P.dma("sp", "cst", lambda e: e.dma_start(out=identf[:, :], in_=ident_d[:, :]), writes=[("identf",)])
            for k in (("cvec",), ("edge",), ("identf",)):
                P.res[k][0] = {"cst": P.dmacnt["cst"]}
            P.op("dve", lambda e: e.tensor_copy(out=identb[:, :], in_=identf[:, :]), reads=[("identf",)], writes=[("identb",)])
            P.op("dve", lambda e: e.memset(onesD[:, :], 1.0 / D), writes=[("onesD",)])
            P.op("dve", lambda e: e.memset(eps_r[:, :], 1e-6), writes=[("eps",)])
            P.op("dve", lambda e: e.memset(eps_l[:, :], 1e-5), writes=[("eps",)])
            for d_ in range(2):
                P.op("act", lambda e, d_=d_: e.activation(out=cact[:, :, d_], in_=cv_ap("c", 0, 8), func=AF.Silu),
                     reads=[("cvec",)], writes=[("cact", d_)])

            P.op("dve", lambda e: e.memset(onescol[:, :], 1.0), writes=[("onescol",)])
            mod_i = [0]

            def enqueue_mod(l):
                pending_pe = []
                for q in range(48):
                    gi_ = mod_i[0]
                    mod_i[0] += 1
                    slot = gi_ % 2
                    src = w_mod[l, :, q * 128:(q + 1) * 128].rearrange("(kc p) n -> p kc n", p=128)
                    bgq.append(lambda slot=slot, src=src: P.dma(
                        "sp", f"wm{slot}", lambda e: e.dma_start(out=wmst[slot][:, :, :], in_=src), writes=[("wmst", slot)]))
                    for kc in range(KC):
                        if kc == 0:
                            bgq.append(lambda slot=slot: P.op("pool", lambda e: e.tensor_scalar(
                                out=macc[slot][:, :], in0=wmst[slot][:, 0, :], scalar1=cact[:, 0, 0:1], scalar2=1.0,
                                op0=ALU.mult, op1=ALU.mult), reads=[("wmst", slot), ("cact", 0)], writes=[("macc", slot)]))
                        else:
                            bgq.append(lambda slot=slot, kc=kc: P.op("pool", lambda e: e.tensor_scalar(
                                out=mtmp[:, :], in0=wmst[slot][:, kc, :], scalar1=cact[:, kc, 0:1], scalar2=1.0,
                                op0=ALU.mult, op1=ALU.mult), reads=[("wmst", slot), ("cact", 0)], writes=[("mtmp",)]))
                            bgq.append(lambda slot=slot: P.op("pool", lambda e: e.tensor_tensor(
                                out=macc[slot][:, :], in0=macc[slot][:, :], in1=mtmp[:, :], op=ALU.add),
                                reads=[("mtmp",), ("macc", slot)], writes=[("macc", slot)]))

                    def pe_thunk(slot=slot, q=q, l=l):
                        pst, pkey = bank()
                        P.op("pe", lambda e: e.matmul(pst[:, 0:2], macc[slot][:, :], onescol[:, :], start=True, stop=True),
                             reads=[("macc", slot), ("onescol",)], writes=[pkey])
                        P.op("dve", lambda e: e.tensor_tensor(out=modv[:, l, q:q + 1], in0=pst[:, 0:1],
                                                             in1=cv_ap(("bmod", l), q, 1), op=ALU.add),
                             reads=[pkey, ("cvec",)], writes=[("modv", l)])
                    pending_pe.append(pe_thunk)
                    if len(pending_pe) > 1:
                        bgq.append(pending_pe.pop(0))
                bgq.append(pending_pe.pop(0))

                def fin(l=l):
                    for (di, mo, gname) in ((0, 8, "gmix"), (3, 32, "gffn")):
                        P.op("dve", lambda e, di=di, mo=mo, gname=gname: e.scalar_tensor_tensor(
                            out=dv[:, l, di, :], in0=modv[:, l, mo:mo + 8], scalar=1.0,
                            in1=cv_ap((gname, l), 0, 8), op0=ALU.add, op1=ALU.mult),
                            reads=[("modv", l), ("cvec",)], writes=[("dv", l, di)])
                    for (di, mo) in ((1, 0), (2, 16), (4, 24), (5, 40)):
                        P.op("dve", lambda e, di=di, mo=mo: e.tensor_copy(out=dv[:, l, di, :], in_=modv[:, l, mo:mo + 8]),
                             reads=[("modv", l)], writes=[("dv", l, di)])
                bgq.append(fin)

            def mod_prologue(l):
                P.new_epoch()
                AR.reset()
                NSL = 6
                Wst = [AR.take(KC * 128, F32).rearrange("p (a b) -> p a b", a=KC) for _ in range(NSL)]
                Acc = [AR.take(128, F32) for _ in range(NSL)]
                pat = ("dve", "dve", "pool", "dve", "dve", "pool", "dve")
                pend = []
                for q in range(48):
                    sl = q % NSL
                    eng = pat[q % len(pat)]
                    src = w_mod[l, :, q * 128:(q + 1) * 128].rearrange("(kc p) n -> p kc n", p=128)
                    P.dma("sp", ("wm0", "wm1", "o0", "o1", "o2", "o3")[sl], lambda e, sl=sl, src=src: e.dma_start(out=Wst[sl][:, :, :], in_=src),
                          writes=[("ar", "wst", sl)])
                    for kc in range(KC):
                        if eng == "dve":
                            if kc == 0:
                                P.op("dve", lambda e, sl=sl: e.tensor_scalar(
                                    out=Acc[sl][:, :], in0=Wst[sl][:, 0, :], scalar1=cact[:, 0, 0:1], scalar2=None, op0=ALU.mult),
                                    reads=[("ar", "wst", sl), ("cact", 0)], writes=[("ar", "acc", sl)])
                            else:
                                P.op("dve", lambda e, sl=sl, kc=kc: e.scalar_tensor_tensor(
                                    out=Acc[sl][:, :], in0=Wst[sl][:, kc, :], scalar=cact[:, kc, 0:1], in1=Acc[sl][:, :],
                                    op0=ALU.mult, op1=ALU.add),
                                    reads=[("ar", "wst", sl), ("cact", 0), ("ar", "acc", sl)], writes=[("ar", "acc", sl)])
                        else:
                            if kc == 0:
                                P.op("pool", lambda e, sl=sl: e.tensor_scalar(
                                    out=Acc[sl][:, :], in0=Wst[sl][:, 0, :], scalar1=cact[:, 0, 0:1], scalar2=1.0,
                                    op0=ALU.mult, op1=ALU.mult),
                                    reads=[("ar", "wst", sl), ("cact", 0)], writes=[("ar", "acc", sl)])
                            else:
                                P.op("pool", lambda e, sl=sl, kc=kc: e.tensor_scalar(
                                    out=mtmp[:, :], in0=Wst[sl][:, kc, :], scalar1=cact[:, kc, 0:1], scalar2=1.0,
                                    op0=ALU.mult, op1=ALU.mult), reads=[("ar", "wst", sl), ("cact", 0)], writes=[("mtmp",)])
                                P.op("pool", lambda e, sl=sl: e.tensor_tensor(
                                    out=Acc[sl][:, :], in0=Acc[sl][:, :], in1=mtmp[:, :], op=ALU.add),
                                    reads=[("mtmp",), ("ar", "acc", sl)], writes=[("ar", "acc", sl)])

                    def pe_part(sl=sl, q=q):
                        pst, pkey = bank()
                        P.op("pe", lambda e: e.matmul(pst[:, 0:2], Acc[sl][:, :], onescol[:, :], start=True, stop=True),
                             reads=[("ar", "acc", sl), ("onescol",)], writes=[pkey])
                        P.op("dve", lambda e: e.tensor_tensor(out=modv[:, l, q:q + 1], in0=pst[:, 0:1],
                                                             in1=cv_ap(("bmod", l), q, 1), op=ALU.add),
                             reads=[pkey, ("cvec",)], writes=[("modv", l)])
                    pend.append(pe_part)
                    if len(pend) > 3:
                        pend.pop(0)()
                while pend:
                    pend.pop(0)()
                for (di, mo, gname) in ((0, 8, "gmix"), (3, 32, "gffn")):
                    P.op("dve", lambda e, di=di, mo=mo, gname=gname: e.scalar_tensor_tensor(
                        out=dv[:, l, di, :], in0=modv[:, l, mo:mo + 8], scalar=1.0,
                        in1=cv_ap((gname, l), 0, 8), op0=ALU.add, op1=ALU.mult),
                        reads=[("modv", l), ("cvec",)], writes=[("dv", l, di)])
                for (di, mo) in ((1, 0), (2, 16), (4, 24), (5, 40)):
                    P.op("dve", lambda e, di=di, mo=mo: e.tensor_copy(out=dv[:, l, di, :], in_=modv[:, l, mo:mo + 8]),
                         reads=[("modv", l)], writes=[("dv", l, di)])

            mod_prologue(layers[0])

            def dvk(l):
                return [("dv", l, i_) for i_ in range(6)]

            def cols(s):
                return slice(s * N, (s + 1) * N)

            def stats_rstd(src_fn, src_keys, s, eps_t):
                pairs = []
                rk = []
                for m in range(KC):
                    sqt, sqk = sq_tile()
                    P.op("act", lambda e, m=m, sqt=sqt: e.activation(out=sqt[:, :], in_=src_fn(m, s), func=AF.Square),
                         reads=[src_keys(m, s)], writes=[sqk])
                    P.op("pe", lambda e, m=m, sqt=sqt: e.matmul(ps[0][:, :N], onesD[:, :], sqt[:, :], start=(m == 0), stop=(m == KC - 1)),
                         reads=[sqk, ("onesD",)], writes=[("ps", 0)])
                tt, tk = tmp_tile()
                P.op("act", lambda e, tt=tt: e.activation(out=tt[:, :], in_=ps[0][:, :N], func=AF.Ln, bias=eps_t[:, 0:1], scale=1.0),
                     reads=[("ps", 0), ("eps",)], writes=[tk])
                P.op("act", lambda e, tt=tt: e.activation(out=ps[2][:, :N], in_=tt[:, :], func=AF.Exp, scale=-0.5),
                     reads=[tk], writes=[("ps", 2)])

            def acc_bank(s):
                return s if s < 4 else 4

            def modulate(l, di_a, di_b, s, rb):
                for m in range(KC):
                    tt, tk = tmp_tile()
                    P.op("dve", lambda e, m=m, s=s, tt=tt: e.scalar_tensor_tensor(
                        out=tt[:, :], in0=xs[:, m, cols(s)], scalar=dv[:, l, di_a, m:m + 1], in1=ps[rb][:, :N],
                        op0=ALU.mult, op1=ALU.mult),
                        reads=[("x", m, s), ("ps", rb)] + dvk(l), writes=[tk])
                    if m % 2 == 0:
                        P.op("pool", lambda e, m=m, s=s, tt=tt: e.tensor_scalar(
                            out=hs[:, m, cols(s)], in0=tt[:, :], scalar1=1.0, scalar2=dv[:, l, di_b, m:m + 1],
                            op0=ALU.mult, op1=ALU.add),
                            reads=[tk] + dvk(l), writes=[("h", m, s)])
                    else:
                        P.op("act", lambda e, m=m, s=s, tt=tt: e.activation(
                            out=hs[:, m, cols(s)], in_=tt[:, :], func=AF.Identity, bias=dv[:, l, di_b, m:m + 1], scale=1.0),
                            reads=[tk] + dvk(l), writes=[("h", m, s)])

            def rstd_from_acc(s):
                rb = acc_bank(s)
                tt, tk = tmp_tile()
                P.op("act", lambda e, tt=tt: e.activation(out=tt[:, :], in_=ps[rb][:, :N], func=AF.Ln, bias=eps_r[:, 0:1], scale=1.0),
                     reads=[("ps", rb), ("eps",)], writes=[tk])
                P.op("act", lambda e, tt=tt: e.activation(out=ps[rb][:, :N], in_=tt[:, :], func=AF.Exp, scale=-0.5),
                     reads=[tk], writes=[("ps", rb)])
                return rb

            def norm_mod(l, di_a, di_b, have_stats=False):
                for s in range(NS):
                    if have_stats:
                        rb = rstd_from_acc(s)
                    else:
                        stats_rstd(lambda m, s_: xs[:, m, cols(s_)], lambda m, s_: ("x", m, s_), s, eps_r)
                        rb = 2
                    modulate(l, di_a, di_b, s, rb)
                excl4[0] = False

            def edge_mask(eng, buf_fn, key_fn, blk):
                eo = blk * EB
                P.op(eng, lambda e: getattr(e, "tensor_tensor")(out=buf_fn(0, EDGE), in0=buf_fn(0, EDGE),
                                                               in1=edge[:, eo:eo + EDGE], op=ALU.mult),
                     reads=[("edge",), key_fn(0)], writes=[key_fn(0)])
                P.op(eng, lambda e: getattr(e, "tensor_tensor")(out=buf_fn(TB - EDGE, TB), in0=buf_fn(TB - EDGE, TB),
                                                               in1=edge[:, eo + EDGE:eo + 2 * EDGE], op=ALU.mult),
                     reads=[("edge",), key_fn(NS - 1)], writes=[key_fn(NS - 1)])

            def proj_residual(l, wsrc_fn, kcn, rhs_fn, rhs_keys, gi, bias_fn=None, acc_stats=False):
                pend = []
                if acc_stats:
                    excl4[0] = True
                for m in range(KC):
                    wt, wk = get_tile(wsrc_fn(m), kcn)
                    for s in range(NS):
                        if len(pend) > 2:
                            pend.pop(0)()
                        pst, pkey = bank()
                        pairs = [(wt[:, k, :], rhs_fn(k, s)) for k in range(kcn)]
                        reads = [wk] + [rhs_keys(k, s) for k in range(kcn)]
                        mm_group(pst[:, :N], pkey, pairs, reads)
                        P.op("dve", lambda e, m=m, s=s, pst=pst: e.scalar_tensor_tensor(
                            out=xs[:, m, cols(s)], in0=pst[:, :N], scalar=dv[:, l, gi, m:m + 1], in1=xs[:, m, cols(s)],
                            op0=ALU.mult, op1=ALU.add),
                            reads=[pkey, ("x", m, s)] + dvk(l), writes=[("x", m, s)])
                        if acc_stats:
                            sqt, sqk = sq_tile()
                            P.op("act", lambda e, m=m, s=s, sqt=sqt: e.activation(out=sqt[:, :], in_=xs[:, m, cols(s)], func=AF.Square),
                                 reads=[("x", m, s)], writes=[sqk])
                            ab_ = acc_bank(s)
                            pend.append(lambda m=m, sqt=sqt, sqk=sqk, ab_=ab_: P.op(
                                "pe", lambda e: e.matmul(ps[ab_][:, :N], onesD[:, :], sqt[:, :], start=(m == 0), stop=(m == KC - 1)),
                                reads=[sqk, ("onesD",)], writes=[("ps", ab_)]))
                while pend:
                    pend.pop(0)()

            def mixer_even(l, blk):
                i = l // 2
                P.new_epoch()
                AR.reset()
                ybuf = AR.take(KC * TB, BF16).rearrange("p (a b) -> p a b", a=KC)
                cvb = [AR.take(TB + 2, BF16) for _ in range(2)]
                pbuf = AR.take(TB + 2 * PP, F32)
                T0 = AR.take(N + 2 * PP + 16, F32)
                T1 = AR.take(N + 2 * PP + 16, F32)
                pooled = [AR.take(N, BF16) for _ in range(2)]
                dg3 = AR.take(3 * 128, BF16).rearrange("p (a b) -> p a b", a=3)
                for c_ in range(2):
                    P.op("dve", lambda e, c_=c_: e.memset(cvb[c_][:, 0:1], 0.0), writes=[("ar", "cvpadl", c_)])
                    P.op("dve", lambda e, c_=c_: e.memset(cvb[c_][:, TB + 1:TB + 2], 0.0), writes=[("ar", "cvpadr", c_)])
                P.op("dve", lambda e: e.memset(pbuf[:, 0:PP], 0.0), writes=[("ar", "ppadl")])
                P.op("dve", lambda e: e.memset(pbuf[:, PP + TB:PP + TB + PP], 0.0), writes=[("ar", "ppadr")])
                win = ab_w_in[i]
                for a in range(4):
                    cb = cvb[a % 2]
                    (wc, wck), (wv, wvk), (wb, wbk) = get_tiles([
                        (win[:, 512 + a * 128:512 + (a + 1) * 128], KC),
                        (win[:, 1024 + a * 128:1024 + (a + 1) * 128], KC),
                        (win[:, a * 128:(a + 1) * 128], KC)])
                    for s in range(NS):
                        pC, pCk = bank()
                        pV, pVk = bank()
                        hk = [("h", k, s) for k in range(KC)]
                        mm_group(pC[:, :N], pCk, [(wc[:, k, :], hs[:, k, cols(s)]) for k in range(KC)], [wck] + hk)
                        mm_group(pV[:, :N], pVk, [(wv[:, k, :], hs[:, k, cols(s)]) for k in range(KC)], [wvk] + hk)
                        tt, tk = tmp_tile()
                        P.op("act", lambda e, tt=tt, pC=pC: e.activation(out=tt[:, :], in_=pC[:, :N], func=AF.Copy),
                             reads=[pCk], writes=[tk])
                        P.op("dve", lambda e, tt=tt, pV=pV, cb=cb, s=s: e.tensor_tensor(
                            out=cb[:, 1 + s * N:1 + (s + 1) * N], in0=tt[:, :], in1=pV[:, :N], op=ALU.mult),
                            reads=[tk, pVk], writes=[("ar", "cv", a % 2, s)])
                    edge_mask("dve", lambda c0, c1, cb=cb: cb[:, 1 + c0:1 + c1], lambda s_, a=a: ("ar", "cv", a % 2, s_), blk)
                    for k in range(3):
                        P.op("dve", lambda e, k=k, a=a: e.tensor_scalar(
                            out=dg3[:, k, :], in0=identb[:, :], scalar1=cv_ap(("conv", i), k * 4 + a, 1), scalar2=None,
                            op0=ALU.mult), reads=[("identb",), ("cvec",)], writes=[("ar", "dg3", k)])
                    for s in range(NS):
                        pY, pYk = bank()
                        pB, pBk = bank()
                        rk = [("ar", "cv", a % 2, s_) for s_ in (s - 1, s, s + 1) if 0 <= s_ < NS]
                        rk += [("ar", "cvpadl", a % 2), ("ar", "cvpadr", a % 2)] + [("ar", "dg3", k) for k in range(3)]
                        mm_group(pY[:, :N], pYk, [(dg3[:, k, :], cb[:, s * N + k:s * N + k + N]) for k in range(3)], rk)
                        mm_group(pB[:, :N], pBk, [(wb[:, k, :], hs[:, k, cols(s)]) for k in range(KC)],
                                 [wbk] + [("h", k, s) for k in range(KC)])
                        tt, tk = tmp_tile()
                        P.op("act", lambda e, tt=tt, pY=pY: e.activation(out=tt[:, :], in_=pY[:, :N], func=AF.Copy),
                             reads=[pYk], writes=[tk])
                        P.op("dve", lambda e, tt=tt, pB=pB, a=a, s=s: e.tensor_tensor(
                            out=ybuf[:, a, cols(s)], in0=tt[:, :], in1=pB[:, :N], op=ALU.mult),
                            reads=[tk, pBk], writes=[("ar", "y", a, s)])
                for g in range(4):
                    w_ = 2 << g
                    (wp, wpk), (wpl, wplk) = get_tiles([(win[:, 1536 + g * 128:1536 + (g + 1) * 128], KC),
                                                        (ab_w_pool[i, g], 1)])
                    for s in range(NS):
                        pP, pPk = bank()
                        mm_group(pP[:, :N], pPk, [(wp[:, k, :], hs[:, k, cols(s)]) for k in range(KC)],
                                 [wpk] + [("h", k, s) for k in range(KC)])
                        P.op("act", lambda e, pP=pP, s=s: e.activation(out=pbuf[:, PP + s * N:PP + (s + 1) * N], in_=pP[:, :N], func=AF.Copy),
                             reads=[pPk], writes=[("ar", "p", s)])
                    edge_mask("dve", lambda c0, c1: pbuf[:, PP + c0:PP + c1], lambda s_: ("ar", "p", s_), blk)
                    for s in range(NS):
                        lo = PP + s * N
                        base = lo - PP
                        L = N + 2 * PP
                        rkeys = [("ar", "p", s_) for s_ in (s - 1, s, s + 1) if 0 <= s_ < NS] + [("ar", "ppadl"), ("ar", "ppadr")]
                        src = pbuf
                        srcoff = base
                        srckey = rkeys
                        Ts = [T0, T1]
                        d = 1
                        for step in range(g + 1):
                            dst = Ts[step % 2]
                            P.op("dve", lambda e, dst=dst, src=src, srcoff=srcoff, d=d, L=L: e.tensor_tensor(
                                out=dst[:, d:L], in0=src[:, srcoff + d:srcoff + L], in1=src[:, srcoff:srcoff + L - d], op=ALU.add),
                                reads=srckey, writes=[("ar", "T", step % 2)])
                            src = dst
                            srcoff = 0
                            srckey = [("ar", "T", step % 2)]
                            d *= 2
                        o_ = PP + w_ // 2 - 1
                        pl = pooled[s % 2]
                        plk = ("ar", "pooled", s % 2)
                        P.op("dve", lambda e, src=src, o_=o_, pl=pl, s=s, w_=w_: e.scalar_tensor_tensor(
                            out=pl[:, :], in0=src[:, o_:o_ + N], scalar=1.0 / w_, in1=pbuf[:, PP + s * N:PP + (s + 1) * N],
                            op0=ALU.mult, op1=ALU.subtract), reads=srckey + [("ar", "p", s)], writes=[plk])
                        eo = blk * EB + 2 * EDGE + g * 2 * PEDGE
                        if s == 0 or s == NS - 1:
                            c0 = POFF if s == 0 else N - POFF - PEDGE
                            et = edge[:, eo:eo + PEDGE] if s == 0 else edge[:, eo + PEDGE:eo + 2 * PEDGE]
                            tt, tk = tmp_tile()
                            P.op("dve", lambda e, src=src, o_=o_, c0=c0, et=et, tt=tt: e.tensor_tensor(
                                out=tt[:, 0:PEDGE], in0=src[:, o_ + c0:o_ + c0 + PEDGE], in1=et, op=ALU.mult),
                                reads=srckey + [("edge",)], writes=[tk])
                            P.op("dve", lambda e, c0=c0, tt=tt, pl=pl, s=s: e.tensor_tensor(
                                out=pl[:, c0:c0 + PEDGE], in0=tt[:, 0:PEDGE],
                                in1=pbuf[:, PP + s * N + c0:PP + s * N + c0 + PEDGE], op=ALU.subtract),
                                reads=[tk, ("ar", "p", s)], writes=[plk])
                        pO, pOk = bank()
                        mm_group(pO[:, :N], pOk, [(wpl[:, 0, :], pl[:, :])], [wplk, plk])
                        P.op("act", lambda e, pO=pO, g=g, s=s: e.activation(
                            out=ybuf[:, 4 + g, cols(s)], in_=pO[:, :N], func=AF.Identity, scale=cv_ap(("pscale", i), g, 1)),
                            reads=[pOk, ("cvec",)], writes=[("ar", "y", 4 + g, s)])
                proj_residual(l, lambda m: ab_w_out[i][:, m * 128:(m + 1) * 128], KC,
                              lambda k, s: ybuf[:, k, cols(s)], lambda k, s: ("ar", "y", k, s), 2, acc_stats=True)

            def mixer_odd(l, blk):
                i = l // 2
                P.new_epoch()
                AR.reset()
                zb = AR.take(KC * (TB + 2 * ZP), BF16).rearrange("p (a b) -> p a b", a=KC)
                dgs = [AR.take(31 * 128, BF16).rearrange("p (a b) -> p a b", a=31) for _ in range(2)]
                for a in range(KC):
                    P.op("dve", lambda e, a=a: e.memset(zb[:, a, 0:ZP], 0.0), writes=[("ar", "zpadl", a)])
                    P.op("dve", lambda e, a=a: e.memset(zb[:, a, ZP + TB:ZP + TB + ZP], 0.0), writes=[("ar", "zpadr", a)])
                w1 = cf_w_pw1[i]
                for a in range(KC):
                    (wa, wak), (wg, wgk) = get_tiles([(w1[:, a * 128:(a + 1) * 128], KC),
                                                      (w1[:, D + a * 128:D + (a + 1) * 128], KC)])
                    for s in range(NS):
                        pA, pAk = bank()
                        pG, pGk = bank()
                        hk = [("h", k, s) for k in range(KC)]
                        mm_group(pA[:, :N], pAk, [(wa[:, k, :], hs[:, k, cols(s)]) for k in range(KC)], [wak] + hk)
                        mm_group(pG[:, :N], pGk, [(wg[:, k, :], hs[:, k, cols(s)]) for k in range(KC)], [wgk] + hk)
                        tt, tk = tmp_tile()
                        P.op("act", lambda e, tt=tt, pG=pG, a=a: e.activation(
                            out=tt[:, :], in_=pG[:, :N], func=AF.Sigmoid, bias=cv_ap(("bpw1", i), 8 + a, 1), scale=1.0),
                            reads=[pGk, ("cvec",)], writes=[tk])
                        P.op("dve", lambda e, tt=tt, pA=pA, a=a, s=s: e.scalar_tensor_tensor(
                            out=zb[:, a, ZP + s * N:ZP + (s + 1) * N], in0=pA[:, :N], scalar=cv_ap(("bpw1", i), a, 1),
                            in1=tt[:, :], op0=ALU.add, op1=ALU.mult),
                            reads=[pAk, tk, ("cvec",)], writes=[("ar", "z", a, s)])
                    edge_mask("dve", lambda c0, c1, a=a: zb[:, a, ZP + c0:ZP + c1], lambda s_, a=a: ("ar", "z", a, s_), blk)
                for a in range(KC):
                    dgt = dgs[a % 2]
                    for k in range(31):
                        P.op("dve", lambda e, k=k, a=a, dgt=dgt: e.tensor_scalar(
                            out=dgt[:, k, :], in0=identb[:, :], scalar1=cv_ap(("wdw", i), k * 8 + a, 1), scalar2=None,
                            op0=ALU.mult), reads=[("identb",), ("cvec",)], writes=[("ar", "dg", a % 2, k)])
                    for s in range(NS):
                        pZ, pZk = bank()
                        rk = [("ar", "z", a, s_) for s_ in (s - 1, s, s + 1) if 0 <= s_ < NS]
                        rk += [("ar", "zpadl", a), ("ar", "zpadr", a)] + [("ar", "dg", a % 2, k) for k in range(31)]
                        mm_group(pZ[:, :N], pZk, [(dgt[:, k, :], zb[:, a, s * N + k:s * N + k + N]) for k in range(31)], rk)
                        P.op("act", lambda e, pZ=pZ, a=a, s=s: e.activation(
                            out=hs[:, a, cols(s)], in_=pZ[:, :N], func=AF.Identity, bias=cv_ap(("bdw", i), a, 1), scale=1.0),
                            reads=[pZk, ("cvec",)], writes=[("h", a, s)])
                for s in range(NS):
                    for a in range(KC):
                        sqt, sqk = sq_tile()
                        P.op("dve", lambda e, a=a, s=s, sqt=sqt: e.tensor_tensor(
                            out=sqt[:, :], in0=hs[:, a, cols(s)], in1=hs[:, a, cols(s)], op=ALU.mult),
                            reads=[("h", a, s)], writes=[sqk])
                        P.op("pe", lambda e, a=a, s=s: e.matmul(ps[0][:, :N], onesD[:, :], hs[:, a, cols(s)], start=(a == 0), stop=(a == KC - 1)),
                             reads=[("h", a, s), ("onesD",)], writes=[("ps", 0)])
                        P.op("pe", lambda e, a=a, sqt=sqt: e.matmul(ps[1][:, :N], onesD[:, :], sqt[:, :], start=(a == 0), stop=(a == KC - 1)),
                             reads=[sqk, ("onesD",)], writes=[("ps", 1)])
                    t1, t1k = tmp_tile()
                    P.op("act", lambda e, t1=t1: e.activation(out=t1[:, :], in_=ps[0][:, :N], func=AF.Square),
                         reads=[("ps", 0)], writes=[t1k])
                    t2, t2k = tmp_tile()
                    P.op("dve", lambda e, t1=t1, t2=t2: e.tensor_tensor(out=t2[:, :], in0=ps[1][:, :N], in1=t1[:, :], op=ALU.subtract),
                         reads=[("ps", 1), t1k], writes=[t2k])
                    P.op("act", lambda e, t1=t1, t2=t2: e.activation(out=t1[:, :], in_=t2[:, :], func=AF.Ln, bias=eps_l[:, 0:1], scale=1.0),
                         reads=[t2k, ("eps",)], writes=[t1k])
                    P.op("act", lambda e, t1=t1: e.activation(out=ps[2][:, :N], in_=t1[:, :], func=AF.Exp, scale=-0.5),
                         reads=[t1k], writes=[("ps", 2)])
                    P.op("act", lambda e, t2=t2: e.activation(out=t2[:, :], in_=ps[2][:, :N], func=AF.Copy),
                         reads=[("ps", 2)], writes=[t2k])
                    P.op("dve", lambda e, t2=t2: e.scalar_tensor_tensor(
                        out=ps[3][:, :N], in0=ps[0][:, :N], scalar=-1.0, in1=t2[:, :], op0=ALU.mult, op1=ALU.mult),
                        reads=[("ps", 0), t2k], writes=[("ps", 3)])
                    for a in range(KC):
                        ta, tak = tmp_tile()
                        P.op("dve", lambda e, a=a, s=s, ta=ta: e.tensor_tensor(
                            out=ta[:, :], in0=hs[:, a, cols(s)], in1=ps[2][:, :N], op=ALU.mult),
                            reads=[("h", a, s), ("ps", 2)], writes=[tak])
                        P.op("dve", lambda e, ta=ta: e.tensor_tensor(out=ta[:, :], in0=ta[:, :], in1=ps[3][:, :N], op=ALU.add),
                             reads=[tak, ("ps", 3)], writes=[tak])
                        P.op("act", lambda e, a=a, s=s, ta=ta: e.activation(
                            out=zb[:, a, ZP + s * N:ZP + (s + 1) * N], in_=ta[:, :], func=AF.Silu,
                            bias=cv_ap(("lnb", i), a, 1), scale=cv_ap(("lng", i), a, 1)),
                            reads=[tak, ("cvec",)], writes=[("ar", "z", a, s)])
                P.op("dve", lambda e: e.tensor_tensor(out=gbt[:, :], in0=dv[:, l, 2, :], in1=cv_ap(("bpw2", i), 0, 8), op=ALU.mult),
                     reads=dvk(l) + [("cvec",)], writes=[("gbt",)])
                for m in range(KC):
                    P.op("pool", lambda e, m=m: e.tensor_scalar(
                        out=xs[:, m, :], in0=xs[:, m, :], scalar1=1.0, scalar2=gbt[:, m:m + 1], op0=ALU.mult, op1=ALU.add),
                        reads=[("gbt",)] + [("x", m, s_) for s_ in range(NS)], writes=[("x", m, s_) for s_ in range(NS)])
                proj_residual(l, lambda m: cf_w_pw2[i][:, m * 128:(m + 1) * 128], KC,
                              lambda k, s: zb[:, k, ZP + s * N:ZP + (s + 1) * N], lambda k, s: ("ar", "z", k, s), 2,
                              acc_stats=True)

            def ffn(l):
                P.new_epoch()
                AR.reset()
                ab = AR.take(JG * TB, BF16).rearrange("p (a b) -> p a b", a=JG)
                for hf in range(NG):
                    for jj in range(JG):
                        j = hf * JG + jj
                        (wgt, wgk), (wut, wuk) = get_tiles([(ffn_w_gate[l][:, j * 128:(j + 1) * 128], KC),
                                                            (ffn_w_up[l][:, j * 128:(j + 1) * 128], KC)])
                        for s in range(NS):
                            pG, pGk = bank()
                            pU, pUk = bank()
                            hk = [("h", k, s) for k in range(KC)]
                            mm_group(pG[:, :N], pGk, [(wgt[:, k, :], hs[:, k, cols(s)]) for k in range(KC)], [wgk] + hk)
                            mm_group(pU[:, :N], pUk, [(wut[:, k, :], hs[:, k, cols(s)]) for k in range(KC)], [wuk] + hk)
                            tt, tk = tmp_tile()
                            P.op("act", lambda e, tt=tt, pG=pG: e.activation(out=tt[:, :], in_=pG[:, :N], func=AF.Silu),
                                 reads=[pGk], writes=[tk])
                            P.op("dve", lambda e, tt=tt, pU=pU, jj=jj, s=s: e.tensor_tensor(
                                out=ab[:, jj, cols(s)], in0=tt[:, :], in1=pU[:, :N], op=ALU.mult),
                                reads=[tk, pUk], writes=[("ar", "a", jj, s)])
                    proj_residual(l, lambda m, hf=hf: ffn_w_down[l][hf * JG * 128:(hf + 1) * JG * 128, m * 128:(m + 1) * 128], JG,
                                  lambda k, s: ab[:, k, cols(s)], lambda k, s: ("ar", "a", k, s), 5,
                                  acc_stats=(hf == NG - 1))

            def final_out(blk):
                P.new_epoch()
                AR.reset()
                ot = [AR.take(N, F32) for _ in range(4)]
                oi = 0
                for s in range(NS):
                    if len(layers) > 0 and stop is None:
                        rb = rstd_from_acc(s)
                    else:
                        stats_rstd(lambda m, s_: xs[:, m, cols(s_)], lambda m, s_: ("x", m, s_), s, eps_r)
                        rb = 2
                    lo = max(s * N, HALO)
                    hi = min((s + 1) * N, HALO + TOK)
                    for m in range(KC):
                        o = oi % 4
                        oi += 1
                        P.op("dve", lambda e, m=m, s=s, o=o, rb=rb: e.scalar_tensor_tensor(
                            out=ot[o][:, :], in0=xs[:, m, cols(s)], scalar=cv_ap("gfin", m, 1), in1=ps[rb][:, :N],
                            op0=ALU.mult, op1=ALU.mult),
                            reads=[("x", m, s), ("ps", rb), ("cvec",)], writes=[("ar", "ot", o)])
                        P.dma("sp", f"o{o}", lambda e, m=m, s=s, o=o, lo=lo, hi=hi: e.dma_start(
                            out=yT[blk, m * 128:(m + 1) * 128, lo - HALO:hi - HALO], in_=ot[o][:, lo - s * N:hi - s * N]),
                            reads=[("ar", "ot", o)])

            def raw_out(blk):
                P.new_epoch()
                for m in range(KC):
                    P.dma("sp", f"o{m % 4}", lambda e, m=m: e.dma_start(
                        out=yT[blk, m * 128:(m + 1) * 128, :], in_=xs[:, m, HALO:HALO + TOK]),
                        reads=[("x", m, s) for s in range(NS)])

            for blk in range(nblocks):
                for m in range(KC):
                    P.dma("sp", f"x{m}", lambda e, m=m, blk=blk: e.dma_start(out=xs[:, m, :], in_=xT[blk, m * 128:(m + 1) * 128, :]),
                          writes=[("x", m, s) for s in range(NS)])
                for li, l in enumerate(layers):
                    if stop == "load":
                        break
                    if blk == 0:
                        drain_bg()
                        if li + 1 < len(layers):
                            enqueue_mod(layers[li + 1])
                    norm_mod(l, 0, 1, have_stats=(li > 0))
                    if stop == "norm":
                        for m in range(KC):
                            tt, tk = tmp_tile()
                            P.op("dve", lambda e, m=m, tt=tt: e.tensor_copy(out=tt[:, :], in_=hs[:, m, 0:N]), reads=[("h", m, 0)], writes=[tk])
                            P.dma("sp", "o0", lambda e, m=m, tt=tt: e.dma_start(out=dbg_d[:, m * N:(m + 1) * N], in_=tt[:, :]), reads=[tk])
                        P.dma("sp", "o1", lambda e: e.dma_start(out=dbg_d[:, 8 * N:8 * N + 192], in_=dv[:, :, :, :].rearrange("p a b c -> p (a b c)")), reads=dvk(l))
                        P.dma("sp", "o1", lambda e: e.dma_start(out=dbg_d[:, 8 * N + 192:8 * N + 384], in_=modv[:, :, :].rearrange("p a b -> p (a b)")), reads=[("modv", l)])
                        break
                    if l % 2 == 0:
                        mixer_even(l, blk)
                    else:
                        mixer_odd(l, blk)
                    if stop == "mixer":
                        break
                    norm_mod(l, 3, 4, have_stats=True)
                    ffn(l)
                if final_norm:
                    final_out(blk)
                else:
                    raw_out(blk)
                excl4[0] = False
            return wstate["specs"]

        Pd = Prog()
        plan = run(Pd, None)
        P = Prog()
        plan2 = run(P, plan)
        assert len(plan2) == len(plan)

        final_waits = {s: c for s, c in P.dmacnt.items() if s.startswith("o")}

        engmap = {"pe": "tensor", "act": "scalar", "dve": "vector", "pool": "gpsimd", "sp": "sync"}
        with nc.Block() as block:
            def make(engname):
                def body(e):
                    for item in P.q[engname]:
                        if item[0] == "wait":
                            e.wait_ge(sems[item[1]], item[2])
                        elif item[0] == "op":
                            ins = item[1](e)
                            ins.then_inc(sems[engname], 1)
                        else:
                            ins = item[1](e)
                            ins.then_inc(sems[item[2]], 16)
                    if engname == "sp":
                        for s_, c_ in final_waits.items():
                            e.wait_ge(sems[s_], c_)
                        for en in ("pe", "act", "dve"):
                            if P.cnt[en] > 0:
                                e.wait_ge(sems[en], P.cnt[en])
                return body
            block.tensor(make("pe"))
            block.scalar(make("act"))
            block.vector(make("dve"))
            block.gpsimd(make("pool"))
            block.sync(make("sp"))
        stats = {e: len(P.q[e]) for e in Prog.ENG}
    return nc, stats


def _fm(v):
    v = np.asarray(v, np.float32)
    lead = v.shape[:-1]
    n = v.shape[-1] // 128
    v = v.reshape(lead + (n, 128))
    return np.moveaxis(v, -1, 0)


def _build_cvec(inp, b):
    cv = np.zeros((128, NV), np.float32)

    def put(name, arr):
        arr = np.asarray(arr, np.float32).reshape(128, -1)
        cv[:, CV[name]:CV[name] + arr.shape[1]] = arr
    for l in range(DEPTH):
        put(("gmix", l), _fm(inp["norm_mix_g"][l]))
        put(("gffn", l), _fm(inp["norm_ffn_g"][l]))
        put(("bmod", l), _fm(inp["b_mod"][l]))
    for i in range(2):
        put(("conv", i), _fm(inp["ab_conv"][i]))
        put(("pscale", i), _fm(inp["ab_pool_scale"][i]))
        put(("bpw1", i), _fm(inp["cf_b_pw1"][i]))
        put(("wdw", i), _fm(inp["cf_w_dw"][i]))
        put(("bdw", i), _fm(inp["cf_b_dw"][i]))
        put(("lng", i), _fm(inp["cf_ln_g"][i]))
        put(("lnb", i), _fm(inp["cf_ln_b"][i]))
        put(("bpw2", i), _fm(inp["cf_b_pw2"][i]))
    put("gfin", _fm(inp["final_norm_g"]))
    put("c", _fm(inp["c"][b]))
    return cv


def _build_edge(half):
    e = np.zeros((NE,), np.float32)
    for blk in range(NB):
        s0 = half * (S // 2) + blk * TOK - HALO
        pos = s0 + np.arange(TB)
        valid = ((pos >= 0) & (pos < S)).astype(np.float32)
        o = blk * EB
        e[o:o + EDGE] = valid[:EDGE]
        e[o + EDGE:o + 2 * EDGE] = valid[TB - EDGE:]
        for g in range(4):
            w = 2 << g
            left = w // 2
            right = w - 1 - left
            cnt = (np.minimum(pos + right, S - 1) - np.maximum(pos - left, 0) + 1).astype(np.float32)
            inv = np.where((pos >= 0) & (pos < S), 1.0 / np.maximum(cnt, 1.0), 1.0 / w).astype(np.float32)
            oo = o + 2 * EDGE + g * 2 * PEDGE
            e[oo:oo + PEDGE] = inv[POFF:POFF + PEDGE]
            e[oo + PEDGE:oo + 2 * PEDGE] = inv[TB - POFF - PEDGE:TB - POFF]
    return np.ascontiguousarray(np.broadcast_to(e[None, :], (128, NE)))


_CACHE = {}


def _get_nc(layers, final_norm):
    key = (tuple(layers), final_norm)
    if key not in _CACHE:
        _CACHE[key] = build_program(layers, final_norm)
    return _CACHE[key][0]


def _make_in_maps(inp, x_full):
    f32 = lambda a: np.ascontiguousarray(np.asarray(a, np.float32))
    shared = {k: f32(inp[k]) for k in ("w_mod", "ab_w_in", "ab_w_pool", "ab_w_out", "cf_w_pw1", "cf_w_pw2",
                                       "ffn_w_gate", "ffn_w_up", "ffn_w_down")}
    ident = np.eye(128, dtype=np.float32)
    in_maps = []
    for cid in range(NCORES):
        b, half = cid // 2, cid % 2
        xt = np.zeros((NB, D, TB), np.float32)
        for blk in range(NB):
            s0 = half * (S // 2) + blk * TOK - HALO
            lo, hi = max(s0, 0), min(s0 + TB, S)
            xt[blk, :, lo - s0:hi - s0] = x_full[b, lo:hi, :].T
        m = dict(shared)
        m.update({"xT": xt, "cvec": _build_cvec(inp, b), "edge": _build_edge(half), "ident": ident})
        in_maps.append(m)
    return in_maps


def _gather(res):
    out = np.empty((BATCH, S, D), np.float32)
    for cid in range(NCORES):
        b, half = cid // 2, cid % 2
        y = res.results[cid]["yT"]
        for blk in range(NB):
            t0 = half * (S // 2) + blk * TOK
            out[b, t0:t0 + TOK, :] = y[blk].T
    return out


def kernel(**inputs):
    inp = {k: np.asarray(v) for k, v in inputs.items()}
    x = np.asarray(inp["x"], np.float32)
    nc = _get_nc((0, 1, 2, 3), True)
    in_maps = _make_in_maps(inp, x)
    res = run_bass_kernel_spmd(nc, in_maps, core_ids=list(range(NCORES)))
    return _gather(res)
```

```python
import numpy as np
import concourse.bass as bass
import concourse.mybir as mybir
from concourse.bass_utils import run_bass_kernel_spmd

F32 = mybir.dt.float32
BF16 = mybir.dt.bfloat16
AF = mybir.ActivationFunctionType
ALU = mybir.AluOpType

D = 1024
S = 8192
BATCH = 4
DEPTH = 4
DFF = 2816
NCORES = 8
NB = 2
TOK = 2048
HALO = 46
TB = TOK + 2 * HALO
NS = 5
N = TB // NS
KC = 8
NJ = DFF // 128
NG = 2
JG = NJ // NG
RING = 5
EDGE = 48
PEDGE = 32
POFF = 32
ZP = 15
PP = 16

CV = {}
_off = 0


def _cv(name, n):
    global _off
    CV[name] = _off
    _off += n


for _l in range(DEPTH):
    _cv(("gmix", _l), 8)
    _cv(("gffn", _l), 8)
    _cv(("bmod", _l), 48)
for _i in range(2):
    _cv(("conv", _i), 12)
    _cv(("pscale", _i), 4)
for _i in range(2):
    _cv(("bpw1", _i), 16)
    _cv(("wdw", _i), 31 * 8)
    _cv(("bdw", _i), 8)
    _cv(("lng", _i), 8)
    _cv(("lnb", _i), 8)
    _cv(("bpw2", _i), 8)
_cv("gfin", 8)
_cv("c", 8)
NV = _off
EB = 2 * EDGE + 4 * 2 * PEDGE
NE = NB * EB


PHASE_MARKS = []


class Prog:
    ENG = ("pe", "act", "dve", "pool", "sp")

    def __init__(self):
        self.q = {e: [] for e in self.ENG}
        self.cnt = {e: 0 for e in self.ENG}
        self.waited = {e: {} for e in self.ENG}
        self.res = {}
        self.dmacnt = {}
        self.epoch = {}
        self.arena_prefixes = set()
        self.nmm = 0
        self.marks = []

    def _res(self, k):
        r = self.res.get(k)
        if r is None:
            if k[0] in self.arena_prefixes:
                r = [dict(self.epoch), {}]
            else:
                r = [{}, {}]
            self.res[k] = r
        return r

    def new_epoch(self):
        ep = {e: c for e, c in self.cnt.items() if c > 0}
        for s, c in self.dmacnt.items():
            if c > 0 and not s.startswith("sg") and not s.startswith("x"):
                ep[s] = c
        self.epoch = ep
        for k in [k for k in self.res if k[0] in self.arena_prefixes]:
            del self.res[k]

    def _deps(self, reads, writes):
        deps = {}

        def add(d):
            for s, c in d.items():
                if deps.get(s, 0) < c:
                    deps[s] = c
        for k in reads:
            add(self._res(k)[0])
        for k in writes:
            r = self._res(k)
            add(r[0])
            add(r[1])
        return deps

    def _emit_waits(self, eng, deps):
        w = self.waited[eng]
        for s, c in deps.items():
            if w.get(s, 0) < c:
                self.q[eng].append(("wait", s, c))
                w[s] = c

    def _commit(self, tok, reads, writes):
        s, c = tok
        for k in reads:
            r = self._res(k)
            if r[1].get(s, 0) < c:
                r[1][s] = c
        for k in writes:
            r = self._res(k)
            r[0] = {s: c}
            r[1] = {}

    def mark(self, label):
        self.marks.append((label, self.nmm))

    def op(self, eng, fn, reads=(), writes=(), nmm=1):
        if eng == "pe":
            self.nmm += nmm
        deps = self._deps(reads, writes)
        if eng == "pe":
            deps.pop("pe", None)
        self._emit_waits(eng, deps)
        self.cnt[eng] += 1
        tok = (eng, self.cnt[eng])
        self.q[eng].append(("op", fn, eng))
        self._commit(tok, reads, writes)
        return tok

    def dma(self, eng, sem, fn, reads=(), writes=()):
        self._emit_waits(eng, self._deps(reads, writes))
        self.dmacnt[sem] = self.dmacnt.get(sem, 0) + 16
        tok = (sem, self.dmacnt[sem])
        self.q[eng].append(("dma", fn, sem))
        self._commit(tok, reads, writes)
        return tok


def build_program(layers=(0, 1, 2, 3), final_norm=True, nblocks=NB, stop=None):
    nc = bass.Bass("TRN2", target_bir_lowering=False)
    dr = {}

    def din(name, shape):
        dr[name] = nc.dram_tensor(name, list(shape), F32, kind="ExternalInput").ap()
        return dr[name]

    xT = din("xT", [NB, D, TB])
    cvec_d = din("cvec", [128, NV])
    edge_d = din("edge", [128, NE])
    ident_d = din("ident", [128, 128])
    w_mod = din("w_mod", [DEPTH, D, 6 * D])
    ab_w_in = din("ab_w_in", [2, D, 2048])
    ab_w_pool = din("ab_w_pool", [2, 4, 128, 128])
    ab_w_out = din("ab_w_out", [2, D, D])
    cf_w_pw1 = din("cf_w_pw1", [2, D, 2 * D])
    cf_w_pw2 = din("cf_w_pw2", [2, D, D])
    ffn_w_gate = din("ffn_w_gate", [DEPTH, D, DFF])
    ffn_w_up = din("ffn_w_up", [DEPTH, D, DFF])
    ffn_w_down = din("ffn_w_down", [DEPTH, DFF, D])
    yT = nc.dram_tensor("yT", [NB, D, TOK], F32, kind="ExternalOutput").ap()
    dbg_d = nc.dram_tensor("dbg", [128, 8 * N + 192 + 192], F32, kind="ExternalOutput").ap() if stop else None

    import contextlib
    st = contextlib.ExitStack()
    with st:
        def sb(name, shape, dt):
            return st.enter_context(nc.sbuf_tensor(name, list(shape), dt))

        xs = sb("xs", [128, KC, TB], F32)
        hs = sb("hs", [128, KC, TB], BF16)
        ARENA_E = 29056
        arena = sb("arena", [128, ARENA_E], BF16)
        wring = [sb(f"wr{r}", [128, JG, 128], BF16) for r in range(RING)]
        sqr = [sb(f"sq{r}", [128, N], BF16) for r in range(4)]
        cvec = sb("cvecs", [128, NV], F32)
        edge = sb("edges", [128, NE], F32)
        identf = sb("identf", [128, 128], F32)
        identb = sb("identb", [128, 128], BF16)
        onesD = sb("onesD", [128, 128], BF16)
        stg = [sb(f"stg{r}", [128, KC, 128], F32) for r in range(2)]
        gbt = sb("gbt", [128, 8], F32)
        modv = sb("modv", [128, DEPTH, 48], F32)
        dv = sb("dv", [128, DEPTH, 6, 8], F32)
        cact = sb("cact", [128, KC, 2], F32)
        eps_r = sb("eps_r", [128, 1], F32)
        eps_l = sb("eps_l", [128, 1], F32)
        tmpf = [sb(f"tmpf{r}", [128, N], F32) for r in range(4)]
        wmst = [sb(f"wmst{r}", [128, KC, 128], F32) for r in range(2)]
        mtmp = sb("mtmp", [128, 128], F32)
        macc = [sb(f"macc{r}", [128, 128], F32) for r in range(2)]
        onescol = sb("onescol", [128, 2], F32)
        ps = [st.enter_context(nc.psum_tensor(f"ps{b}", [128, 512], F32)) for b in range(8)]

        sems = {}
        for e in Prog.ENG:
            sems[e] = st.enter_context(nc.semaphore(f"s_{e}"))
        dma_sem_names = ["sg0", "sg1"] + [f"x{m}" for m in range(KC)] + \
            ["o0", "o1", "o2", "o3", "wm0", "wm1", "cst"]
        for s_ in dma_sem_names:
            sems[s_] = st.enter_context(nc.semaphore(f"d_{s_}"))

        class Arena:
            def __init__(self):
                self.off = 0

            def reset(self):
                self.off = 0

            def take(self, nelem, dt):
                if dt == F32:
                    self.off = (self.off + 1) // 2 * 2
                    v = arena[:, self.off:self.off + 2 * nelem].bitcast(F32)
                    self.off += 2 * nelem
                else:
                    v = arena[:, self.off:self.off + nelem]
                    self.off += nelem
                assert self.off <= ARENA_E, (self.off, ARENA_E)
                return v
        AR = Arena()

        def run(P, wplan):
            P.arena_prefixes = {"ar"}
            wstate = {"next_load": 0, "next_use": 0, "specs": [], "consumed": 0}
            tf_i = [0]
            sq_i = [0]
            ring_i = [0]

            def tmp_tile():
                i = tf_i[0] % 4
                tf_i[0] += 1
                return tmpf[i], ("tmpf", i)

            def sq_tile():
                i = sq_i[0] % 4
                sq_i[0] += 1
                return sqr[i], ("sq", i)

            excl4 = [False]

            def bank():
                b = 4 + ring_i[0] % 4
                ring_i[0] += 1
                if excl4[0] and b == 4:
                    b = 4 + ring_i[0] % 4
                    ring_i[0] += 1
                return ps[b], ("ps", b)

            def cv_ap(name, j=0, n=1):
                o = CV[name] + j
                return cvec[:, o:o + n]

            import collections as _c
            bgq = _c.deque()
            BG_RATE = 2

            def drain_bg(n=None):
                k = 0
                while bgq and (n is None or k < n):
                    bgq.popleft()()
                    k += 1

            stg_i = [0]

            def issue_load(j):
                src, kcn = wplan[j]
                slot = j % RING
                k0 = 0
                while k0 < kcn:
                    n_ = min(KC, kcn - k0)
                    si = stg_i[0] % 2
                    stg_i[0] += 1
                    P.dma("sp", f"sg{si}",
                          lambda e, si=si, n_=n_, k0=k0, src=src: e.dma_start(out=stg[si][:, 0:n_, :], in_=src[:, k0:k0 + n_, :]),
                          writes=[("stg", si)])
                    P.op("pool", lambda e, si=si, n_=n_, k0=k0, slot=slot: e.tensor_copy(
                        out=wring[slot][:, k0:k0 + n_, :], in_=stg[si][:, 0:n_, :]),
                        reads=[("stg", si)], writes=[("w", slot)])
                    k0 += n_

            def get_tiles(specs):
                assert len(specs) <= RING
                j0 = wstate["next_use"]
                outs = []
                for (src2d, kcn) in specs:
                    j = wstate["next_use"]
                    wstate["next_use"] += 1
                    src = src2d.rearrange("(kc p) n -> p kc n", p=128)
                    wstate["specs"].append((src, kcn))
                    slot = j % RING
                    outs.append((wring[slot], ("w", slot)))
                wstate["consumed"] = j0
                wpump()
                return outs

            def wpump():
                if wplan is not None:
                    lim = min(len(wplan), wstate["consumed"] + RING)
                    while wstate["next_load"] < lim:
                        issue_load(wstate["next_load"])
                        wstate["next_load"] += 1

            def wrelease(k):
                wstate["consumed"] += k
                assert wstate["consumed"] <= wstate["next_use"]
                wpump()

            def get_tile(src2d, kcn):
                return get_tiles([(src2d, kcn)])[0]

            def mm_group(pst, pkey, pairs, reads, extra_first=None):
                def fn(e, pairs=pairs, pst=pst):
                    ins = None
                    n_ = len(pairs)
                    for i_, (l_, r_) in enumerate(pairs):
                        ins = e.matmul(pst, l_, r_, start=(i_ == 0), stop=(i_ == n_ - 1))
                    return ins
                tok = P.op("pe", fn, reads=reads, writes=[pkey], nmm=len(pairs))
                drain_bg(BG_RATE)
                return tok

            P.dma("sp", "cst", lambda e: e.dma_start(out=cvec[:, :], in_=cvec_d[:, :]), writes=[("cvec",)])
            P.dma("sp", "cst", lambda e: e.dma_start(out=edge[:, :], in_=edge_d[:, :]), writes=[("edge",)])
            P.dma("sp", "cst", lambda e: e.dma_start(out=identf[:, :], in_=ident_d[:, :]), writes=[("identf",)])
            for k in (("cvec",), ("edge",), ("identf",)):
                P.res[k][0] = {"cst": P.dmacnt["cst"]}
            P.op("dve", lambda e: e.tensor_copy(out=identb[:, :], in_=identf[:, :]), reads=[("identf",)], writes=[("identb",)])
            P.op("dve", lambda e: e.memset(onesD[:, :], 1.0 / D), writes=[("onesD",)])
            P.op("dve", lambda e: e.memset(eps_r[:, :], 1e-6), writes=[("eps",)])
            P.op("dve", lambda e: e.memset(eps_l[:, :], 1e-5), writes=[("eps",)])
            for d_ in range(2):
                P.op("act", lambda e, d_=d_: e.activation(out=cact[:, :, d_], in_=cv_ap("c", 0, 8), func=AF.Silu),
                     reads=[("cvec",)], writes=[("cact", d_)])

            P.op("dve", lambda e: e.memset(onescol[:, :], 1.0), writes=[("onescol",)])
            mod_i = [0]

            def enqueue_mod(l):
                pending_pe = []
                for q in range(48):
                    gi_ = mod_i[0]
                    mod_i[0] += 1
                    slot = gi_ % 2
                    src = w_mod[l, :, q * 128:(q + 1) * 128].rearrange("(kc p) n -> p kc n", p=128)
                    bgq.append(lambda slot=slot, src=src: P.dma(
                        "sp", f"wm{slot}", lambda e: e.dma_start(out=wmst[slot][:, :, :], in_=src), writes=[("wmst", slot)]))
                    for kc in range(KC):
                        if kc == 0:
                            bgq.append(lambda slot=slot: P.op("pool", lambda e: e.tensor_scalar(
                                out=macc[slot][:, :], in0=wmst[slot][:, 0, :], scalar1=cact[:, 0, 0:1], scalar2=1.0,
                                op0=ALU.mult, op1=ALU.mult), reads=[("wmst", slot), ("cact", 0)], writes=[("macc", slot)]))
                        else:
                            bgq.append(lambda slot=slot, kc=kc: P.op("pool", lambda e: e.tensor_scalar(
                                out=mtmp[:, :], in0=wmst[slot][:, kc, :], scalar1=cact[:, kc, 0:1], scalar2=1.0,
                                op0=ALU.mult, op1=ALU.mult), reads=[("wmst", slot), ("cact", 0)], writes=[("mtmp",)]))
                            bgq.append(lambda slot=slot: P.op("pool", lambda e: e.tensor_tensor(
                                out=macc[slot][:, :], in0=macc[slot][:, :], in1=mtmp[:, :], op=ALU.add),
                                reads=[("mtmp",), ("macc", slot)], writes=[("macc", slot)]))

                    def pe_thunk(slot=slot, q=q, l=l):
                        pst, pkey = bank()
                        P.op("pe", lambda e: e.matmul(pst[:, 0:2], macc[slot][:, :], onescol[:, :], start=True, stop=True),
                             reads=[("macc", slot), ("onescol",)], writes=[pkey])
                        P.op("dve", lambda e: e.tensor_tensor(out=modv[:, l, q:q + 1], in0=pst[:, 0:1],
                                                             in1=cv_ap(("bmod", l), q, 1), op=ALU.add),
                             reads=[pkey, ("cvec",)], writes=[("modv", l)])
                    pending_pe.append(pe_thunk)
                    if len(pending_pe) > 1:
                        bgq.append(pending_pe.pop(0))
                bgq.append(pending_pe.pop(0))

                def fin(l=l):
                    for (di, mo, gname) in ((0, 8, "gmix"), (3, 32, "gffn")):
                        P.op("dve", lambda e, di=di, mo=mo, gname=gname: e.scalar_tensor_tensor(
                            out=dv[:, l, di, :], in0=modv[:, l, mo:mo + 8], scalar=1.0,
                            in1=cv_ap((gname, l), 0, 8), op0=ALU.add, op1=ALU.mult),
                            reads=[("modv", l), ("cvec",)], writes=[("dv", l, di)])
                    for (di, mo) in ((1, 0), (2, 16), (4, 24), (5, 40)):
                        P.op("dve", lambda e, di=di, mo=mo: e.tensor_copy(out=dv[:, l, di, :], in_=modv[:, l, mo:mo + 8]),
                             reads=[("modv", l)], writes=[("dv", l, di)])
                bgq.append(fin)

            def mod_prologue(l):
                P.mark("prologue")
                P.new_epoch()
                AR.reset()
                NSL = 6
                Wst = [AR.take(KC * 128, F32).rearrange("p (a b) -> p a b", a=KC) for _ in range(NSL)]
                Acc = [AR.take(128, F32) for _ in range(NSL)]
                pat = ("dve", "dve", "pool", "dve", "dve", "pool", "dve")
                pend = []
                for q in range(48):
                    sl = q % NSL
                    eng = pat[q % len(pat)]
                    src = w_mod[l, :, q * 128:(q + 1) * 128].rearrange("(kc p) n -> p kc n", p=128)
                    P.dma("sp", ("wm0", "wm1", "o0", "o1", "o2", "o3")[sl], lambda e, sl=sl, src=src: e.dma_start(out=Wst[sl][:, :, :], in_=src),
                          writes=[("ar", "wst", sl)])
                    for kc in range(KC):
                        if eng == "dve":
                            if kc == 0:
                                P.op("dve", lambda e, sl=sl: e.tensor_scalar(
                                    out=Acc[sl][:, :], in0=Wst[sl][:, 0, :], scalar1=cact[:, 0, 0:1], scalar2=None, op0=ALU.mult),
                                    reads=[("ar", "wst", sl), ("cact", 0)], writes=[("ar", "acc", sl)])
                            else:
                                P.op("dve", lambda e, sl=sl, kc=kc: e.scalar_tensor_tensor(
                                    out=Acc[sl][:, :], in0=Wst[sl][:, kc, :], scalar=cact[:, kc, 0:1], in1=Acc[sl][:, :],
                                    op0=ALU.mult, op1=ALU.add),
                                    reads=[("ar", "wst", sl), ("cact", 0), ("ar", "acc", sl)], writes=[("ar", "acc", sl)])
                        else:
                            if kc == 0:
                                P.op("pool", lambda e, sl=sl: e.tensor_scalar(
                                    out=Acc[sl][:, :], in0=Wst[sl][:, 0, :], scalar1=cact[:, 0, 0:1], scalar2=1.0,
                                    op0=ALU.mult, op1=ALU.mult),
                                    reads=[("ar", "wst", sl), ("cact", 0)], writes=[("ar", "acc", sl)])
                            else:
                                P.op("pool", lambda e, sl=sl, kc=kc: e.tensor_scalar(
                                    out=mtmp[:, :], in0=Wst[sl][:, kc, :], scalar1=cact[:, kc, 0:1], scalar2=1.0,
                                    op0=ALU.mult, op1=ALU.mult), reads=[("ar", "wst", sl), ("cact", 0)], writes=[("mtmp",)])
                                P.op("pool", lambda e, sl=sl: e.tensor_tensor(
                                    out=Acc[sl][:, :], in0=Acc[sl][:, :], in1=mtmp[:, :], op=ALU.add),
                                    reads=[("mtmp",), ("ar", "acc", sl)], writes=[("ar", "acc", sl)])

                    def pe_part(sl=sl, q=q):
                        pst, pkey = bank()
                        P.op("pe", lambda e: e.matmul(pst[:, 0:2], Acc[sl][:, :], onescol[:, :], start=True, stop=True),
                             reads=[("ar", "acc", sl), ("onescol",)], writes=[pkey])
                        P.op("dve", lambda e: e.tensor_tensor(out=modv[:, l, q:q + 1], in0=pst[:, 0:1],
                                                             in1=cv_ap(("bmod", l), q, 1), op=ALU.add),
                             reads=[pkey, ("cvec",)], writes=[("modv", l)])
                    pend.append(pe_part)
                    if len(pend) > 3:
                        pend.pop(0)()
                while pend:
                    pend.pop(0)()
                for (di, mo, gname) in ((0, 8, "gmix"), (3, 32, "gffn")):
                    P.op("dve", lambda e, di=di, mo=mo, gname=gname: e.scalar_tensor_tensor(
                        out=dv[:, l, di, :], in0=modv[:, l, mo:mo + 8], scalar=1.0,
                        in1=cv_ap((gname, l), 0, 8), op0=ALU.add, op1=ALU.mult),
                        reads=[("modv", l), ("cvec",)], writes=[("dv", l, di)])
                for (di, mo) in ((1, 0), (2, 16), (4, 24), (5, 40)):
                    P.op("dve", lambda e, di=di, mo=mo: e.tensor_copy(out=dv[:, l, di, :], in_=modv[:, l, mo:mo + 8]),
                         reads=[("modv", l)], writes=[("dv", l, di)])

            wpump()
            mod_prologue(layers[0])

            def dvk(l):
                return [("dv", l, i_) for i_ in range(6)]

            def cols(s):
                return slice(s * N, (s + 1) * N)

            def stats_rstd(src_fn, src_keys, s, eps_t):
                pairs = []
                rk = []
                for m in range(KC):
                    sqt, sqk = sq_tile()
                    P.op("act", lambda e, m=m, sqt=sqt: e.activation(out=sqt[:, :], in_=src_fn(m, s), func=AF.Square),
                         reads=[src_keys(m, s)], writes=[sqk])
                    P.op("pe", lambda e, m=m, sqt=sqt: e.matmul(ps[0][:, :N], onesD[:, :], sqt[:, :], start=(m == 0), stop=(m == KC - 1)),
                         reads=[sqk, ("onesD",)], writes=[("ps", 0)])
                tt, tk = tmp_tile()
                P.op("act", lambda e, tt=tt: e.activation(out=tt[:, :], in_=ps[0][:, :N], func=AF.Ln, bias=eps_t[:, 0:1], scale=1.0),
                     reads=[("ps", 0), ("eps",)], writes=[tk])
                P.op("act", lambda e, tt=tt: e.activation(out=ps[2][:, :N], in_=tt[:, :], func=AF.Exp, scale=-0.5),
                     reads=[tk], writes=[("ps", 2)])

            def acc_bank(s):
                return s if s < 4 else 4

            def modulate(l, di_a, di_b, s, rb):
                for m in range(KC):
                    tt, tk = tmp_tile()
                    P.op("dve", lambda e, m=m, s=s, tt=tt: e.scalar_tensor_tensor(
                        out=tt[:, :], in0=xs[:, m, cols(s)], scalar=dv[:, l, di_a, m:m + 1], in1=ps[rb][:, :N],
                        op0=ALU.mult, op1=ALU.mult),
                        reads=[("x", m, s), ("ps", rb)] + dvk(l), writes=[tk])
                    if m % 2 == 0:
                        P.op("pool", lambda e, m=m, s=s, tt=tt: e.tensor_scalar(
                            out=hs[:, m, cols(s)], in0=tt[:, :], scalar1=1.0, scalar2=dv[:, l, di_b, m:m + 1],
                            op0=ALU.mult, op1=ALU.add),
                            reads=[tk] + dvk(l), writes=[("h", m, s)])
                    else:
                        P.op("act", lambda e, m=m, s=s, tt=tt: e.activation(
                            out=hs[:, m, cols(s)], in_=tt[:, :], func=AF.Identity, bias=dv[:, l, di_b, m:m + 1], scale=1.0),
                            reads=[tk] + dvk(l), writes=[("h", m, s)])

            def rstd_from_acc(s):
                rb = acc_bank(s)
                tt, tk = tmp_tile()
                P.op("act", lambda e, tt=tt: e.activation(out=tt[:, :], in_=ps[rb][:, :N], func=AF.Ln, bias=eps_r[:, 0:1], scale=1.0),
                     reads=[("ps", rb), ("eps",)], writes=[tk])
                P.op("act", lambda e, tt=tt: e.activation(out=ps[rb][:, :N], in_=tt[:, :], func=AF.Exp, scale=-0.5),
                     reads=[tk], writes=[("ps", rb)])
                return rb

            def norm_mod(l, di_a, di_b, have_stats=False, after_sub=None):
                P.mark("norm")
                for s in range(NS):
                    if have_stats:
                        rb = rstd_from_acc(s)
                    else:
                        stats_rstd(lambda m, s_: xs[:, m, cols(s_)], lambda m, s_: ("x", m, s_), s, eps_r)
                        rb = 2
                    modulate(l, di_a, di_b, s, rb)
                    if s == NS - 1:
                        excl4[0] = False
                    if after_sub is not None and s >= 1:
                        after_sub(s - 1)
                excl4[0] = False
                if after_sub is not None:
                    after_sub(NS - 1)

            def edge_mask(eng, buf_fn, key_fn, blk):
                eo = blk * EB
                P.op(eng, lambda e: getattr(e, "tensor_tensor")(out=buf_fn(0, EDGE), in0=buf_fn(0, EDGE),
                                                               in1=edge[:, eo:eo + EDGE], op=ALU.mult),
                     reads=[("edge",), key_fn(0)], writes=[key_fn(0)])
                P.op(eng, lambda e: getattr(e, "tensor_tensor")(out=buf_fn(TB - EDGE, TB), in0=buf_fn(TB - EDGE, TB),
                                                               in1=edge[:, eo + EDGE:eo + 2 * EDGE], op=ALU.mult),
                     reads=[("edge",), key_fn(NS - 1)], writes=[key_fn(NS - 1)])

            def proj_residual(l, wsrc_fn, kcn, rhs_fn, rhs_keys, gi, bias_fn=None, acc_stats=False):
                pend = []
                if acc_stats:
                    excl4[0] = True
                for m in range(KC):
                    wt, wk = get_tile(wsrc_fn(m), kcn)
                    for s in range(NS):
                        if len(pend) > 2:
                            pend.pop(0)()
                        pst, pkey = bank()
                        pairs = [(wt[:, k, :], rhs_fn(k, s)) for k in range(kcn)]
                        reads = [wk] + [rhs_keys(k, s) for k in range(kcn)]
                        mm_group(pst[:, :N], pkey, pairs, reads)
                        P.op("dve", lambda e, m=m, s=s, pst=pst: e.scalar_tensor_tensor(
                            out=xs[:, m, cols(s)], in0=pst[:, :N], scalar=dv[:, l, gi, m:m + 1], in1=xs[:, m, cols(s)],
                            op0=ALU.mult, op1=ALU.add),
                            reads=[pkey, ("x", m, s)] + dvk(l), writes=[("x", m, s)])
                        if acc_stats:
                            sqt, sqk = sq_tile()
                            P.op("act", lambda e, m=m, s=s, sqt=sqt: e.activation(out=sqt[:, :], in_=xs[:, m, cols(s)], func=AF.Square),
                                 reads=[("x", m, s)], writes=[sqk])
                            ab_ = acc_bank(s)
                            pend.append(lambda m=m, sqt=sqt, sqk=sqk, ab_=ab_: P.op(
                                "pe", lambda e: e.matmul(ps[ab_][:, :N], onesD[:, :], sqt[:, :], start=(m == 0), stop=(m == KC - 1)),
                                reads=[sqk, ("onesD",)], writes=[("ps", ab_)]))
                    wrelease(1)
                while pend:
                    pend.pop(0)()

            def mixer_even(l, blk, norm_call):
                P.mark("mixer_even")
                i = l // 2
                P.new_epoch()
                AR.reset()
                ybuf = AR.take(KC * TB, BF16).rearrange("p (a b) -> p a b", a=KC)
                cvb = [AR.take(TB + 2, BF16) for _ in range(2)]
                pbuf = AR.take(TB + 2 * PP, F32)
                T0 = AR.take(N + 2 * PP + 16, F32)
                T1 = AR.take(N + 2 * PP + 16, F32)
                pooled = [AR.take(N, BF16) for _ in range(2)]
                wplt = AR.take(4 * 128, BF16).rearrange("p (a b) -> p a b", a=4)
                dg3s = [sqr[2][:, 0:384].rearrange("p (a b) -> p a b", a=3), sqr[3][:, 0:384].rearrange("p (a b) -> p a b", a=3)]
                dg3k = [("sq", 2), ("sq", 3)]
                win = ab_w_in[i]
                eo_m = blk * EB
                for c_ in range(2):
                    P.op("dve", lambda e, c_=c_: e.memset(cvb[c_][:, 0:1], 0.0), writes=[("ar", "cvpadl", c_)])
                    P.op("dve", lambda e, c_=c_: e.memset(cvb[c_][:, TB + 1:TB + 2], 0.0), writes=[("ar", "cvpadr", c_)])
                P.op("dve", lambda e: e.memset(pbuf[:, 0:PP], 0.0), writes=[("ar", "ppadl")])
                P.op("dve", lambda e: e.memset(pbuf[:, PP + TB:PP + TB + PP], 0.0), writes=[("ar", "ppadr")])
                si = stg_i[0] % 2
                stg_i[0] += 1
                P.dma("sp", f"sg{si}", lambda e, si=si: e.dma_start(
                    out=stg[si][:, 0:4, :], in_=ab_w_pool[i].rearrange("g c e -> c g e")), writes=[("stg", si)])
                P.op("pool", lambda e, si=si: e.tensor_copy(out=wplt[:, :, :], in_=stg[si][:, 0:4, :]),
                     reads=[("stg", si)], writes=[("ar", "wpl")])

                def P_step(g, s, wp, wpk):
                    pP, pPk = bank()
                    mm_group(pP[:, :N], pPk, [(wp[:, k, :], hs[:, k, cols(s)]) for k in range(KC)],
                             [wpk] + [("h", k, s) for k in range(KC)])
                    P.op("act", lambda e, pP=pP, s=s: e.activation(out=pbuf[:, PP + s * N:PP + (s + 1) * N], in_=pP[:, :N], func=AF.Copy),
                         reads=[pPk], writes=[("ar", "p", s)])
                    if s == 0:
                        P.op("dve", lambda e: e.tensor_tensor(out=pbuf[:, PP:PP + EDGE], in0=pbuf[:, PP:PP + EDGE],
                                                             in1=edge[:, eo_m:eo_m + EDGE], op=ALU.mult),
                             reads=[("edge",), ("ar", "p", 0)], writes=[("ar", "p", 0)])
                    if s == NS - 1:
                        P.op("dve", lambda e: e.tensor_tensor(out=pbuf[:, PP + TB - EDGE:PP + TB], in0=pbuf[:, PP + TB - EDGE:PP + TB],
                                                             in1=edge[:, eo_m + EDGE:eo_m + 2 * EDGE], op=ALU.mult),
                             reads=[("edge",), ("ar", "p", NS - 1)], writes=[("ar", "p", NS - 1)])

                def chain_step(g, s):
                    w_ = 2 << g
                    base = s * N
                    L = N + 2 * PP
                    rkeys = [("ar", "p", s_) for s_ in (s - 1, s, s + 1) if 0 <= s_ < NS] + [("ar", "ppadl"), ("ar", "ppadr")]
                    src, srcoff, srckey = pbuf, base, rkeys
                    Ts = [T0, T1]
                    d = 1
                    for step in range(g + 1):
                        dst = Ts[step % 2]
                        P.op("dve", lambda e, dst=dst, src=src, srcoff=srcoff, d=d, L=L: e.tensor_tensor(
                            out=dst[:, d:L], in0=src[:, srcoff + d:srcoff + L], in1=src[:, srcoff:srcoff + L - d], op=ALU.add),
                            reads=srckey, writes=[("ar", "T", step % 2)])
                        src, srcoff, srckey = dst, 0, [("ar", "T", step % 2)]
                        d *= 2
                    o_ = PP + w_ // 2 - 1
                    pl = pooled[s % 2]
                    plk = ("ar", "pooled", s % 2)
                    P.op("dve", lambda e, src=src, o_=o_, pl=pl, s=s, w_=w_: e.scalar_tensor_tensor(
                        out=pl[:, :], in0=src[:, o_:o_ + N], scalar=1.0 / w_, in1=pbuf[:, PP + s * N:PP + (s + 1) * N],
                        op0=ALU.mult, op1=ALU.subtract), reads=srckey + [("ar", "p", s)], writes=[plk])
                    eo = blk * EB + 2 * EDGE + g * 2 * PEDGE
                    if s == 0 or s == NS - 1:
                        c0 = POFF if s == 0 else N - POFF - PEDGE
                        et = edge[:, eo:eo + PEDGE] if s == 0 else edge[:, eo + PEDGE:eo + 2 * PEDGE]
                        tt, tk = tmp_tile()
                        P.op("dve", lambda e, src=src, o_=o_, c0=c0, et=et, tt=tt: e.tensor_tensor(
                            out=tt[:, 0:PEDGE], in0=src[:, o_ + c0:o_ + c0 + PEDGE], in1=et, op=ALU.mult),
                            reads=srckey + [("edge",)], writes=[tk])
                        P.op("dve", lambda e, c0=c0, tt=tt, pl=pl, s=s: e.tensor_tensor(
                            out=pl[:, c0:c0 + PEDGE], in0=tt[:, 0:PEDGE],
                            in1=pbuf[:, PP + s * N + c0:PP + s * N + c0 + PEDGE], op=ALU.subtract),
                            reads=[tk, ("ar", "p", s)], writes=[plk])

                def poolmm_step(g, s):
                    pl = pooled[s % 2]
                    plk = ("ar", "pooled", s % 2)
                    pO, pOk = bank()
                    mm_group(pO[:, :N], pOk, [(wplt[:, g, :], pl[:, :])], [("ar", "wpl"), plk])
                    P.op("act", lambda e, pO=pO, g=g, s=s: e.activation(
                        out=ybuf[:, 4 + g, cols(s)], in_=pO[:, :N], func=AF.Identity, scale=cv_ap(("pscale", i), g, 1)),
                        reads=[pOk, ("cvec",)], writes=[("ar", "y", 4 + g, s)])

                def CV_step(a, s, wc, wck, wv, wvk):
                    cb = cvb[a % 2]
                    pC, pCk = bank()
                    pV, pVk = bank()
                    hk = [("h", k, s) for k in range(KC)]
                    mm_group(pC[:, :N], pCk, [(wc[:, k, :], hs[:, k, cols(s)]) for k in range(KC)], [wck] + hk)
                    mm_group(pV[:, :N], pVk, [(wv[:, k, :], hs[:, k, cols(s)]) for k in range(KC)], [wvk] + hk)
                    tt, tk = tmp_tile()
                    P.op("act", lambda e, tt=tt, pC=pC: e.activation(out=tt[:, :], in_=pC[:, :N], func=AF.Copy),
                         reads=[pCk], writes=[tk])
                    P.op("dve", lambda e, tt=tt, pV=pV, cb=cb, s=s: e.tensor_tensor(
                        out=cb[:, 1 + s * N:1 + (s + 1) * N], in0=tt[:, :], in1=pV[:, :N], op=ALU.mult),
                        reads=[tk, pVk], writes=[("ar", "cv", a % 2, s)])
                    if s == 0:
                        P.op("dve", lambda e, cb=cb: e.tensor_tensor(out=cb[:, 1:1 + EDGE], in0=cb[:, 1:1 + EDGE],
                                                                    in1=edge[:, eo_m:eo_m + EDGE], op=ALU.mult),
                             reads=[("edge",), ("ar", "cv", a % 2, 0)], writes=[("ar", "cv", a % 2, 0)])
                    if s == NS - 1:
                        P.op("dve", lambda e, cb=cb: e.tensor_tensor(out=cb[:, 1 + TB - EDGE:1 + TB], in0=cb[:, 1 + TB - EDGE:1 + TB],
                                                                    in1=edge[:, eo_m + EDGE:eo_m + 2 * EDGE], op=ALU.mult),
                             reads=[("edge",), ("ar", "cv", a % 2, NS - 1)], writes=[("ar", "cv", a % 2, NS - 1)])

                def diag_build(a):
                    for k in range(3):
                        P.op("dve", lambda e, k=k, a=a: e.tensor_scalar(
                            out=dg3s[a % 2][:, k, :], in0=identb[:, :], scalar1=cv_ap(("conv", i), k * 4 + a, 1), scalar2=None,
                            op0=ALU.mult), reads=[("identb",), ("cvec",)], writes=[dg3k[a % 2]])

                def conv_step(a, s, wb, wbk):
                    cb = cvb[a % 2]
                    dg3 = dg3s[a % 2]
                    pY, pYk = bank()
                    pB, pBk = bank()
                    rk = [("ar", "cv", a % 2, s_) for s_ in (s - 1, s, s + 1) if 0 <= s_ < NS]
                    rk += [("ar", "cvpadl", a % 2), ("ar", "cvpadr", a % 2), dg3k[a % 2]]
                    mm_group(pY[:, :N], pYk, [(dg3[:, k, :], cb[:, s * N + k:s * N + k + N]) for k in range(3)], rk)
                    mm_group(pB[:, :N], pBk, [(wb[:, k, :], hs[:, k, cols(s)]) for k in range(KC)],
                             [wbk] + [("h", k, s) for k in range(KC)])
                    tt, tk = tmp_tile()
                    P.op("act", lambda e, tt=tt, pY=pY: e.activation(out=tt[:, :], in_=pY[:, :N], func=AF.Copy),
                         reads=[pYk], writes=[tk])
                    P.op("dve", lambda e, tt=tt, pB=pB, a=a, s=s: e.tensor_tensor(
                        out=ybuf[:, a, cols(s)], in0=tt[:, :], in1=pB[:, :N], op=ALU.mult),
                        reads=[tk, pBk], writes=[("ar", "y", a, s)])

                def wsl(c0):
                    return win[:, c0:c0 + 128]

                (wc, wck), (wv, wvk), (wp, wpk) = get_tiles([(wsl(512), KC), (wsl(1024), KC), (wsl(1536), KC)])
                def first_pass(s):
                    CV_step(0, s, wc, wck, wv, wvk)
                    P_step(0, s, wp, wpk)
                norm_call(first_pass)
                wrelease(3)
                diag_build(0)
                for a in range(4):
                    specs = []
                    if a >= 1:
                        specs += [(wsl(512 + a * 128), KC), (wsl(1024 + a * 128), KC)]
                    specs += [(wsl((a) * 128), KC)]
                    if a + 1 < 4:
                        specs += [(wsl(1536 + (a + 1) * 128), KC)]
                    tl = get_tiles(specs)
                    if a >= 1:
                        (wc, wck), (wv, wvk) = tl[0], tl[1]
                        tl = tl[2:]
                    (wb, wbk) = tl[0]
                    wpn = tl[1] if a + 1 < 4 else None
                    for s in range(NS):
                        if a >= 1:
                            CV_step(a, s, wc, wck, wv, wvk)
                        chain_step(a, s)
                        if s >= 1:
                            poolmm_step(a, s - 1)
                        if a == 0:
                            if s >= 1:
                                conv_step(0, s - 1, wb, wbk)
                    poolmm_step(a, NS - 1)
                    if a == 0:
                        conv_step(0, NS - 1, wb, wbk)
                        wrelease(1)
                    else:
                        wrelease(2)
                        diag_build(a)
                        for s in range(NS):
                            conv_step(a, s, wb, wbk)
                        wrelease(1)
                    if wpn is not None:
                        for s in range(NS):
                            P_step(a + 1, s, wpn[0], wpn[1])
                proj_residual(l, lambda m: ab_w_out[i][:, m * 128:(m + 1) * 128], KC,
                              lambda k, s: ybuf[:, k, cols(s)], lambda k, s: ("ar", "y", k, s), 2, acc_stats=True)

            def mixer_odd(l, blk, norm_call):
                P.mark("mixer_odd")
                i = l // 2
                P.new_epoch()
                AR.reset()
                zb = AR.take(KC * (TB + 2 * ZP), BF16).rearrange("p (a b) -> p a b", a=KC)
                dgs = [AR.take(31 * 128, BF16).rearrange("p (a b) -> p a b", a=31) for _ in range(2)]
                for a in range(KC):
                    P.op("dve", lambda e, a=a: e.memset(zb[:, a, 0:ZP], 0.0), writes=[("ar", "zpadl", a)])
                    P.op("dve", lambda e, a=a: e.memset(zb[:, a, ZP + TB:ZP + TB + ZP], 0.0), writes=[("ar", "zpadr", a)])
                w1 = cf_w_pw1[i]
                for a in range(KC):
                    (wa, wak), (wg, wgk) = get_tiles([(w1[:, a * 128:(a + 1) * 128], KC),
                                                      (w1[:, D + a * 128:D + (a + 1) * 128], KC)])

                    def glu_step(s, a=a, wa=wa, wak=wak, wg=wg, wgk=wgk):
                        pA, pAk = bank()
                        pG, pGk = bank()
                        hk = [("h", k, s) for k in range(KC)]
                        mm_group(pA[:, :N], pAk, [(wa[:, k, :], hs[:, k, cols(s)]) for k in range(KC)], [wak] + hk)
                        mm_group(pG[:, :N], pGk, [(wg[:, k, :], hs[:, k, cols(s)]) for k in range(KC)], [wgk] + hk)
                        tt, tk = tmp_tile()
                        P.op("act", lambda e, tt=tt, pG=pG, a=a: e.activation(
                            out=tt[:, :], in_=pG[:, :N], func=AF.Sigmoid, bias=cv_ap(("bpw1", i), 8 + a, 1), scale=1.0),
                            reads=[pGk, ("cvec",)], writes=[tk])
                        P.op("dve", lambda e, tt=tt, pA=pA, a=a, s=s: e.scalar_tensor_tensor(
                            out=zb[:, a, ZP + s * N:ZP + (s + 1) * N], in0=pA[:, :N], scalar=cv_ap(("bpw1", i), a, 1),
                            in1=tt[:, :], op0=ALU.add, op1=ALU.mult),
                            reads=[pAk, tk, ("cvec",)], writes=[("ar", "z", a, s)])
                    if a == 0:
                        norm_call(glu_step)
                    else:
                        for s in range(NS):
                            glu_step(s)
                    wrelease(2)
                    edge_mask("dve", lambda c0, c1, a=a: zb[:, a, ZP + c0:ZP + c1], lambda s_, a=a: ("ar", "z", a, s_), blk)
                for a in range(KC):
                    dgt = dgs[a % 2]
                    for k in range(31):
                        P.op("dve", lambda e, k=k, a=a, dgt=dgt: e.tensor_scalar(
                            out=dgt[:, k, :], in0=identb[:, :], scalar1=cv_ap(("wdw", i), k * 8 + a, 1), scalar2=None,
                            op0=ALU.mult), reads=[("identb",), ("cvec",)], writes=[("ar", "dg", a % 2, k)])
                    for s in range(NS):
                        pZ, pZk = bank()
                        rk = [("ar", "z", a, s_) for s_ in (s - 1, s, s + 1) if 0 <= s_ < NS]
                        rk += [("ar", "zpadl", a), ("ar", "zpadr", a)] + [("ar", "dg", a % 2, k) for k in range(31)]
                        mm_group(pZ[:, :N], pZk, [(dgt[:, k, :], zb[:, a, s * N + k:s * N + k + N]) for k in range(31)], rk)
                        P.op("act", lambda e, pZ=pZ, a=a, s=s: e.activation(
                            out=hs[:, a, cols(s)], in_=pZ[:, :N], func=AF.Identity, bias=cv_ap(("bdw", i), a, 1), scale=1.0),
                            reads=[pZk, ("cvec",)], writes=[("h", a, s)])
                for s in range(NS):
                    for a in range(KC):
                        sqt, sqk = sq_tile()
                        P.op("dve", lambda e, a=a, s=s, sqt=sqt: e.tensor_tensor(
                            out=sqt[:, :], in0=hs[:, a, cols(s)], in1=hs[:, a, cols(s)], op=ALU.mult),
                            reads=[("h", a, s)], writes=[sqk])
                        P.op("pe", lambda e, a=a, s=s: e.matmul(ps[0][:, :N], onesD[:, :], hs[:, a, cols(s)], start=(a == 0), stop=(a == KC - 1)),
                             reads=[("h", a, s), ("onesD",)], writes=[("ps", 0)])
                        P.op("pe", lambda e, a=a, sqt=sqt: e.matmul(ps[1][:, :N], onesD[:, :], sqt[:, :], start=(a == 0), stop=(a == KC - 1)),
                             reads=[sqk, ("onesD",)], writes=[("ps", 1)])
                    t1, t1k = tmp_tile()
                    P.op("act", lambda e, t1=t1: e.activation(out=t1[:, :], in_=ps[0][:, :N], func=AF.Square),
                         reads=[("ps", 0)], writes=[t1k])
                    t2, t2k = tmp_tile()
                    P.op("dve", lambda e, t1=t1, t2=t2: e.tensor_tensor(out=t2[:, :], in0=ps[1][:, :N], in1=t1[:, :], op=ALU.subtract),
                         reads=[("ps", 1), t1k], writes=[t2k])
                    P.op("act", lambda e, t1=t1, t2=t2: e.activation(out=t1[:, :], in_=t2[:, :], func=AF.Ln, bias=eps_l[:, 0:1], scale=1.0),
                         reads=[t2k, ("eps",)], writes=[t1k])
                    P.op("act", lambda e, t1=t1: e.activation(out=ps[2][:, :N], in_=t1[:, :], func=AF.Exp, scale=-0.5),
                         reads=[t1k], writes=[("ps", 2)])
                    P.op("act", lambda e, t2=t2: e.activation(out=t2[:, :], in_=ps[2][:, :N], func=AF.Copy),
                         reads=[("ps", 2)], writes=[t2k])
                    P.op("dve", lambda e, t2=t2: e.scalar_tensor_tensor(
                        out=ps[3][:, :N], in0=ps[0][:, :N], scalar=-1.0, in1=t2[:, :], op0=ALU.mult, op1=ALU.mult),
                        reads=[("ps", 0), t2k], writes=[("ps", 3)])
                    for a in range(KC):
                        ta, tak = tmp_tile()
                        P.op("dve", lambda e, a=a, s=s, ta=ta: e.tensor_tensor(
                            out=ta[:, :], in0=hs[:, a, cols(s)], in1=ps[2][:, :N], op=ALU.mult),
                            reads=[("h", a, s), ("ps", 2)], writes=[tak])
                        P.op("dve", lambda e, ta=ta: e.tensor_tensor(out=ta[:, :], in0=ta[:, :], in1=ps[3][:, :N], op=ALU.add),
                             reads=[tak, ("ps", 3)], writes=[tak])
                        P.op("act", lambda e, a=a, s=s, ta=ta: e.activation(
                            out=zb[:, a, ZP + s * N:ZP + (s + 1) * N], in_=ta[:, :], func=AF.Silu,
                            bias=cv_ap(("lnb", i), a, 1), scale=cv_ap(("lng", i), a, 1)),
                            reads=[tak, ("cvec",)], writes=[("ar", "z", a, s)])
                P.op("dve", lambda e: e.tensor_tensor(out=gbt[:, :], in0=dv[:, l, 2, :], in1=cv_ap(("bpw2", i), 0, 8), op=ALU.mult),
                     reads=dvk(l) + [("cvec",)], writes=[("gbt",)])
                for m in range(KC):
                    P.op("pool", lambda e, m=m: e.tensor_scalar(
                        out=xs[:, m, :], in0=xs[:, m, :], scalar1=1.0, scalar2=gbt[:, m:m + 1], op0=ALU.mult, op1=ALU.add),
                        reads=[("gbt",)] + [("x", m, s_) for s_ in range(NS)], writes=[("x", m, s_) for s_ in range(NS)])
                proj_residual(l, lambda m: cf_w_pw2[i][:, m * 128:(m + 1) * 128], KC,
                              lambda k, s: zb[:, k, ZP + s * N:ZP + (s + 1) * N], lambda k, s: ("ar", "z", k, s), 2,
                              acc_stats=True)

            def ffn(l, norm_call):
                P.mark("ffn")
                P.new_epoch()
                AR.reset()
                ab = AR.take(JG * TB, BF16).rearrange("p (a b) -> p a b", a=JG)
                for hf in range(NG):
                    for jj in range(JG):
                        j = hf * JG + jj
                        (wgt, wgk), (wut, wuk) = get_tiles([(ffn_w_gate[l][:, j * 128:(j + 1) * 128], KC),
                                                            (ffn_w_up[l][:, j * 128:(j + 1) * 128], KC)])

                        def gu_step(s, jj=jj, wgt=wgt, wgk=wgk, wut=wut, wuk=wuk):
                            pG, pGk = bank()
                            pU, pUk = bank()
                            hk = [("h", k, s) for k in range(KC)]
                            mm_group(pG[:, :N], pGk, [(wgt[:, k, :], hs[:, k, cols(s)]) for k in range(KC)], [wgk] + hk)
                            mm_group(pU[:, :N], pUk, [(wut[:, k, :], hs[:, k, cols(s)]) for k in range(KC)], [wuk] + hk)
                            tt, tk = tmp_tile()
                            P.op("act", lambda e, tt=tt, pG=pG: e.activation(out=tt[:, :], in_=pG[:, :N], func=AF.Silu),
                                 reads=[pGk], writes=[tk])
                            P.op("dve", lambda e, tt=tt, pU=pU, jj=jj, s=s: e.tensor_tensor(
                                out=ab[:, jj, cols(s)], in0=tt[:, :], in1=pU[:, :N], op=ALU.mult),
                                reads=[tk, pUk], writes=[("ar", "a", jj, s)])
                        if j == 0:
                            norm_call(gu_step)
                        else:
                            for s in range(NS):
                                gu_step(s)
                        wrelease(2)
                    proj_residual(l, lambda m, hf=hf: ffn_w_down[l][hf * JG * 128:(hf + 1) * JG * 128, m * 128:(m + 1) * 128], JG,
                                  lambda k, s: ab[:, k, cols(s)], lambda k, s: ("ar", "a", k, s), 5,
                                  acc_stats=(hf == NG - 1))

            def final_out(blk):
                P.mark("final")
                P.new_epoch()
                AR.reset()
                ot = [AR.take(N, F32) for _ in range(4)]
                oi = 0
                for s in range(NS):
                    if len(layers) > 0 and stop is None:
                        rb = rstd_from_acc(s)
                    else:
                        stats_rstd(lambda m, s_: xs[:, m, cols(s_)], lambda m, s_: ("x", m, s_), s, eps_r)
                        rb = 2
                    lo = max(s * N, HALO)
                    hi = min((s + 1) * N, HALO + TOK)
                    for m in range(KC):
                        o = oi % 4
                        oi += 1
                        P.op("dve", lambda e, m=m, s=s, o=o, rb=rb: e.scalar_tensor_tensor(
                            out=ot[o][:, :], in0=xs[:, m, cols(s)], scalar=cv_ap("gfin", m, 1), in1=ps[rb][:, :N],
                            op0=ALU.mult, op1=ALU.mult),
                            reads=[("x", m, s), ("ps", rb), ("cvec",)], writes=[("ar", "ot", o)])
                        P.dma("sp", f"o{o}", lambda e, m=m, s=s, o=o, lo=lo, hi=hi: e.dma_start(
                            out=yT[blk, m * 128:(m + 1) * 128, lo - HALO:hi - HALO], in_=ot[o][:, lo - s * N:hi - s * N]),
                            reads=[("ar", "ot", o)])

            def raw_out(blk):
                P.new_epoch()
                for m in range(KC):
                    P.dma("sp", f"o{m % 4}", lambda e, m=m: e.dma_start(
                        out=yT[blk, m * 128:(m + 1) * 128, :], in_=xs[:, m, HALO:HALO + TOK]),
                        reads=[("x", m, s) for s in range(NS)])

            for blk in range(nblocks):
                for m in range(KC):
                    P.dma("sp", f"x{m}", lambda e, m=m, blk=blk: e.dma_start(out=xs[:, m, :], in_=xT[blk, m * 128:(m + 1) * 128, :]),
                          writes=[("x", m, s) for s in range(NS)])
                for li, l in enumerate(layers):
                    if stop == "load":
                        break
                    if blk == 0:
                        drain_bg()
                        if li + 1 < len(layers):
                            enqueue_mod(layers[li + 1])
                    nm1 = lambda cb, l=l, li=li: norm_mod(l, 0, 1, have_stats=(li > 0), after_sub=cb)
                    if stop == "norm":
                        nm1(None)
                    if stop == "norm":
                        for m in range(KC):
                            tt, tk = tmp_tile()
                            P.op("dve", lambda e, m=m, tt=tt: e.tensor_copy(out=tt[:, :], in_=hs[:, m, 0:N]), reads=[("h", m, 0)], writes=[tk])
                            P.dma("sp", "o0", lambda e, m=m, tt=tt: e.dma_start(out=dbg_d[:, m * N:(m + 1) * N], in_=tt[:, :]), reads=[tk])
                        P.dma("sp", "o1", lambda e: e.dma_start(out=dbg_d[:, 8 * N:8 * N + 192], in_=dv[:, :, :, :].rearrange("p a b c -> p (a b c)")), reads=dvk(l))
                        P.dma("sp", "o1", lambda e: e.dma_start(out=dbg_d[:, 8 * N + 192:8 * N + 384], in_=modv[:, :, :].rearrange("p a b -> p (a b)")), reads=[("modv", l)])
                        break
                    if l % 2 == 0:
                        mixer_even(l, blk, nm1)
                    else:
                        mixer_odd(l, blk, nm1)
                    if stop == "mixer":
                        break
                    ffn(l, lambda cb, l=l: norm_mod(l, 3, 4, have_stats=True, after_sub=cb))
                if final_norm:
                    final_out(blk)
                else:
                    raw_out(blk)
                excl4[0] = False
            return wstate["specs"]

        Pd = Prog()
        plan = run(Pd, None)
        P = Prog()
        plan2 = run(P, plan)
        assert len(plan2) == len(plan)

        final_waits = {s: c for s, c in P.dmacnt.items() if s.startswith("o")}

        engmap = {"pe": "tensor", "act": "scalar", "dve": "vector", "pool": "gpsimd", "sp": "sync"}
        with nc.Block() as block:
            def make(engname):
                def body(e):
                    for item in P.q[engname]:
                        if item[0] == "wait":
                            e.wait_ge(sems[item[1]], item[2])
                        elif item[0] == "op":
                            ins = item[1](e)
                            ins.then_inc(sems[engname], 1)
                        else:
                            ins = item[1](e)
                            ins.then_inc(sems[item[2]], 16)
                    if engname == "sp":
                        for s_, c_ in final_waits.items():
                            e.wait_ge(sems[s_], c_)
                        for en in ("pe", "act", "dve"):
                            if P.cnt[en] > 0:
                                e.wait_ge(sems[en], P.cnt[en])
                return body
            block.tensor(make("pe"))
            block.scalar(make("act"))
            block.vector(make("dve"))
            block.gpsimd(make("pool"))
            block.sync(make("sp"))
        stats = {e: len(P.q[e]) for e in Prog.ENG}
        PHASE_MARKS[:] = P.marks
    return nc, stats


def _fm(v):
    v = np.asarray(v, np.float32)
    lead = v.shape[:-1]
    n = v.shape[-1] // 128
    v = v.reshape(lead + (n, 128))
    return np.moveaxis(v, -1, 0)


def _build_cvec(inp, b):
    cv = np.zeros((128, NV), np.float32)

    def put(name, arr):
        arr = np.asarray(arr, np.float32).reshape(128, -1)
        cv[:, CV[name]:CV[name] + arr.shape[1]] = arr
    for l in range(DEPTH):
        put(("gmix", l), _fm(inp["norm_mix_g"][l]))
        put(("gffn", l), _fm(inp["norm_ffn_g"][l]))
        put(("bmod", l), _fm(inp["b_mod"][l]))
    for i in range(2):
        put(("conv", i), _fm(inp["ab_conv"][i]))
        put(("pscale", i), _fm(inp["ab_pool_scale"][i]))
        put(("bpw1", i), _fm(inp["cf_b_pw1"][i]))
        put(("wdw", i), _fm(inp["cf_w_dw"][i]))
        put(("bdw", i), _fm(inp["cf_b_dw"][i]))
        put(("lng", i), _fm(inp["cf_ln_g"][i]))
        put(("lnb", i), _fm(inp["cf_ln_b"][i]))
        put(("bpw2", i), _fm(inp["cf_b_pw2"][i]))
    put("gfin", _fm(inp["final_norm_g"]))
    put("c", _fm(inp["c"][b]))
    return cv


def _build_edge(half):
    e = np.zeros((NE,), np.float32)
    for blk in range(NB):
        s0 = half * (S // 2) + blk * TOK - HALO
        pos = s0 + np.arange(TB)
        valid = ((pos >= 0) & (pos < S)).astype(np.float32)
        o = blk * EB
        e[o:o + EDGE] = valid[:EDGE]
        e[o + EDGE:o + 2 * EDGE] = valid[TB - EDGE:]
        for g in range(4):
            w = 2 << g
            left = w // 2
            right = w - 1 - left
            cnt = (np.minimum(pos + right, S - 1) - np.maximum(pos - left, 0) + 1).astype(np.float32)
            inv = np.where((pos >= 0) & (pos < S), 1.0 / np.maximum(cnt, 1.0), 1.0 / w).astype(np.float32)
            oo = o + 2 * EDGE + g * 2 * PEDGE
            e[oo:oo + PEDGE] = inv[POFF:POFF + PEDGE]
            e[oo + PEDGE:oo + 2 * PEDGE] = inv[TB - POFF - PEDGE:TB - POFF]
    return np.ascontiguousarray(np.broadcast_to(e[None, :], (128, NE)))


_CACHE = {}


def _get_nc(layers, final_norm):
    key = (tuple(layers), final_norm)
    if key not in _CACHE:
        _CACHE[key] = build_program(layers, final_norm)
    return _CACHE[key][0]


def _make_in_maps(inp, x_full):
    f32 = lambda a: np.ascontiguousarray(np.asarray(a, np.float32))
    shared = {k: f32(inp[k]) for k in ("w_mod", "ab_w_in", "ab_w_pool", "ab_w_out", "cf_w_pw1", "cf_w_pw2",
                                       "ffn_w_gate", "ffn_w_up", "ffn_w_down")}
    ident = np.eye(128, dtype=np.float32)
    in_maps = []
    for cid in range(NCORES):
        b, half = cid // 2, cid % 2
        xt = np.zeros((NB, D, TB), np.float32)
        for blk in range(NB):
            s0 = half * (S // 2) + blk * TOK - HALO
            lo, hi = max(s0, 0), min(s0 + TB, S)
            xt[blk, :, lo - s0:hi - s0] = x_full[b, lo:hi, :].T
        m = dict(shared)
        m.update({"xT": xt, "cvec": _build_cvec(inp, b), "edge": _build_edge(half), "ident": ident})
        in_maps.append(m)
    return in_maps


def _gather(res):
    out = np.empty((BATCH, S, D), np.float32)
    for cid in range(NCORES):
        b, half = cid // 2, cid % 2
        y = res.results[cid]["yT"]
        for blk in range(NB):
            t0 = half * (S // 2) + blk * TOK
            out[b, t0:t0 + TOK, :] = y[blk].T
    return out


def kernel(**inputs):
    inp = {k: np.asarray(v) for k, v in inputs.items()}
    x = np.asarray(inp["x"], np.float32)
    nc = _get_nc((0, 1, 2, 3), True)
    in_maps = _make_in_maps(inp, x)
    res = run_bass_kernel_spmd(nc, in_maps, core_ids=list(range(NCORES)))
    return _gather(res)
```

```python
import numpy as np
import concourse.bass as bass
import concourse.mybir as mybir
from concourse.bass_utils import run_bass_kernel_spmd

F32 = mybir.dt.float32
BF16 = mybir.dt.bfloat16
AF = mybir.ActivationFunctionType
ALU = mybir.AluOpType

D = 1024
S = 8192
BATCH = 4
DEPTH = 4
DFF = 2816
NCORES = 8
NB = 2
TOK = 2048
HALO = 46
TB = TOK + 2 * HALO
NS = 5
N = TB // NS
KC = 8
NJ = DFF // 128
NG = 2
JG = NJ // NG
RING = 5
EDGE = 48
PEDGE = 32
POFF = 32
ZP = 15
PP = 16

CV = {}
_off = 0


def _cv(name, n):
    global _off
    CV[name] = _off
    _off += n


for _l in range(DEPTH):
    _cv(("gmix", _l), 8)
    _cv(("gffn", _l), 8)
    _cv(("bmod", _l), 48)
for _i in range(2):
    _cv(("conv", _i), 12)
    _cv(("pscale", _i), 4)
for _i in range(2):
    _cv(("bpw1", _i), 16)
    _cv(("wdw", _i), 31 * 8)
    _cv(("bdw", _i), 8)
    _cv(("lng", _i), 8)
    _cv(("lnb", _i), 8)
    _cv(("bpw2", _i), 8)
_cv("gfin", 8)
_cv("c", 8)
NV = _off
EB = 2 * EDGE + 4 * 2 * PEDGE
NE = NB * EB


PHASE_MARKS = []


class Prog:
    ENG = ("pe", "act", "dve", "pool", "sp")

    def __init__(self):
        self.q = {e: [] for e in self.ENG}
        self.cnt = {e: 0 for e in self.ENG}
        self.waited = {e: {} for e in self.ENG}
        self.res = {}
        self.dmacnt = {}
        self.epoch = {}
        self.arena_prefixes = set()
        self.nmm = 0
        self.marks = []

    def _res(self, k):
        r = self.res.get(k)
        if r is None:
            if k[0] in self.arena_prefixes:
                r = [dict(self.epoch), {}]
            else:
                r = [{}, {}]
            self.res[k] = r
        return r

    def new_epoch(self):
        ep = {e: c for e, c in self.cnt.items() if c > 0}
        for s, c in self.dmacnt.items():
            if c > 0 and not s.startswith("sg") and not s.startswith("x"):
                ep[s] = c
        self.epoch = ep
        for k in [k for k in self.res if k[0] in self.arena_prefixes]:
            del self.res[k]

    def _deps(self, reads, writes):
        deps = {}

        def add(d):
            for s, c in d.items():
                if deps.get(s, 0) < c:
                    deps[s] = c
        for k in reads:
            add(self._res(k)[0])
        for k in writes:
            r = self._res(k)
            add(r[0])
            add(r[1])
        return deps

    def _emit_waits(self, eng, deps):
        w = self.waited[eng]
        for s, c in deps.items():
            if w.get(s, 0) < c:
                self.q[eng].append(("wait", s, c))
                w[s] = c

    def _commit(self, tok, reads, writes):
        s, c = tok
        for k in reads:
            r = self._res(k)
            if r[1].get(s, 0) < c:
                r[1][s] = c
        for k in writes:
            r = self._res(k)
            r[0] = {s: c}
            r[1] = {}

    def mark(self, label):
        self.marks.append((label, self.nmm))

    def op(self, eng, fn, reads=(), writes=(), nmm=1):
        if eng == "pe":
            self.nmm += nmm
        deps = self._deps(reads, writes)
        if eng == "pe":
            deps.pop("pe", None)
        self._emit_waits(eng, deps)
        self.cnt[eng] += 1
        tok = (eng, self.cnt[eng])
        self.q[eng].append(("op", fn, eng))
        self._commit(tok, reads, writes)
        return tok

    def dma(self, eng, sem, fn, reads=(), writes=()):
        self._emit_waits(eng, self._deps(reads, writes))
        self.dmacnt[sem] = self.dmacnt.get(sem, 0) + 16
        tok = (sem, self.dmacnt[sem])
        self.q[eng].append(("dma", fn, sem))
        self._commit(tok, reads, writes)
        return tok


def build_program(layers=(0, 1, 2, 3), final_norm=True, nblocks=NB, stop=None):
    nc = bass.Bass("TRN2", target_bir_lowering=False)
    dr = {}

    def din(name, shape):
        dr[name] = nc.dram_tensor(name, list(shape), F32, kind="ExternalInput").ap()
        return dr[name]

    xT = din("xT", [NB, D, TB])
    cvec_d = din("cvec", [128, NV])
    edge_d = din("edge", [128, NE])
    ident_d = din("ident", [128, 128])
    w_mod = din("w_mod", [DEPTH, D, 6 * D])
    ab_w_in = din("ab_w_in", [2, D, 2048])
    ab_w_pool = din("ab_w_pool", [2, 4, 128, 128])
    ab_w_out = din("ab_w_out", [2, D, D])
    cf_w_pw1 = din("cf_w_pw1", [2, D, 2 * D])
    cf_w_pw2 = din("cf_w_pw2", [2, D, D])
    ffn_w_gate = din("ffn_w_gate", [DEPTH, D, DFF])
    ffn_w_up = din("ffn_w_up", [DEPTH, D, DFF])
    ffn_w_down = din("ffn_w_down", [DEPTH, DFF, D])
    yT = nc.dram_tensor("yT", [NB, D, TOK], F32, kind="ExternalOutput").ap()
    dbg_d = nc.dram_tensor("dbg", [128, 8 * N + 192 + 192], F32, kind="ExternalOutput").ap() if stop else None

    import contextlib
    st = contextlib.ExitStack()
    with st:
        def sb(name, shape, dt):
            return st.enter_context(nc.sbuf_tensor(name, list(shape), dt))

        xs = sb("xs", [128, KC, TB], F32)
        hs = sb("hs", [128, KC, TB], BF16)
        ARENA_E = 29056
        arena = sb("arena", [128, ARENA_E], BF16)
        wring = [sb(f"wr{r}", [128, JG, 128], BF16) for r in range(RING)]
        sqr = [sb(f"sq{r}", [128, N], BF16) for r in range(4)]
        cvec = sb("cvecs", [128, NV], F32)
        edge = sb("edges", [128, NE], F32)
        identf = sb("identf", [128, 128], F32)
        identb = sb("identb", [128, 128], BF16)
        onesD = sb("onesD", [128, 128], BF16)
        stg = [sb(f"stg{r}", [128, KC, 128], F32) for r in range(2)]
        gbt = sb("gbt", [128, 8], F32)
        modv = sb("modv", [128, DEPTH, 48], F32)
        dv = sb("dv", [128, DEPTH, 6, 8], F32)
        cact = sb("cact", [128, KC, 2], F32)
        eps_r = sb("eps_r", [128, 1], F32)
        eps_l = sb("eps_l", [128, 1], F32)
        tmpf = [sb(f"tmpf{r}", [128, N], F32) for r in range(4)]
        wmst = [sb(f"wmst{r}", [128, KC, 128], F32) for r in range(2)]
        mtmp = sb("mtmp", [128, 128], F32)
        macc = [sb(f"macc{r}", [128, 128], F32) for r in range(2)]
        onescol = sb("onescol", [128, 2], F32)
        ps = [st.enter_context(nc.psum_tensor(f"ps{b}", [128, 512], F32)) for b in range(8)]

        sems = {}
        for e in Prog.ENG:
            sems[e] = st.enter_context(nc.semaphore(f"s_{e}"))
        dma_sem_names = ["sg0", "sg1"] + [f"x{m}" for m in range(KC)] + \
            ["o0", "o1", "o2", "o3", "wm0", "wm1", "cst"]
        for s_ in dma_sem_names:
            sems[s_] = st.enter_context(nc.semaphore(f"d_{s_}"))

        class Arena:
            def __init__(self):
                self.off = 0

            def reset(self):
                self.off = 0

            def take(self, nelem, dt):
                if dt == F32:
                    self.off = (self.off + 1) // 2 * 2
                    v = arena[:, self.off:self.off + 2 * nelem].bitcast(F32)
                    self.off += 2 * nelem
                else:
                    v = arena[:, self.off:self.off + nelem]
                    self.off += nelem
                assert self.off <= ARENA_E, (self.off, ARENA_E)
                return v
        AR = Arena()

        def run(P, wplan):
            P.arena_prefixes = {"ar"}
            wstate = {"next_load": 0, "next_use": 0, "specs": [], "consumed": 0}
            tf_i = [0]
            sq_i = [0]
            ring_i = [0]

            def tmp_tile():
                i = tf_i[0] % 4
                tf_i[0] += 1
                return tmpf[i], ("tmpf", i)

            def sq_tile():
                i = sq_i[0] % 4
                sq_i[0] += 1
                return sqr[i], ("sq", i)

            excl4 = [False]

            def bank():
                b = 4 + ring_i[0] % 4
                ring_i[0] += 1
                if excl4[0] and b == 4:
                    b = 4 + ring_i[0] % 4
                    ring_i[0] += 1
                return ps[b], ("ps", b)

            def cv_ap(name, j=0, n=1):
                o = CV[name] + j
                return cvec[:, o:o + n]

            import collections as _c
            bgq = _c.deque()
            BG_RATE = 2

            def drain_bg(n=None):
                k = 0
                while bgq and (n is None or k < n):
                    bgq.popleft()()
                    k += 1

            stg_i = [0]

            def issue_load(j):
                src, kcn = wplan[j]
                slot = j % RING
                k0 = 0
                while k0 < kcn:
                    n_ = min(KC, kcn - k0)
                    si = stg_i[0] % 2
                    stg_i[0] += 1
                    P.dma("sp", f"sg{si}",
                          lambda e, si=si, n_=n_, k0=k0, src=src: e.dma_start(out=stg[si][:, 0:n_, :], in_=src[:, k0:k0 + n_, :]),
                          writes=[("stg", si)])
                    P.op("pool", lambda e, si=si, n_=n_, k0=k0, slot=slot: e.tensor_copy(
                        out=wring[slot][:, k0:k0 + n_, :], in_=stg[si][:, 0:n_, :]),
                        reads=[("stg", si)], writes=[("w", slot)])
                    k0 += n_

            def get_tiles(specs):
                assert len(specs) <= RING
                j0 = wstate["next_use"]
                outs = []
                for (src2d, kcn) in specs:
                    j = wstate["next_use"]
                    wstate["next_use"] += 1
                    src = src2d.rearrange("(kc p) n -> p kc n", p=128)
                    wstate["specs"].append((src, kcn))
                    slot = j % RING
                    outs.append((wring[slot], ("w", slot)))
                wstate["consumed"] = j0
                wpump()
                return outs

            def wpump():
                if wplan is not None:
                    lim = min(len(wplan), wstate["consumed"] + RING)
                    while wstate["next_load"] < lim:
                        issue_load(wstate["next_load"])
                        wstate["next_load"] += 1

            def wrelease(k):
                wstate["consumed"] += k
                assert wstate["consumed"] <= wstate["next_use"]
                wpump()

            def get_tile(src2d, kcn):
                return get_tiles([(src2d, kcn)])[0]

            def mm_group(pst, pkey, pairs, reads, extra_first=None):
                def fn(e, pairs=pairs, pst=pst):
                    ins = None
                    n_ = len(pairs)
                    for i_, (l_, r_) in enumerate(pairs):
                        ins = e.matmul(pst, l_, r_, start=(i_ == 0), stop=(i_ == n_ - 1))
                    return ins
                tok = P.op("pe", fn, reads=reads, writes=[pkey], nmm=len(pairs))
                drain_bg(BG_RATE)
                return tok

            P.dma("sp", "cst", lambda e: e.dma_start(out=cvec[:, :], in_=cvec_d[:, :]), writes=[("cvec",)])
            P.dma("sp", "cst", lambda e: e.dma_start(out=edge[:, :], in_=edge_d[:, :]), writes=[("edge",)])
            P.dma("sp", "cst", lambda e: e.dma_start(out=identf[:, :], in_=ident_d[:, :]), writes=[("identf",)])
            for k in (("cvec",), ("edge",), ("identf",)):
                P.res[k][0] = {"cst": P.dmacnt["cst"]}
            P.op("dve", lambda e: e.tensor_copy(out=identb[:, :], in_=identf[:, :]), reads=[("identf",)], writes=[("identb",)])
            P.op("dve", lambda e: e.memset(onesD[:, :], 1.0 / D), writes=[("onesD",)])
            P.op("dve", lambda e: e.memset(eps_r[:, :], 1e-6), writes=[("eps",)])
            P.op("dve", lambda e: e.memset(eps_l[:, :], 1e-5), writes=[("eps",)])
            for d_ in range(2):
                P.op("act", lambda e, d_=d_: e.activation(out=cact[:, :, d_], in_=cv_ap("c", 0, 8), func=AF.Silu),
                     reads=[("cvec",)], writes=[("cact", d_)])

            P.op("dve", lambda e: e.memset(onescol[:, :], 1.0), writes=[("onescol",)])
            mod_i = [0]

            def enqueue_mod(l):
                pending_pe = []
                for q in range(48):
                    gi_ = mod_i[0]
                    mod_i[0] += 1
                    slot = gi_ % 2
                    src = w_mod[l, :, q * 128:(q + 1) * 128].rearrange("(kc p) n -> p kc n", p=128)
                    bgq.append(lambda slot=slot, src=src: P.dma(
                        "sp", f"wm{slot}", lambda e: e.dma_start(out=wmst[slot][:, :, :], in_=src), writes=[("wmst", slot)]))
                    for kc in range(KC):
                        if kc == 0:
                            bgq.append(lambda slot=slot: P.op("pool", lambda e: e.tensor_scalar(
                                out=macc[slot][:, :], in0=wmst[slot][:, 0, :], scalar1=cact[:, 0, 0:1], scalar2=1.0,
                                op0=ALU.mult, op1=ALU.mult), reads=[("wmst", slot), ("cact", 0)], writes=[("macc", slot)]))
                        else:
                            bgq.append(lambda slot=slot, kc=kc: P.op("pool", lambda e: e.tensor_scalar(
                                out=mtmp[:, :], in0=wmst[slot][:, kc, :], scalar1=cact[:, kc, 0:1], scalar2=1.0,
                                op0=ALU.mult, op1=ALU.mult), reads=[("wmst", slot), ("cact", 0)], writes=[("mtmp",)]))
                            bgq.append(lambda slot=slot: P.op("pool", lambda e: e.tensor_tensor(
                                out=macc[slot][:, :], in0=macc[slot][:, :], in1=mtmp[:, :], op=ALU.add),
                                reads=[("mtmp",), ("macc", slot)], writes=[("macc", slot)]))

                    def pe_thunk(slot=slot, q=q, l=l):
                        pst, pkey = bank()
                        P.op("pe", lambda e: e.matmul(pst[:, 0:2], macc[slot][:, :], onescol[:, :], start=True, stop=True),
                             reads=[("macc", slot), ("onescol",)], writes=[pkey])
                        P.op("dve", lambda e: e.tensor_tensor(out=modv[:, l, q:q + 1], in0=pst[:, 0:1],
                                                             in1=cv_ap(("bmod", l), q, 1), op=ALU.add),
                             reads=[pkey, ("cvec",)], writes=[("modv", l)])
                    pending_pe.append(pe_thunk)
                    if len(pending_pe) > 1:
                        bgq.append(pending_pe.pop(0))
                bgq.append(pending_pe.pop(0))

                def fin(l=l):
                    for (di, mo, gname) in ((0, 8, "gmix"), (3, 32, "gffn")):
                        P.op("dve", lambda e, di=di, mo=mo, gname=gname: e.scalar_tensor_tensor(
                            out=dv[:, l, di, :], in0=modv[:, l, mo:mo + 8], scalar=1.0,
                            in1=cv_ap((gname, l), 0, 8), op0=ALU.add, op1=ALU.mult),
                            reads=[("modv", l), ("cvec",)], writes=[("dv", l, di)])
                    for (di, mo) in ((1, 0), (2, 16), (4, 24), (5, 40)):
                        P.op("dve", lambda e, di=di, mo=mo: e.tensor_copy(out=dv[:, l, di, :], in_=modv[:, l, mo:mo + 8]),
                             reads=[("modv", l)], writes=[("dv", l, di)])
                bgq.append(fin)

            def mod_prologue(l):
                P.mark("prologue")
                P.new_epoch()
                AR.reset()
                NSL = 6
                Wst = [AR.take(KC * 128, F32).rearrange("p (a b) -> p a b", a=KC) for _ in range(NSL)]
                Acc = [AR.take(128, F32) for _ in range(NSL)]
                pat = ("dve", "act", "pool", "dve", "act", "dve", "act")
                Tq = [AR.take(128, F32) for _ in range(2)]
                tq_i = 0
                pend = []
                for q in range(48):
                    while pend and pend[0][0] <= q - 3:
                        pend.pop(0)[1]()
                    sl = q % NSL
                    eng = pat[q % len(pat)]
                    src = w_mod[l, :, q * 128:(q + 1) * 128].rearrange("(kc p) n -> p kc n", p=128)
                    P.dma("sp", ("wm0", "wm1", "o0", "o1", "o2", "o3")[sl], lambda e, sl=sl, src=src: e.dma_start(out=Wst[sl][:, :, :], in_=src),
                          writes=[("ar", "wst", sl)])
                    if eng == "act":
                        pst, pkey = bank()
                        for kc in range(KC):
                            tq = Tq[tq_i % 2]
                            tqk = ("ar", "tq", tq_i % 2)
                            tq_i += 1
                            P.op("act", lambda e, sl=sl, kc=kc, tq=tq: e.activation(
                                out=tq[:, :], in_=Wst[sl][:, kc, :], func=AF.Identity, scale=cact[:, kc, 0:1]),
                                reads=[("ar", "wst", sl), ("cact", 0)], writes=[tqk])
                            P.op("pe", lambda e, kc=kc, tq=tq, pst=pst: e.matmul(
                                pst[:, 0:2], tq[:, :], onescol[:, :], start=(kc == 0), stop=(kc == KC - 1)),
                                reads=[tqk, ("onescol",)], writes=[pkey])
                        P.op("dve", lambda e, pst=pst, q=q: e.tensor_tensor(out=modv[:, l, q:q + 1], in0=pst[:, 0:1],
                                                                          in1=cv_ap(("bmod", l), q, 1), op=ALU.add),
                             reads=[pkey, ("cvec",), ("ar", "wst", sl)], writes=[("modv", l)])
                        continue
                    for kc in range(KC):
                        if eng == "dve":
                            if kc == 0:
                                P.op("dve", lambda e, sl=sl: e.tensor_scalar(
                                    out=Acc[sl][:, :], in0=Wst[sl][:, 0, :], scalar1=cact[:, 0, 0:1], scalar2=None, op0=ALU.mult),
                                    reads=[("ar", "wst", sl), ("cact", 0)], writes=[("ar", "acc", sl)])
                            else:
                                P.op("dve", lambda e, sl=sl, kc=kc: e.scalar_tensor_tensor(
                                    out=Acc[sl][:, :], in0=Wst[sl][:, kc, :], scalar=cact[:, kc, 0:1], in1=Acc[sl][:, :],
                                    op0=ALU.mult, op1=ALU.add),
                                    reads=[("ar", "wst", sl), ("cact", 0), ("ar", "acc", sl)], writes=[("ar", "acc", sl)])
                        else:
                            if kc == 0:
                                P.op("pool", lambda e, sl=sl: e.tensor_scalar(
                                    out=Acc[sl][:, :], in0=Wst[sl][:, 0, :], scalar1=cact[:, 0, 0:1], scalar2=1.0,
                                    op0=ALU.mult, op1=ALU.mult),
                                    reads=[("ar", "wst", sl), ("cact", 0)], writes=[("ar", "acc", sl)])
                            else:
                                P.op("pool", lambda e, sl=sl, kc=kc: e.tensor_scalar(
                                    out=mtmp[:, :], in0=Wst[sl][:, kc, :], scalar1=cact[:, kc, 0:1], scalar2=1.0,
                                    op0=ALU.mult, op1=ALU.mult), reads=[("ar", "wst", sl), ("cact", 0)], writes=[("mtmp",)])
                                P.op("pool", lambda e, sl=sl: e.tensor_tensor(
                                    out=Acc[sl][:, :], in0=Acc[sl][:, :], in1=mtmp[:, :], op=ALU.add),
                                    reads=[("mtmp",), ("ar", "acc", sl)], writes=[("ar", "acc", sl)])

                    def pe_part(sl=sl, q=q):
                        pst, pkey = bank()
                        P.op("pe", lambda e: e.matmul(pst[:, 0:2], Acc[sl][:, :], onescol[:, :], start=True, stop=True),
                             reads=[("ar", "acc", sl), ("onescol",)], writes=[pkey])
                        P.op("dve", lambda e: e.tensor_tensor(out=modv[:, l, q:q + 1], in0=pst[:, 0:1],
                                                             in1=cv_ap(("bmod", l), q, 1), op=ALU.add),
                             reads=[pkey, ("cvec",)], writes=[("modv", l)])
                    pend.append((q, pe_part))
                while pend:
                    pend.pop(0)[1]()
                for (di, mo, gname) in ((0, 8, "gmix"), (3, 32, "gffn")):
                    P.op("dve", lambda e, di=di, mo=mo, gname=gname: e.scalar_tensor_tensor(
                        out=dv[:, l, di, :], in0=modv[:, l, mo:mo + 8], scalar=1.0,
                        in1=cv_ap((gname, l), 0, 8), op0=ALU.add, op1=ALU.mult),
                        reads=[("modv", l), ("cvec",)], writes=[("dv", l, di)])
                for (di, mo) in ((1, 0), (2, 16), (4, 24), (5, 40)):
                    P.op("dve", lambda e, di=di, mo=mo: e.tensor_copy(out=dv[:, l, di, :], in_=modv[:, l, mo:mo + 8]),
                         reads=[("modv", l)], writes=[("dv", l, di)])

            wpump()
            mod_prologue(layers[0])

            def dvk(l):
                return [("dv", l, i_) for i_ in range(6)]

            def cols(s):
                return slice(s * N, (s + 1) * N)

            def stats_rstd(src_fn, src_keys, s, eps_t):
                pairs = []
                rk = []
                for m in range(KC):
                    sqt, sqk = sq_tile()
                    P.op("act", lambda e, m=m, sqt=sqt: e.activation(out=sqt[:, :], in_=src_fn(m, s), func=AF.Square),
                         reads=[src_keys(m, s)], writes=[sqk])
                    P.op("pe", lambda e, m=m, sqt=sqt: e.matmul(ps[0][:, :N], onesD[:, :], sqt[:, :], start=(m == 0), stop=(m == KC - 1)),
                         reads=[sqk, ("onesD",)], writes=[("ps", 0)])
                tt, tk = tmp_tile()
                P.op("act", lambda e, tt=tt: e.activation(out=tt[:, :], in_=ps[0][:, :N], func=AF.Ln, bias=eps_t[:, 0:1], scale=1.0),
                     reads=[("ps", 0), ("eps",)], writes=[tk])
                P.op("act", lambda e, tt=tt: e.activation(out=ps[2][:, :N], in_=tt[:, :], func=AF.Exp, scale=-0.5),
                     reads=[tk], writes=[("ps", 2)])

            def acc_bank(s):
                return s if s < 4 else 4

            def modulate(l, di_a, di_b, s, rb):
                for m in range(KC):
                    tt, tk = tmp_tile()
                    P.op("dve", lambda e, m=m, s=s, tt=tt: e.scalar_tensor_tensor(
                        out=tt[:, :], in0=xs[:, m, cols(s)], scalar=dv[:, l, di_a, m:m + 1], in1=ps[rb][:, :N],
                        op0=ALU.mult, op1=ALU.mult),
                        reads=[("x", m, s), ("ps", rb)] + dvk(l), writes=[tk])
                    if m % 2 == 0:
                        P.op("pool", lambda e, m=m, s=s, tt=tt: e.tensor_scalar(
                            out=hs[:, m, cols(s)], in0=tt[:, :], scalar1=1.0, scalar2=dv[:, l, di_b, m:m + 1],
                            op0=ALU.mult, op1=ALU.add),
                            reads=[tk] + dvk(l), writes=[("h", m, s)])
                    else:
                        P.op("act", lambda e, m=m, s=s, tt=tt: e.activation(
                            out=hs[:, m, cols(s)], in_=tt[:, :], func=AF.Identity, bias=dv[:, l, di_b, m:m + 1], scale=1.0),
                            reads=[tk] + dvk(l), writes=[("h", m, s)])

            def rstd_from_acc(s):
                rb = acc_bank(s)
                tt, tk = tmp_tile()
                P.op("act", lambda e, tt=tt: e.activation(out=tt[:, :], in_=ps[rb][:, :N], func=AF.Ln, bias=eps_r[:, 0:1], scale=1.0),
                     reads=[("ps", rb), ("eps",)], writes=[tk])
                P.op("act", lambda e, tt=tt: e.activation(out=ps[rb][:, :N], in_=tt[:, :], func=AF.Exp, scale=-0.5),
                     reads=[tk], writes=[("ps", rb)])
                return rb

            def norm_mod(l, di_a, di_b, have_stats=False, after_sub=None):
                P.mark("norm")
                for s in range(NS):
                    if have_stats:
                        rb = rstd_from_acc(s)
                    else:
                        stats_rstd(lambda m, s_: xs[:, m, cols(s_)], lambda m, s_: ("x", m, s_), s, eps_r)
                        rb = 2
                    modulate(l, di_a, di_b, s, rb)
                    if s == NS - 1:
                        excl4[0] = False
                    if after_sub is not None and s >= 1:
                        after_sub(s - 1)
                excl4[0] = False
                if after_sub is not None:
                    after_sub(NS - 1)

            def edge_mask(eng, buf_fn, key_fn, blk):
                eo = blk * EB
                P.op(eng, lambda e: getattr(e, "tensor_tensor")(out=buf_fn(0, EDGE), in0=buf_fn(0, EDGE),
                                                               in1=edge[:, eo:eo + EDGE], op=ALU.mult),
                     reads=[("edge",), key_fn(0)], writes=[key_fn(0)])
                P.op(eng, lambda e: getattr(e, "tensor_tensor")(out=buf_fn(TB - EDGE, TB), in0=buf_fn(TB - EDGE, TB),
                                                               in1=edge[:, eo + EDGE:eo + 2 * EDGE], op=ALU.mult),
                     reads=[("edge",), key_fn(NS - 1)], writes=[key_fn(NS - 1)])

            def proj_residual(l, wsrc_fn, kcn, rhs_fn, rhs_keys, gi, bias_fn=None, acc_stats=False):
                pend = []
                if acc_stats:
                    excl4[0] = True
                for m in range(KC):
                    wt, wk = get_tile(wsrc_fn(m), kcn)
                    for s in range(NS):
                        if len(pend) > 2:
                            pend.pop(0)()
                        pst, pkey = bank()
                        pairs = [(wt[:, k, :], rhs_fn(k, s)) for k in range(kcn)]
                        reads = [wk] + [rhs_keys(k, s) for k in range(kcn)]
                        mm_group(pst[:, :N], pkey, pairs, reads)
                        P.op("dve", lambda e, m=m, s=s, pst=pst: e.scalar_tensor_tensor(
                            out=xs[:, m, cols(s)], in0=pst[:, :N], scalar=dv[:, l, gi, m:m + 1], in1=xs[:, m, cols(s)],
                            op0=ALU.mult, op1=ALU.add),
                            reads=[pkey, ("x", m, s)] + dvk(l), writes=[("x", m, s)])
                        if acc_stats:
                            sqt, sqk = sq_tile()
                            P.op("act", lambda e, m=m, s=s, sqt=sqt: e.activation(out=sqt[:, :], in_=xs[:, m, cols(s)], func=AF.Square),
                                 reads=[("x", m, s)], writes=[sqk])
                            ab_ = acc_bank(s)
                            pend.append(lambda m=m, sqt=sqt, sqk=sqk, ab_=ab_: P.op(
                                "pe", lambda e: e.matmul(ps[ab_][:, :N], onesD[:, :], sqt[:, :], start=(m == 0), stop=(m == KC - 1)),
                                reads=[sqk, ("onesD",)], writes=[("ps", ab_)]))
                    wrelease(1)
                while pend:
                    pend.pop(0)()

            def mixer_even(l, blk, norm_call):
                P.mark("mixer_even")
                i = l // 2
                P.new_epoch()
                AR.reset()
                ybuf = AR.take(KC * TB, BF16).rearrange("p (a b) -> p a b", a=KC)
                cvb = [AR.take(TB + 2, BF16) for _ in range(2)]
                pbuf = AR.take(TB + 2 * PP, F32)
                T0 = AR.take(N + 2 * PP + 16, F32)
                T1 = AR.take(N + 2 * PP + 16, F32)
                pooled = [AR.take(N, BF16) for _ in range(2)]
                wplt = AR.take(4 * 128, BF16).rearrange("p (a b) -> p a b", a=4)
                dg3s = [sqr[2][:, 0:384].rearrange("p (a b) -> p a b", a=3), sqr[3][:, 0:384].rearrange("p (a b) -> p a b", a=3)]
                dg3k = [("sq", 2), ("sq", 3)]
                win = ab_w_in[i]
                eo_m = blk * EB
                for c_ in range(2):
                    P.op("dve", lambda e, c_=c_: e.memset(cvb[c_][:, 0:1], 0.0), writes=[("ar", "cvpadl", c_)])
                    P.op("dve", lambda e, c_=c_: e.memset(cvb[c_][:, TB + 1:TB + 2], 0.0), writes=[("ar", "cvpadr", c_)])
                P.op("dve", lambda e: e.memset(pbuf[:, 0:PP], 0.0), writes=[("ar", "ppadl")])
                P.op("dve", lambda e: e.memset(pbuf[:, PP + TB:PP + TB + PP], 0.0), writes=[("ar", "ppadr")])
                si = stg_i[0] % 2
                stg_i[0] += 1
                P.dma("sp", f"sg{si}", lambda e, si=si: e.dma_start(
                    out=stg[si][:, 0:4, :], in_=ab_w_pool[i].rearrange("g c e -> c g e")), writes=[("stg", si)])
                P.op("pool", lambda e, si=si: e.tensor_copy(out=wplt[:, :, :], in_=stg[si][:, 0:4, :]),
                     reads=[("stg", si)], writes=[("ar", "wpl")])

                def P_step(g, s, wp, wpk):
                    pP, pPk = bank()
                    mm_group(pP[:, :N], pPk, [(wp[:, k, :], hs[:, k, cols(s)]) for k in range(KC)],
                             [wpk] + [("h", k, s) for k in range(KC)])
                    P.op("act", lambda e, pP=pP, s=s: e.activation(out=pbuf[:, PP + s * N:PP + (s + 1) * N], in_=pP[:, :N], func=AF.Copy),
                         reads=[pPk], writes=[("ar", "p", s)])
                    if s == 0:
                        P.op("dve", lambda e: e.tensor_tensor(out=pbuf[:, PP:PP + EDGE], in0=pbuf[:, PP:PP + EDGE],
                                                             in1=edge[:, eo_m:eo_m + EDGE], op=ALU.mult),
                             reads=[("edge",), ("ar", "p", 0)], writes=[("ar", "p", 0)])
                    if s == NS - 1:
                        P.op("dve", lambda e: e.tensor_tensor(out=pbuf[:, PP + TB - EDGE:PP + TB], in0=pbuf[:, PP + TB - EDGE:PP + TB],
                                                             in1=edge[:, eo_m + EDGE:eo_m + 2 * EDGE], op=ALU.mult),
                             reads=[("edge",), ("ar", "p", NS - 1)], writes=[("ar", "p", NS - 1)])

                def chain_step(g, s):
                    w_ = 2 << g
                    base = s * N
                    L = N + 2 * PP
                    rkeys = [("ar", "p", s_) for s_ in (s - 1, s, s + 1) if 0 <= s_ < NS] + [("ar", "ppadl"), ("ar", "ppadr")]
                    src, srcoff, srckey = pbuf, base, rkeys
                    Ts = [T0, T1]
                    d = 1
                    for step in range(g + 1):
                        dst = Ts[step % 2]
                        P.op("dve", lambda e, dst=dst, src=src, srcoff=srcoff, d=d, L=L: e.tensor_tensor(
                            out=dst[:, d:L], in0=src[:, srcoff + d:srcoff + L], in1=src[:, srcoff:srcoff + L - d], op=ALU.add),
                            reads=srckey, writes=[("ar", "T", step % 2)])
                        src, srcoff, srckey = dst, 0, [("ar", "T", step % 2)]
                        d *= 2
                    o_ = PP + w_ // 2 - 1
                    pl = pooled[s % 2]
                    plk = ("ar", "pooled", s % 2)
                    P.op("dve", lambda e, src=src, o_=o_, pl=pl, s=s, w_=w_: e.scalar_tensor_tensor(
                        out=pl[:, :], in0=src[:, o_:o_ + N], scalar=1.0 / w_, in1=pbuf[:, PP + s * N:PP + (s + 1) * N],
                        op0=ALU.mult, op1=ALU.subtract), reads=srckey + [("ar", "p", s)], writes=[plk])
                    eo = blk * EB + 2 * EDGE + g * 2 * PEDGE
                    if s == 0 or s == NS - 1:
                        c0 = POFF if s == 0 else N - POFF - PEDGE
                        et = edge[:, eo:eo + PEDGE] if s == 0 else edge[:, eo + PEDGE:eo + 2 * PEDGE]
                        tt, tk = tmp_tile()
                        P.op("dve", lambda e, src=src, o_=o_, c0=c0, et=et, tt=tt: e.tensor_tensor(
                            out=tt[:, 0:PEDGE], in0=src[:, o_ + c0:o_ + c0 + PEDGE], in1=et, op=ALU.mult),
                            reads=srckey + [("edge",)], writes=[tk])
                        P.op("dve", lambda e, c0=c0, tt=tt, pl=pl, s=s: e.tensor_tensor(
                            out=pl[:, c0:c0 + PEDGE], in0=tt[:, 0:PEDGE],
                            in1=pbuf[:, PP + s * N + c0:PP + s * N + c0 + PEDGE], op=ALU.subtract),
                            reads=[tk, ("ar", "p", s)], writes=[plk])

                def poolmm_step(g, s):
                    pl = pooled[s % 2]
                    plk = ("ar", "pooled", s % 2)
                    pO, pOk = bank()
                    mm_group(pO[:, :N], pOk, [(wplt[:, g, :], pl[:, :])], [("ar", "wpl"), plk])
                    P.op("act", lambda e, pO=pO, g=g, s=s: e.activation(
                        out=ybuf[:, 4 + g, cols(s)], in_=pO[:, :N], func=AF.Identity, scale=cv_ap(("pscale", i), g, 1)),
                        reads=[pOk, ("cvec",)], writes=[("ar", "y", 4 + g, s)])

                def CV_step(a, s, wc, wck, wv, wvk):
                    cb = cvb[a % 2]
                    pC, pCk = bank()
                    pV, pVk = bank()
                    hk = [("h", k, s) for k in range(KC)]
                    mm_group(pC[:, :N], pCk, [(wc[:, k, :], hs[:, k, cols(s)]) for k in range(KC)], [wck] + hk)
                    mm_group(pV[:, :N], pVk, [(wv[:, k, :], hs[:, k, cols(s)]) for k in range(KC)], [wvk] + hk)
                    tt, tk = tmp_tile()
                    P.op("act", lambda e, tt=tt, pC=pC: e.activation(out=tt[:, :], in_=pC[:, :N], func=AF.Copy),
                         reads=[pCk], writes=[tk])
                    P.op("dve", lambda e, tt=tt, pV=pV, cb=cb, s=s: e.tensor_tensor(
                        out=cb[:, 1 + s * N:1 + (s + 1) * N], in0=tt[:, :], in1=pV[:, :N], op=ALU.mult),
                        reads=[tk, pVk], writes=[("ar", "cv", a % 2, s)])
                    if s == 0:
                        P.op("dve", lambda e, cb=cb: e.tensor_tensor(out=cb[:, 1:1 + EDGE], in0=cb[:, 1:1 + EDGE],
                                                                    in1=edge[:, eo_m:eo_m + EDGE], op=ALU.mult),
                             reads=[("edge",), ("ar", "cv", a % 2, 0)], writes=[("ar", "cv", a % 2, 0)])
                    if s == NS - 1:
                        P.op("dve", lambda e, cb=cb: e.tensor_tensor(out=cb[:, 1 + TB - EDGE:1 + TB], in0=cb[:, 1 + TB - EDGE:1 + TB],
                                                                    in1=edge[:, eo_m + EDGE:eo_m + 2 * EDGE], op=ALU.mult),
                             reads=[("edge",), ("ar", "cv", a % 2, NS - 1)], writes=[("ar", "cv", a % 2, NS - 1)])

                def diag_build(a):
                    for k in range(3):
                        P.op("dve", lambda e, k=k, a=a: e.tensor_scalar(
                            out=dg3s[a % 2][:, k, :], in0=identb[:, :], scalar1=cv_ap(("conv", i), k * 4 + a, 1), scalar2=None,
                            op0=ALU.mult), reads=[("identb",), ("cvec",)], writes=[dg3k[a % 2]])

                def conv_step(a, s, wb, wbk):
                    cb = cvb[a % 2]
                    dg3 = dg3s[a % 2]
                    pY, pYk = bank()
                    pB, pBk = bank()
                    rk = [("ar", "cv", a % 2, s_) for s_ in (s - 1, s, s + 1) if 0 <= s_ < NS]
                    rk += [("ar", "cvpadl", a % 2), ("ar", "cvpadr", a % 2), dg3k[a % 2]]
                    mm_group(pY[:, :N], pYk, [(dg3[:, k, :], cb[:, s * N + k:s * N + k + N]) for k in range(3)], rk)
                    mm_group(pB[:, :N], pBk, [(wb[:, k, :], hs[:, k, cols(s)]) for k in range(KC)],
                             [wbk] + [("h", k, s) for k in range(KC)])
                    tt, tk = tmp_tile()
                    P.op("act", lambda e, tt=tt, pY=pY: e.activation(out=tt[:, :], in_=pY[:, :N], func=AF.Copy),
                         reads=[pYk], writes=[tk])
                    P.op("dve", lambda e, tt=tt, pB=pB, a=a, s=s: e.tensor_tensor(
                        out=ybuf[:, a, cols(s)], in0=tt[:, :], in1=pB[:, :N], op=ALU.mult),
                        reads=[tk, pBk], writes=[("ar", "y", a, s)])

                def wsl(c0):
                    return win[:, c0:c0 + 128]

                (wc, wck), (wv, wvk), (wp, wpk) = get_tiles([(wsl(512), KC), (wsl(1024), KC), (wsl(1536), KC)])
                def first_pass(s):
                    CV_step(0, s, wc, wck, wv, wvk)
                    P_step(0, s, wp, wpk)
                norm_call(first_pass)
                wrelease(3)
                diag_build(0)
                for a in range(4):
                    specs = []
                    if a >= 1:
                        specs += [(wsl(512 + a * 128), KC), (wsl(1024 + a * 128), KC)]
                    specs += [(wsl((a) * 128), KC)]
                    if a + 1 < 4:
                        specs += [(wsl(1536 + (a + 1) * 128), KC)]
                    tl = get_tiles(specs)
                    if a >= 1:
                        (wc, wck), (wv, wvk) = tl[0], tl[1]
                        tl = tl[2:]
                    (wb, wbk) = tl[0]
                    wpn = tl[1] if a + 1 < 4 else None
                    for s in range(NS):
                        if a >= 1:
                            CV_step(a, s, wc, wck, wv, wvk)
                        chain_step(a, s)
                        if s >= 1:
                            poolmm_step(a, s - 1)
                        if a == 0:
                            if s >= 1:
                                conv_step(0, s - 1, wb, wbk)
                    poolmm_step(a, NS - 1)
                    if a == 0:
                        conv_step(0, NS - 1, wb, wbk)
                        wrelease(1)
                    else:
                        wrelease(2)
                        diag_build(a)
                        for s in range(NS):
                            conv_step(a, s, wb, wbk)
                        wrelease(1)
                    if wpn is not None:
                        for s in range(NS):
                            P_step(a + 1, s, wpn[0], wpn[1])
                proj_residual(l, lambda m: ab_w_out[i][:, m * 128:(m + 1) * 128], KC,
                              lambda k, s: ybuf[:, k, cols(s)], lambda k, s: ("ar", "y", k, s), 2, acc_stats=True)

            def mixer_odd(l, blk, norm_call):
                P.mark("mixer_odd")
                i = l // 2
                P.new_epoch()
                AR.reset()
                zb = AR.take(KC * (TB + 2 * ZP), BF16).rearrange("p (a b) -> p a b", a=KC)
                dgs = [AR.take(31 * 128, BF16).rearrange("p (a b) -> p a b", a=31) for _ in range(2)]
                for a in range(KC):
                    P.op("dve", lambda e, a=a: e.memset(zb[:, a, 0:ZP], 0.0), writes=[("ar", "zpadl", a)])
                    P.op("dve", lambda e, a=a: e.memset(zb[:, a, ZP + TB:ZP + TB + ZP], 0.0), writes=[("ar", "zpadr", a)])
                w1 = cf_w_pw1[i]
                for a in range(KC):
                    (wa, wak), (wg, wgk) = get_tiles([(w1[:, a * 128:(a + 1) * 128], KC),
                                                      (w1[:, D + a * 128:D + (a + 1) * 128], KC)])

                    def glu_step(s, a=a, wa=wa, wak=wak, wg=wg, wgk=wgk):
                        pA, pAk = bank()
                        pG, pGk = bank()
                        hk = [("h", k, s) for k in range(KC)]
                        mm_group(pA[:, :N], pAk, [(wa[:, k, :], hs[:, k, cols(s)]) for k in range(KC)], [wak] + hk)
                        mm_group(pG[:, :N], pGk, [(wg[:, k, :], hs[:, k, cols(s)]) for k in range(KC)], [wgk] + hk)
                        tt, tk = tmp_tile()
                        P.op("act", lambda e, tt=tt, pG=pG, a=a: e.activation(
                            out=tt[:, :], in_=pG[:, :N], func=AF.Sigmoid, bias=cv_ap(("bpw1", i), 8 + a, 1), scale=1.0),
                            reads=[pGk, ("cvec",)], writes=[tk])
                        P.op("dve", lambda e, tt=tt, pA=pA, a=a, s=s: e.scalar_tensor_tensor(
                            out=zb[:, a, ZP + s * N:ZP + (s + 1) * N], in0=pA[:, :N], scalar=cv_ap(("bpw1", i), a, 1),
                            in1=tt[:, :], op0=ALU.add, op1=ALU.mult),
                            reads=[pAk, tk, ("cvec",)], writes=[("ar", "z", a, s)])
                    if a == 0:
                        norm_call(glu_step)
                    else:
                        for s in range(NS):
                            glu_step(s)
                    wrelease(2)
                    edge_mask("dve", lambda c0, c1, a=a: zb[:, a, ZP + c0:ZP + c1], lambda s_, a=a: ("ar", "z", a, s_), blk)
                for a in range(KC):
                    dgt = dgs[a % 2]
                    for k in range(31):
                        P.op("dve", lambda e, k=k, a=a, dgt=dgt: e.tensor_scalar(
                            out=dgt[:, k, :], in0=identb[:, :], scalar1=cv_ap(("wdw", i), k * 8 + a, 1), scalar2=None,
                            op0=ALU.mult), reads=[("identb",), ("cvec",)], writes=[("ar", "dg", a % 2, k)])
                    for s in range(NS):
                        pZ, pZk = bank()
                        rk = [("ar", "z", a, s_) for s_ in (s - 1, s, s + 1) if 0 <= s_ < NS]
                        rk += [("ar", "zpadl", a), ("ar", "zpadr", a)] + [("ar", "dg", a % 2, k) for k in range(31)]
                        mm_group(pZ[:, :N], pZk, [(dgt[:, k, :], zb[:, a, s * N + k:s * N + k + N]) for k in range(31)], rk)
                        P.op("act", lambda e, pZ=pZ, a=a, s=s: e.activation(
                            out=hs[:, a, cols(s)], in_=pZ[:, :N], func=AF.Identity, bias=cv_ap(("bdw", i), a, 1), scale=1.0),
                            reads=[pZk, ("cvec",)], writes=[("h", a, s)])
                c1 = AR.take(N, F32)
                c2 = AR.take(N, F32)
                c3 = AR.take(N, F32)

                def rn(s):
                    return (2, 3) if s % 2 == 0 else (4, 5)

                def ln_stats(s):
                    for a in range(KC):
                        sqt, sqk = sq_tile()
                        P.op("dve", lambda e, a=a, s=s, sqt=sqt: e.tensor_tensor(
                            out=sqt[:, :], in0=hs[:, a, cols(s)], in1=hs[:, a, cols(s)], op=ALU.mult),
                            reads=[("h", a, s)], writes=[sqk])
                        P.op("pe", lambda e, a=a, s=s: e.matmul(ps[0][:, :N], onesD[:, :], hs[:, a, cols(s)], start=(a == 0), stop=(a == KC - 1)),
                             reads=[("h", a, s), ("onesD",)], writes=[("ps", 0)])
                        P.op("pe", lambda e, a=a, sqt=sqt: e.matmul(ps[1][:, :N], onesD[:, :], sqt[:, :], start=(a == 0), stop=(a == KC - 1)),
                             reads=[sqk, ("onesD",)], writes=[("ps", 1)])

                def chainA(s):
                    P.op("act", lambda e: e.activation(out=c1[:, :], in_=ps[0][:, :N], func=AF.Square),
                         reads=[("ps", 0)], writes=[("ar", "c1")])
                    P.op("act", lambda e: e.activation(out=c3[:, :], in_=ps[0][:, :N], func=AF.Copy),
                         reads=[("ps", 0)], writes=[("ar", "c3")])

                def chainB(s):
                    P.op("dve", lambda e: e.tensor_tensor(out=c2[:, :], in0=ps[1][:, :N], in1=c1[:, :], op=ALU.subtract),
                         reads=[("ps", 1), ("ar", "c1")], writes=[("ar", "c2")])

                def chainC(s):
                    rb, nb_ = rn(s)
                    P.op("act", lambda e: e.activation(out=c1[:, :], in_=c2[:, :], func=AF.Ln, bias=eps_l[:, 0:1], scale=1.0),
                         reads=[("ar", "c2"), ("eps",)], writes=[("ar", "c1")])
                    P.op("act", lambda e, rb=rb: e.activation(out=ps[rb][:, :N], in_=c1[:, :], func=AF.Exp, scale=-0.5),
                         reads=[("ar", "c1")], writes=[("ps", rb)])
                    P.op("act", lambda e: e.activation(out=c2[:, :], in_=c1[:, :], func=AF.Exp, scale=-0.5),
                         reads=[("ar", "c1")], writes=[("ar", "c2")])

                def chainD(s):
                    rb, nb_ = rn(s)
                    P.op("dve", lambda e, nb_=nb_: e.scalar_tensor_tensor(
                        out=ps[nb_][:, :N], in0=c3[:, :], scalar=-1.0, in1=c2[:, :], op0=ALU.mult, op1=ALU.mult),
                        reads=[("ar", "c3"), ("ar", "c2")], writes=[("ps", nb_)])

                def ln_norm_half(s, half):
                    rb, nb_ = rn(s)
                    for a in range(half * 4, half * 4 + 4):
                        ta, tak = tmp_tile()
                        P.op("dve", lambda e, a=a, s=s, ta=ta, rb=rb: e.tensor_tensor(
                            out=ta[:, :], in0=hs[:, a, cols(s)], in1=ps[rb][:, :N], op=ALU.mult),
                            reads=[("h", a, s), ("ps", rb)], writes=[tak])
                        P.op("dve", lambda e, ta=ta, nb_=nb_: e.tensor_tensor(out=ta[:, :], in0=ta[:, :], in1=ps[nb_][:, :N], op=ALU.add),
                             reads=[tak, ("ps", nb_)], writes=[tak])
                        P.op("act", lambda e, a=a, s=s, ta=ta: e.activation(
                            out=zb[:, a, ZP + s * N:ZP + (s + 1) * N], in_=ta[:, :], func=AF.Silu,
                            bias=cv_ap(("lnb", i), a, 1), scale=cv_ap(("lng", i), a, 1)),
                            reads=[tak, ("cvec",)], writes=[("ar", "z", a, s)])

                ln_stats(0)
                chainA(0)
                chainB(0)
                chainC(0)
                chainD(0)
                for s in range(NS):
                    nxt = s + 1 < NS
                    if nxt:
                        ln_stats(s + 1)
                        chainA(s + 1)
                    ln_norm_half(s, 0)
                    if nxt:
                        chainB(s + 1)
                        chainC(s + 1)
                    ln_norm_half(s, 1)
                    if nxt:
                        chainD(s + 1)
                P.op("dve", lambda e: e.tensor_tensor(out=gbt[:, :], in0=dv[:, l, 2, :], in1=cv_ap(("bpw2", i), 0, 8), op=ALU.mult),
                     reads=dvk(l) + [("cvec",)], writes=[("gbt",)])
                for m in range(KC):
                    P.op("pool", lambda e, m=m: e.tensor_scalar(
                        out=xs[:, m, :], in0=xs[:, m, :], scalar1=1.0, scalar2=gbt[:, m:m + 1], op0=ALU.mult, op1=ALU.add),
                        reads=[("gbt",)] + [("x", m, s_) for s_ in range(NS)], writes=[("x", m, s_) for s_ in range(NS)])
                proj_residual(l, lambda m: cf_w_pw2[i][:, m * 128:(m + 1) * 128], KC,
                              lambda k, s: zb[:, k, ZP + s * N:ZP + (s + 1) * N], lambda k, s: ("ar", "z", k, s), 2,
                              acc_stats=True)

            def ffn(l, norm_call):
                P.mark("ffn")
                P.new_epoch()
                AR.reset()
                ab = AR.take(JG * TB, BF16).rearrange("p (a b) -> p a b", a=JG)
                for hf in range(NG):
                    for jj in range(JG):
                        j = hf * JG + jj
                        (wgt, wgk), (wut, wuk) = get_tiles([(ffn_w_gate[l][:, j * 128:(j + 1) * 128], KC),
                                                            (ffn_w_up[l][:, j * 128:(j + 1) * 128], KC)])

                        def gu_step(s, jj=jj, wgt=wgt, wgk=wgk, wut=wut, wuk=wuk):
                            pG, pGk = bank()
                            pU, pUk = bank()
                            hk = [("h", k, s) for k in range(KC)]
                            mm_group(pG[:, :N], pGk, [(wgt[:, k, :], hs[:, k, cols(s)]) for k in range(KC)], [wgk] + hk)
                            mm_group(pU[:, :N], pUk, [(wut[:, k, :], hs[:, k, cols(s)]) for k in range(KC)], [wuk] + hk)
                            tt, tk = tmp_tile()
                            P.op("act", lambda e, tt=tt, pG=pG: e.activation(out=tt[:, :], in_=pG[:, :N], func=AF.Silu),
                                 reads=[pGk], writes=[tk])
                            P.op("dve", lambda e, tt=tt, pU=pU, jj=jj, s=s: e.tensor_tensor(
                                out=ab[:, jj, cols(s)], in0=tt[:, :], in1=pU[:, :N], op=ALU.mult),
                                reads=[tk, pUk], writes=[("ar", "a", jj, s)])
                        if j == 0:
                            norm_call(gu_step)
                        else:
                            for s in range(NS):
                                gu_step(s)
                        wrelease(2)
                    proj_residual(l, lambda m, hf=hf: ffn_w_down[l][hf * JG * 128:(hf + 1) * JG * 128, m * 128:(m + 1) * 128], JG,
                                  lambda k, s: ab[:, k, cols(s)], lambda k, s: ("ar", "a", k, s), 5,
                                  acc_stats=(hf == NG - 1))

            def final_out(blk):
                P.mark("final")
                P.new_epoch()
                AR.reset()
                ot = [AR.take(N, F32) for _ in range(4)]
                oi = 0
                for s in range(NS):
                    if len(layers) > 0 and stop is None:
                        rb = rstd_from_acc(s)
                    else:
                        stats_rstd(lambda m, s_: xs[:, m, cols(s_)], lambda m, s_: ("x", m, s_), s, eps_r)
                        rb = 2
                    lo = max(s * N, HALO)
                    hi = min((s + 1) * N, HALO + TOK)
                    for m in range(KC):
                        o = oi % 4
                        oi += 1
                        P.op("dve", lambda e, m=m, s=s, o=o, rb=rb: e.scalar_tensor_tensor(
                            out=ot[o][:, :], in0=xs[:, m, cols(s)], scalar=cv_ap("gfin", m, 1), in1=ps[rb][:, :N],
                            op0=ALU.mult, op1=ALU.mult),
                            reads=[("x", m, s), ("ps", rb), ("cvec",)], writes=[("ar", "ot", o)])
                        P.dma("sp", f"o{o}", lambda e, m=m, s=s, o=o, lo=lo, hi=hi: e.dma_start(
                            out=yT[blk, m * 128:(m + 1) * 128, lo - HALO:hi - HALO], in_=ot[o][:, lo - s * N:hi - s * N]),
                            reads=[("ar", "ot", o)])

            def raw_out(blk):
                P.new_epoch()
                for m in range(KC):
                    P.dma("sp", f"o{m % 4}", lambda e, m=m: e.dma_start(
                        out=yT[blk, m * 128:(m + 1) * 128, :], in_=xs[:, m, HALO:HALO + TOK]),
                        reads=[("x", m, s) for s in range(NS)])

            for blk in range(nblocks):
                for m in range(KC):
                    P.dma("sp", f"x{m}", lambda e, m=m, blk=blk: e.dma_start(out=xs[:, m, :], in_=xT[blk, m * 128:(m + 1) * 128, :]),
                          writes=[("x", m, s) for s in range(NS)])
                for li, l in enumerate(layers):
                    if stop == "load":
                        break
                    if blk == 0:
                        drain_bg()
                        if li + 1 < len(layers):
                            enqueue_mod(layers[li + 1])
                    nm1 = lambda cb, l=l, li=li: norm_mod(l, 0, 1, have_stats=(li > 0), after_sub=cb)
                    if stop == "norm":
                        nm1(None)
                    if stop == "norm":
                        for m in range(KC):
                            tt, tk = tmp_tile()
                            P.op("dve", lambda e, m=m, tt=tt: e.tensor_copy(out=tt[:, :], in_=hs[:, m, 0:N]), reads=[("h", m, 0)], writes=[tk])
                            P.dma("sp", "o0", lambda e, m=m, tt=tt: e.dma_start(out=dbg_d[:, m * N:(m + 1) * N], in_=tt[:, :]), reads=[tk])
                        P.dma("sp", "o1", lambda e: e.dma_start(out=dbg_d[:, 8 * N:8 * N + 192], in_=dv[:, :, :, :].rearrange("p a b c -> p (a b c)")), reads=dvk(l))
                        P.dma("sp", "o1", lambda e: e.dma_start(out=dbg_d[:, 8 * N + 192:8 * N + 384], in_=modv[:, :, :].rearrange("p a b -> p (a b)")), reads=[("modv", l)])
                        break
                    if l % 2 == 0:
                        mixer_even(l, blk, nm1)
                    else:
                        mixer_odd(l, blk, nm1)
                    if stop == "mixer":
                        break
                    ffn(l, lambda cb, l=l: norm_mod(l, 3, 4, have_stats=True, after_sub=cb))
                if final_norm:
                    final_out(blk)
                else:
                    raw_out(blk)
                excl4[0] = False
            return wstate["specs"]

        Pd = Prog()
        plan = run(Pd, None)
        P = Prog()
        plan2 = run(P, plan)
        assert len(plan2) == len(plan)

        final_waits = {s: c for s, c in P.dmacnt.items() if s.startswith("o")}

        engmap = {"pe": "tensor", "act": "scalar", "dve": "vector", "pool": "gpsimd", "sp": "sync"}
        with nc.Block() as block:
            def make(engname):
                def body(e):
                    for item in P.q[engname]:
                        if item[0] == "wait":
                            e.wait_ge(sems[item[1]], item[2])
                        elif item[0] == "op":
                            ins = item[1](e)
                            ins.then_inc(sems[engname], 1)
                        else:
                            ins = item[1](e)
                            ins.then_inc(sems[item[2]], 16)
                    if engname == "sp":
                        for s_, c_ in final_waits.items():
                            e.wait_ge(sems[s_], c_)
                        for en in ("pe", "act", "dve"):
                            if P.cnt[en] > 0:
                                e.wait_ge(sems[en], P.cnt[en])
                return body
            block.tensor(make("pe"))
            block.scalar(make("act"))
            block.vector(make("dve"))
            block.gpsimd(make("pool"))
            block.sync(make("sp"))
        stats = {e: len(P.q[e]) for e in Prog.ENG}
        PHASE_MARKS[:] = P.marks
    return nc, stats


def _fm(v):
    v = np.asarray(v, np.float32)
    lead = v.shape[:-1]
    n = v.shape[-1] // 128
    v = v.reshape(lead + (n, 128))
    return np.moveaxis(v, -1, 0)


def _build_cvec(inp, b):
    cv = np.zeros((128, NV), np.float32)

    def put(name, arr):
        arr = np.asarray(arr, np.float32).reshape(128, -1)
        cv[:, CV[name]:CV[name] + arr.shape[1]] = arr
    for l in range(DEPTH):
        put(("gmix", l), _fm(inp["norm_mix_g"][l]))
        put(("gffn", l), _fm(inp["norm_ffn_g"][l]))
        put(("bmod", l), _fm(inp["b_mod"][l]))
    for i in range(2):
        put(("conv", i), _fm(inp["ab_conv"][i]))
        put(("pscale", i), _fm(inp["ab_pool_scale"][i]))
        put(("bpw1", i), _fm(inp["cf_b_pw1"][i]))
        put(("wdw", i), _fm(inp["cf_w_dw"][i]))
        put(("bdw", i), _fm(inp["cf_b_dw"][i]))
        put(("lng", i), _fm(inp["cf_ln_g"][i]))
        put(("lnb", i), _fm(inp["cf_ln_b"][i]))
        put(("bpw2", i), _fm(inp["cf_b_pw2"][i]))
    put("gfin", _fm(inp["final_norm_g"]))
    put("c", _fm(inp["c"][b]))
    return cv


def _build_edge(half):
    e = np.zeros((NE,), np.float32)
    for blk in range(NB):
        s0 = half * (S // 2) + blk * TOK - HALO
        pos = s0 + np.arange(TB)
        valid = ((pos >= 0) & (pos < S)).astype(np.float32)
        o = blk * EB
        e[o:o + EDGE] = valid[:EDGE]
        e[o + EDGE:o + 2 * EDGE] = valid[TB - EDGE:]
        for g in range(4):
            w = 2 << g
            left = w // 2
            right = w - 1 - left
            cnt = (np.minimum(pos + right, S - 1) - np.maximum(pos - left, 0) + 1).astype(np.float32)
            inv = np.where((pos >= 0) & (pos < S), 1.0 / np.maximum(cnt, 1.0), 1.0 / w).astype(np.float32)
            oo = o + 2 * EDGE + g * 2 * PEDGE
            e[oo:oo + PEDGE] = inv[POFF:POFF + PEDGE]
            e[oo + PEDGE:oo + 2 * PEDGE] = inv[TB - POFF - PEDGE:TB - POFF]
    return np.ascontiguousarray(np.broadcast_to(e[None, :], (128, NE)))


_CACHE = {}


def _get_nc(layers, final_norm):
    key = (tuple(layers), final_norm)
    if key not in _CACHE:
        _CACHE[key] = build_program(layers, final_norm)
    return _CACHE[key][0]


def _make_in_maps(inp, x_full):
    f32 = lambda a: np.ascontiguousarray(np.asarray(a, np.float32))
    shared = {k: f32(inp[k]) for k in ("w_mod", "ab_w_in", "ab_w_pool", "ab_w_out", "cf_w_pw1", "cf_w_pw2",
                                       "ffn_w_gate", "ffn_w_up", "ffn_w_down")}
    ident = np.eye(128, dtype=np.float32)
    in_maps = []
    for cid in range(NCORES):
        b, half = cid // 2, cid % 2
        xt = np.zeros((NB, D, TB), np.float32)
        for blk in range(NB):
            s0 = half * (S // 2) + blk * TOK - HALO
            lo, hi = max(s0, 0), min(s0 + TB, S)
            xt[blk, :, lo - s0:hi - s0] = x_full[b, lo:hi, :].T
        m = dict(shared)
        m.update({"xT": xt, "cvec": _build_cvec(inp, b), "edge": _build_edge(half), "ident": ident})
        in_maps.append(m)
    return in_maps


def _gather(res):
    out = np.empty((BATCH, S, D), np.float32)
    for cid in range(NCORES):
        b, half = cid // 2, cid % 2
        y = res.results[cid]["yT"]
        for blk in range(NB):
            t0 = half * (S // 2) + blk * TOK
            out[b, t0:t0 + TOK, :] = y[blk].T
    return out


def kernel(**inputs):
    inp = {k: np.asarray(v) for k, v in inputs.items()}
    x = np.asarray(inp["x"], np.float32)
    nc = _get_nc((0, 1, 2, 3), True)
    in_maps = _make_in_maps(inp, x)
    res = run_bass_kernel_spmd(nc, in_maps, core_ids=list(range(NCORES)))
    return _gather(res)
```

```python
import numpy as np
import concourse.bass as bass
import concourse.mybir as mybir
from concourse.bass_utils import run_bass_kernel_spmd

F32 = mybir.dt.float32
BF16 = mybir.dt.bfloat16
AF = mybir.ActivationFunctionType
ALU = mybir.AluOpType

D = 1024
S = 8192
BATCH = 4
DEPTH = 4
DFF = 2816
NCORES = 8
NB = 2
TOK = 2048
HALO = 46
TB = TOK + 2 * HALO
NS = 5
N = TB // NS
KC = 8
NJ = DFF // 128
NG = 2
JG = NJ // NG
RING = 5
EDGE = 48
PEDGE = 32
POFF = 32
ZP = 15
PP = 16

CV = {}
_off = 0


def _cv(name, n):
    global _off
    CV[name] = _off
    _off += n


for _l in range(DEPTH):
    _cv(("gmix", _l), 8)
    _cv(("gffn", _l), 8)
    _cv(("bmod", _l), 48)
for _i in range(2):
    _cv(("conv", _i), 12)
    _cv(("pscale", _i), 4)
for _i in range(2):
    _cv(("bpw1", _i), 16)
    _cv(("wdw", _i), 31 * 8)
    _cv(("bdw", _i), 8)
    _cv(("lng", _i), 8)
    _cv(("lnb", _i), 8)
    _cv(("bpw2", _i), 8)
_cv("gfin", 8)
_cv("c", 8)
NV = _off
EB = 2 * EDGE + 4 * 2 * PEDGE
NE = NB * EB


PHASE_MARKS = []


class Prog:
    ENG = ("pe", "act", "dve", "pool", "sp")

    def __init__(self):
        self.q = {e: [] for e in self.ENG}
        self.cnt = {e: 0 for e in self.ENG}
        self.waited = {e: {} for e in self.ENG}
        self.res = {}
        self.dmacnt = {}
        self.epoch = {}
        self.arena_prefixes = set()
        self.nmm = 0
        self.marks = []

    def _res(self, k):
        r = self.res.get(k)
        if r is None:
            if k[0] in self.arena_prefixes:
                r = [dict(self.epoch), {}]
            else:
                r = [{}, {}]
            self.res[k] = r
        return r

    def new_epoch(self):
        ep = {e: c for e, c in self.cnt.items() if c > 0}
        for s, c in self.dmacnt.items():
            if c > 0 and not s.startswith("sg") and not s.startswith("x"):
                ep[s] = c
        self.epoch = ep
        for k in [k for k in self.res if k[0] in self.arena_prefixes]:
            del self.res[k]

    def _deps(self, reads, writes):
        deps = {}

        def add(d):
            for s, c in d.items():
                if deps.get(s, 0) < c:
                    deps[s] = c
        for k in reads:
            add(self._res(k)[0])
        for k in writes:
            r = self._res(k)
            add(r[0])
            add(r[1])
        return deps

    def _emit_waits(self, eng, deps):
        w = self.waited[eng]
        for s, c in deps.items():
            if w.get(s, 0) < c:
                self.q[eng].append(("wait", s, c))
                w[s] = c

    def _commit(self, tok, reads, writes):
        s, c = tok
        for k in reads:
            r = self._res(k)
            if r[1].get(s, 0) < c:
                r[1][s] = c
        for k in writes:
            r = self._res(k)
            r[0] = {s: c}
            r[1] = {}

    def mark(self, label):
        self.marks.append((label, self.nmm))

    def op(self, eng, fn, reads=(), writes=(), nmm=1):
        if eng == "pe":
            self.nmm += nmm
        deps = self._deps(reads, writes)
        if eng == "pe":
            deps.pop("pe", None)
        self._emit_waits(eng, deps)
        self.cnt[eng] += 1
        tok = (eng, self.cnt[eng])
        self.q[eng].append(("op", fn, eng))
        self._commit(tok, reads, writes)
        return tok

    def dma(self, eng, sem, fn, reads=(), writes=()):
        self._emit_waits(eng, self._deps(reads, writes))
        self.dmacnt[sem] = self.dmacnt.get(sem, 0) + 16
        tok = (sem, self.dmacnt[sem])
        self.q[eng].append(("dma", fn, sem))
        self._commit(tok, reads, writes)
        return tok


def build_program(layers=(0, 1, 2, 3), final_norm=True, nblocks=NB, stop=None):
    nc = bass.Bass("TRN2", target_bir_lowering=False)
    dr = {}

    def din(name, shape):
        dr[name] = nc.dram_tensor(name, list(shape), F32, kind="ExternalInput").ap()
        return dr[name]

    xT = din("xT", [NB, D, TB])
    cvec_d = din("cvec", [128, NV])
    edge_d = din("edge", [128, NE])
    ident_d = din("ident", [128, 128])
    w_mod = din("w_mod", [DEPTH, D, 6 * D])
    ab_w_in = din("ab_w_in", [2, D, 2048])
    ab_w_pool = din("ab_w_pool", [2, 4, 128, 128])
    ab_w_out = din("ab_w_out", [2, D, D])
    cf_w_pw1 = din("cf_w_pw1", [2, D, 2 * D])
    cf_w_pw2 = din("cf_w_pw2", [2, D, D])
    ffn_w_gate = din("ffn_w_gate", [DEPTH, D, DFF])
    ffn_w_up = din("ffn_w_up", [DEPTH, D, DFF])
    ffn_w_down = din("ffn_w_down", [DEPTH, DFF, D])
    yT = nc.dram_tensor("yT", [NB, D, TOK], F32, kind="ExternalOutput").ap()
    dbg_d = nc.dram_tensor("dbg", [128, 8 * N + 192 + 192], F32, kind="ExternalOutput").ap() if stop else None

    import contextlib
    st = contextlib.ExitStack()
    with st:
        def sb(name, shape, dt):
            return st.enter_context(nc.sbuf_tensor(name, list(shape), dt))

        xs = sb("xs", [128, KC, TB], F32)
        hs = sb("hs", [128, KC, TB], BF16)
        ARENA_E = 29056
        arena = sb("arena", [128, ARENA_E], BF16)
        wring = [sb(f"wr{r}", [128, JG, 128], BF16) for r in range(RING)]
        sqr = [sb(f"sq{r}", [128, N], BF16) for r in range(4)]
        cvec = sb("cvecs", [128, NV], F32)
        edge = sb("edges", [128, NE], F32)
        identf = sb("identf", [128, 128], F32)
        identb = sb("identb", [128, 128], BF16)
        onesD = sb("onesD", [128, 128], BF16)
        stg = [sb(f"stg{r}", [128, KC, 128], F32) for r in range(2)]
        gbt = sb("gbt", [128, 8], F32)
        modv = sb("modv", [128, DEPTH, 48], F32)
        dv = sb("dv", [128, DEPTH, 6, 8], F32)
        cact = sb("cact", [128, KC, 2], F32)
        eps_r = sb("eps_r", [128, 1], F32)
        eps_l = sb("eps_l", [128, 1], F32)
        tmpf = [sb(f"tmpf{r}", [128, N], F32) for r in range(4)]
        wmst = [sb(f"wmst{r}", [128, KC, 128], F32) for r in range(2)]
        mtmp = sb("mtmp", [128, 128], F32)
        macc = [sb(f"macc{r}", [128, 128], F32) for r in range(2)]
        onescol = sb("onescol", [128, 2], F32)
        ps = [st.enter_context(nc.psum_tensor(f"ps{b}", [128, 512], F32)) for b in range(8)]

        sems = {}
        for e in Prog.ENG:
            sems[e] = st.enter_context(nc.semaphore(f"s_{e}"))
        dma_sem_names = ["sg0", "sg1"] + [f"x{m}" for m in range(KC)] + \
            ["o0", "o1", "o2", "o3", "wm0", "wm1", "cst"]
        for s_ in dma_sem_names:
            sems[s_] = st.enter_context(nc.semaphore(f"d_{s_}"))

        class Arena:
            def __init__(self):
                self.off = 0

            def reset(self):
                self.off = 0

            def take(self, nelem, dt):
                if dt == F32:
                    self.off = (self.off + 1) // 2 * 2
                    v = arena[:, self.off:self.off + 2 * nelem].bitcast(F32)
                    self.off += 2 * nelem
                else:
                    v = arena[:, self.off:self.off + nelem]
                    self.off += nelem
                assert self.off <= ARENA_E, (self.off, ARENA_E)
                return v
        AR = Arena()

        def run(P, wplan):
            P.arena_prefixes = {"ar"}
            wstate = {"next_load": 0, "next_use": 0, "specs": [], "consumed": 0}
            tf_i = [0]
            sq_i = [0]
            ring_i = [0]

            def tmp_tile():
                i = tf_i[0] % 4
                tf_i[0] += 1
                return tmpf[i], ("tmpf", i)

            def sq_tile():
                i = sq_i[0] % 4
                sq_i[0] += 1
                return sqr[i], ("sq", i)

            excl4 = [False]

            def bank():
                b = 4 + ring_i[0] % 4
                ring_i[0] += 1
                if excl4[0] and b == 4:
                    b = 4 + ring_i[0] % 4
                    ring_i[0] += 1
                return ps[b], ("ps", b)

            def cv_ap(name, j=0, n=1):
                o = CV[name] + j
                return cvec[:, o:o + n]

            import collections as _c
            bgq = _c.deque()
            BG_RATE = 2

            def drain_bg(n=None):
                k = 0
                while bgq and (n is None or k < n):
                    bgq.popleft()()
                    k += 1

            stg_i = [0]

            def issue_load(j):
                src, kcn = wplan[j]
                slot = j % RING
                k0 = 0
                while k0 < kcn:
                    n_ = min(KC, kcn - k0)
                    si = stg_i[0] % 2
                    stg_i[0] += 1
                    P.dma("sp", f"sg{si}",
                          lambda e, si=si, n_=n_, k0=k0, src=src: e.dma_start(out=stg[si][:, 0:n_, :], in_=src[:, k0:k0 + n_, :]),
                          writes=[("stg", si)])
                    P.op("pool", lambda e, si=si, n_=n_, k0=k0, slot=slot: e.tensor_copy(
                        out=wring[slot][:, k0:k0 + n_, :], in_=stg[si][:, 0:n_, :]),
                        reads=[("stg", si)], writes=[("w", slot)])
                    k0 += n_

            def get_tiles(specs):
                assert len(specs) <= RING
                j0 = wstate["next_use"]
                outs = []
                for (src2d, kcn) in specs:
                    j = wstate["next_use"]
                    wstate["next_use"] += 1
                    src = src2d.rearrange("(kc p) n -> p kc n", p=128)
                    wstate["specs"].append((src, kcn))
                    slot = j % RING
                    outs.append((wring[slot], ("w", slot)))
                wstate["consumed"] = j0
                wpump()
                return outs

            def wpump():
                if wplan is not None:
                    lim = min(len(wplan), wstate["consumed"] + RING)
                    while wstate["next_load"] < lim:
                        issue_load(wstate["next_load"])
                        wstate["next_load"] += 1

            def wrelease(k):
                wstate["consumed"] += k
                assert wstate["consumed"] <= wstate["next_use"]
                wpump()

            def get_tile(src2d, kcn):
                return get_tiles([(src2d, kcn)])[0]

            def mm_group(pst, pkey, pairs, reads, extra_first=None):
                def fn(e, pairs=pairs, pst=pst):
                    ins = None
                    n_ = len(pairs)
                    for i_, (l_, r_) in enumerate(pairs):
                        ins = e.matmul(pst, l_, r_, start=(i_ == 0), stop=(i_ == n_ - 1))
                    return ins
                tok = P.op("pe", fn, reads=reads, writes=[pkey], nmm=len(pairs))
                drain_bg(BG_RATE)
                return tok

            P.dma("sp", "cst", lambda e: e.dma_start(out=cvec[:, :], in_=cvec_d[:, :]), writes=[("cvec",)])
            P.dma("sp", "cst", lambda e: e.dma_start(out=edge[:, :], in_=edge_d[:, :]), writes=[("edge",)])
            P.dma("sp", "cst", lambda e: e.dma_start(out=identf[:, :], in_=ident_d[:, :]), writes=[("identf",)])
            for k in (("cvec",), ("edge",), ("identf",)):
                P.res[k][0] = {"cst": P.dmacnt["cst"]}
            P.op("dve", lambda e: e.tensor_copy(out=identb[:, :], in_=identf[:, :]), reads=[("identf",)], writes=[("identb",)])
            P.op("dve", lambda e: e.memset(onesD[:, :], 1.0 / D), writes=[("onesD",)])
            P.op("dve", lambda e: e.memset(eps_r[:, :], 1e-6), writes=[("eps",)])
            P.op("dve", lambda e: e.memset(eps_l[:, :], 1e-5), writes=[("eps",)])
            for d_ in range(2):
                P.op("act", lambda e, d_=d_: e.activation(out=cact[:, :, d_], in_=cv_ap("c", 0, 8), func=AF.Silu),
                     reads=[("cvec",)], writes=[("cact", d_)])

            P.op("dve", lambda e: e.memset(onescol[:, :], 1.0), writes=[("onescol",)])
            mod_i = [0]

            def enqueue_mod(l):
                pending_pe = []
                for q in range(48):
                    gi_ = mod_i[0]
                    mod_i[0] += 1
                    slot = gi_ % 2
                    src = w_mod[l, :, q * 128:(q + 1) * 128].rearrange("(kc p) n -> p kc n", p=128)
                    bgq.append(lambda slot=slot, src=src: P.dma(
                        "sp", f"wm{slot}", lambda e: e.dma_start(out=wmst[slot][:, :, :], in_=src), writes=[("wmst", slot)]))
                    for kc in range(KC):
                        if slot == 1:
                            if kc == 0:
                                bgq.append(lambda slot=slot: P.op("dve", lambda e: e.tensor_scalar(
                                    out=macc[slot][:, :], in0=wmst[slot][:, 0, :], scalar1=cact[:, 0, 0:1], scalar2=None,
                                    op0=ALU.mult), reads=[("wmst", slot), ("cact", 0)], writes=[("macc", slot)]))
                            else:
                                bgq.append(lambda slot=slot, kc=kc: P.op("dve", lambda e: e.scalar_tensor_tensor(
                                    out=macc[slot][:, :], in0=wmst[slot][:, kc, :], scalar=cact[:, kc, 0:1], in1=macc[slot][:, :],
                                    op0=ALU.mult, op1=ALU.add),
                                    reads=[("wmst", slot), ("cact", 0), ("macc", slot)], writes=[("macc", slot)]))
                            continue
                        if kc == 0:
                            bgq.append(lambda slot=slot: P.op("pool", lambda e: e.tensor_scalar(
                                out=macc[slot][:, :], in0=wmst[slot][:, 0, :], scalar1=cact[:, 0, 0:1], scalar2=1.0,
                                op0=ALU.mult, op1=ALU.mult), reads=[("wmst", slot), ("cact", 0)], writes=[("macc", slot)]))
                        else:
                            bgq.append(lambda slot=slot, kc=kc: P.op("pool", lambda e: e.tensor_scalar(
                                out=mtmp[:, :], in0=wmst[slot][:, kc, :], scalar1=cact[:, kc, 0:1], scalar2=1.0,
                                op0=ALU.mult, op1=ALU.mult), reads=[("wmst", slot), ("cact", 0)], writes=[("mtmp",)]))
                            bgq.append(lambda slot=slot: P.op("pool", lambda e: e.tensor_tensor(
                                out=macc[slot][:, :], in0=macc[slot][:, :], in1=mtmp[:, :], op=ALU.add),
                                reads=[("mtmp",), ("macc", slot)], writes=[("macc", slot)]))

                    def pe_thunk(slot=slot, q=q, l=l):
                        pst, pkey = bank()
                        P.op("pe", lambda e: e.matmul(pst[:, 0:2], macc[slot][:, :], onescol[:, :], start=True, stop=True),
                             reads=[("macc", slot), ("onescol",)], writes=[pkey])
                        P.op("dve", lambda e: e.tensor_tensor(out=modv[:, l, q:q + 1], in0=pst[:, 0:1],
                                                             in1=cv_ap(("bmod", l), q, 1), op=ALU.add),
                             reads=[pkey, ("cvec",)], writes=[("modv", l)])
                    pending_pe.append(pe_thunk)
                    if len(pending_pe) > 1:
                        bgq.append(pending_pe.pop(0))
                bgq.append(pending_pe.pop(0))

                def fin(l=l):
                    for (di, mo, gname) in ((0, 8, "gmix"), (3, 32, "gffn")):
                        P.op("dve", lambda e, di=di, mo=mo, gname=gname: e.scalar_tensor_tensor(
                            out=dv[:, l, di, :], in0=modv[:, l, mo:mo + 8], scalar=1.0,
                            in1=cv_ap((gname, l), 0, 8), op0=ALU.add, op1=ALU.mult),
                            reads=[("modv", l), ("cvec",)], writes=[("dv", l, di)])
                    for (di, mo) in ((1, 0), (2, 16), (4, 24), (5, 40)):
                        P.op("dve", lambda e, di=di, mo=mo: e.tensor_copy(out=dv[:, l, di, :], in_=modv[:, l, mo:mo + 8]),
                             reads=[("modv", l)], writes=[("dv", l, di)])
                bgq.append(fin)

            def mod_prologue(l):
                P.mark("prologue")
                P.new_epoch()
                AR.reset()
                NSL = 6
                Wst = [AR.take(KC * 128, F32).rearrange("p (a b) -> p a b", a=KC) for _ in range(NSL)]
                Acc = [AR.take(128, F32) for _ in range(NSL)]
                pat = ("dve", "act", "pool", "dve", "act", "dve", "act")
                Tq = [AR.take(128, F32) for _ in range(2)]
                tq_i = 0
                pend = []
                for q in range(48):
                    while pend and pend[0][0] <= q - 3:
                        pend.pop(0)[1]()
                    sl = q % NSL
                    eng = pat[q % len(pat)]
                    src = w_mod[l, :, q * 128:(q + 1) * 128].rearrange("(kc p) n -> p kc n", p=128)
                    P.dma("sp", ("wm0", "wm1", "o0", "o1", "o2", "o3")[sl], lambda e, sl=sl, src=src: e.dma_start(out=Wst[sl][:, :, :], in_=src),
                          writes=[("ar", "wst", sl)])
                    if eng == "act":
                        pst, pkey = bank()
                        for kc in range(KC):
                            tq = Tq[tq_i % 2]
                            tqk = ("ar", "tq", tq_i % 2)
                            tq_i += 1
                            P.op("act", lambda e, sl=sl, kc=kc, tq=tq: e.activation(
                                out=tq[:, :], in_=Wst[sl][:, kc, :], func=AF.Identity, scale=cact[:, kc, 0:1]),
                                reads=[("ar", "wst", sl), ("cact", 0)], writes=[tqk])
                            P.op("pe", lambda e, kc=kc, tq=tq, pst=pst: e.matmul(
                                pst[:, 0:2], tq[:, :], onescol[:, :], start=(kc == 0), stop=(kc == KC - 1)),
                                reads=[tqk, ("onescol",)], writes=[pkey])
                        P.op("dve", lambda e, pst=pst, q=q: e.tensor_tensor(out=modv[:, l, q:q + 1], in0=pst[:, 0:1],
                                                                          in1=cv_ap(("bmod", l), q, 1), op=ALU.add),
                             reads=[pkey, ("cvec",), ("ar", "wst", sl)], writes=[("modv", l)])
                        continue
                    for kc in range(KC):
                        if eng == "dve":
                            if kc == 0:
                                P.op("dve", lambda e, sl=sl: e.tensor_scalar(
                                    out=Acc[sl][:, :], in0=Wst[sl][:, 0, :], scalar1=cact[:, 0, 0:1], scalar2=None, op0=ALU.mult),
                                    reads=[("ar", "wst", sl), ("cact", 0)], writes=[("ar", "acc", sl)])
                            else:
                                P.op("dve", lambda e, sl=sl, kc=kc: e.scalar_tensor_tensor(
                                    out=Acc[sl][:, :], in0=Wst[sl][:, kc, :], scalar=cact[:, kc, 0:1], in1=Acc[sl][:, :],
                                    op0=ALU.mult, op1=ALU.add),
                                    reads=[("ar", "wst", sl), ("cact", 0), ("ar", "acc", sl)], writes=[("ar", "acc", sl)])
                        else:
                            if kc == 0:
                                P.op("pool", lambda e, sl=sl: e.tensor_scalar(
                                    out=Acc[sl][:, :], in0=Wst[sl][:, 0, :], scalar1=cact[:, 0, 0:1], scalar2=1.0,
                                    op0=ALU.mult, op1=ALU.mult),
                                    reads=[("ar", "wst", sl), ("cact", 0)], writes=[("ar", "acc", sl)])
                            else:
                                P.op("pool", lambda e, sl=sl, kc=kc: e.tensor_scalar(
                                    out=mtmp[:, :], in0=Wst[sl][:, kc, :], scalar1=cact[:, kc, 0:1], scalar2=1.0,
                                    op0=ALU.mult, op1=ALU.mult), reads=[("ar", "wst", sl), ("cact", 0)], writes=[("mtmp",)])
                                P.op("pool", lambda e, sl=sl: e.tensor_tensor(
                                    out=Acc[sl][:, :], in0=Acc[sl][:, :], in1=mtmp[:, :], op=ALU.add),
                                    reads=[("mtmp",), ("ar", "acc", sl)], writes=[("ar", "acc", sl)])

                    def pe_part(sl=sl, q=q):
                        pst, pkey = bank()
                        P.op("pe", lambda e: e.matmul(pst[:, 0:2], Acc[sl][:, :], onescol[:, :], start=True, stop=True),
                             reads=[("ar", "acc", sl), ("onescol",)], writes=[pkey])
                        P.op("dve", lambda e: e.tensor_tensor(out=modv[:, l, q:q + 1], in0=pst[:, 0:1],
                                                             in1=cv_ap(("bmod", l), q, 1), op=ALU.add),
                             reads=[pkey, ("cvec",)], writes=[("modv", l)])
                    pend.append((q, pe_part))
                while pend:
                    pend.pop(0)[1]()
                for (di, mo, gname) in ((0, 8, "gmix"), (3, 32, "gffn")):
                    P.op("dve", lambda e, di=di, mo=mo, gname=gname: e.scalar_tensor_tensor(
                        out=dv[:, l, di, :], in0=modv[:, l, mo:mo + 8], scalar=1.0,
                        in1=cv_ap((gname, l), 0, 8), op0=ALU.add, op1=ALU.mult),
                        reads=[("modv", l), ("cvec",)], writes=[("dv", l, di)])
                for (di, mo) in ((1, 0), (2, 16), (4, 24), (5, 40)):
                    P.op("dve", lambda e, di=di, mo=mo: e.tensor_copy(out=dv[:, l, di, :], in_=modv[:, l, mo:mo + 8]),
                         reads=[("modv", l)], writes=[("dv", l, di)])

            wpump()
            mod_prologue(layers[0])

            def dvk(l):
                return [("dv", l, i_) for i_ in range(6)]

            def cols(s):
                return slice(s * N, (s + 1) * N)

            def stats_rstd(src_fn, src_keys, s, eps_t):
                pairs = []
                rk = []
                for m in range(KC):
                    sqt, sqk = sq_tile()
                    P.op("act", lambda e, m=m, sqt=sqt: e.activation(out=sqt[:, :], in_=src_fn(m, s), func=AF.Square),
                         reads=[src_keys(m, s)], writes=[sqk])
                    P.op("pe", lambda e, m=m, sqt=sqt: e.matmul(ps[0][:, :N], onesD[:, :], sqt[:, :], start=(m == 0), stop=(m == KC - 1)),
                         reads=[sqk, ("onesD",)], writes=[("ps", 0)])
                tt, tk = tmp_tile()
                P.op("act", lambda e, tt=tt: e.activation(out=tt[:, :], in_=ps[0][:, :N], func=AF.Ln, bias=eps_t[:, 0:1], scale=1.0),
                     reads=[("ps", 0), ("eps",)], writes=[tk])
                P.op("act", lambda e, tt=tt: e.activation(out=ps[2][:, :N], in_=tt[:, :], func=AF.Exp, scale=-0.5),
                     reads=[tk], writes=[("ps", 2)])

            def acc_bank(s):
                return s if s < 4 else 4

            def modulate(l, di_a, di_b, s, rb):
                for m in range(KC):
                    tt, tk = tmp_tile()
                    P.op("dve", lambda e, m=m, s=s, tt=tt: e.scalar_tensor_tensor(
                        out=tt[:, :], in0=xs[:, m, cols(s)], scalar=dv[:, l, di_a, m:m + 1], in1=ps[rb][:, :N],
                        op0=ALU.mult, op1=ALU.mult),
                        reads=[("x", m, s), ("ps", rb)] + dvk(l), writes=[tk])
                    if m % 2 == 0:
                        P.op("pool", lambda e, m=m, s=s, tt=tt: e.tensor_scalar(
                            out=hs[:, m, cols(s)], in0=tt[:, :], scalar1=1.0, scalar2=dv[:, l, di_b, m:m + 1],
                            op0=ALU.mult, op1=ALU.add),
                            reads=[tk] + dvk(l), writes=[("h", m, s)])
                    else:
                        P.op("act", lambda e, m=m, s=s, tt=tt: e.activation(
                            out=hs[:, m, cols(s)], in_=tt[:, :], func=AF.Identity, bias=dv[:, l, di_b, m:m + 1], scale=1.0),
                            reads=[tk] + dvk(l), writes=[("h", m, s)])

            def rstd_from_acc(s):
                rb = acc_bank(s)
                tt, tk = tmp_tile()
                P.op("act", lambda e, tt=tt: e.activation(out=tt[:, :], in_=ps[rb][:, :N], func=AF.Ln, bias=eps_r[:, 0:1], scale=1.0),
                     reads=[("ps", rb), ("eps",)], writes=[tk])
                P.op("act", lambda e, tt=tt: e.activation(out=ps[rb][:, :N], in_=tt[:, :], func=AF.Exp, scale=-0.5),
                     reads=[tk], writes=[("ps", rb)])
                return rb

            def norm_mod(l, di_a, di_b, have_stats=False, after_sub=None):
                P.mark("norm")
                for s in range(NS):
                    if have_stats:
                        rb = rstd_from_acc(s)
                    else:
                        stats_rstd(lambda m, s_: xs[:, m, cols(s_)], lambda m, s_: ("x", m, s_), s, eps_r)
                        rb = 2
                    modulate(l, di_a, di_b, s, rb)
                    if s == NS - 1:
                        excl4[0] = False
                    if after_sub is not None and s >= 1:
                        after_sub(s - 1)
                excl4[0] = False
                if after_sub is not None:
                    after_sub(NS - 1)

            def edge_mask(eng, buf_fn, key_fn, blk):
                eo = blk * EB
                P.op(eng, lambda e: getattr(e, "tensor_tensor")(out=buf_fn(0, EDGE), in0=buf_fn(0, EDGE),
                                                               in1=edge[:, eo:eo + EDGE], op=ALU.mult),
                     reads=[("edge",), key_fn(0)], writes=[key_fn(0)])
                P.op(eng, lambda e: getattr(e, "tensor_tensor")(out=buf_fn(TB - EDGE, TB), in0=buf_fn(TB - EDGE, TB),
                                                               in1=edge[:, eo + EDGE:eo + 2 * EDGE], op=ALU.mult),
                     reads=[("edge",), key_fn(NS - 1)], writes=[key_fn(NS - 1)])

            def proj_residual(l, wsrc_fn, kcn, rhs_fn, rhs_keys, gi, bias_fn=None, acc_stats=False):
                pend = []
                if acc_stats:
                    excl4[0] = True
                for m in range(KC):
                    wt, wk = get_tile(wsrc_fn(m), kcn)
                    for s in range(NS):
                        if len(pend) > 2:
                            pend.pop(0)()
                        pst, pkey = bank()
                        pairs = [(wt[:, k, :], rhs_fn(k, s)) for k in range(kcn)]
                        reads = [wk] + [rhs_keys(k, s) for k in range(kcn)]
                        mm_group(pst[:, :N], pkey, pairs, reads)
                        P.op("dve", lambda e, m=m, s=s, pst=pst: e.scalar_tensor_tensor(
                            out=xs[:, m, cols(s)], in0=pst[:, :N], scalar=dv[:, l, gi, m:m + 1], in1=xs[:, m, cols(s)],
                            op0=ALU.mult, op1=ALU.add),
                            reads=[pkey, ("x", m, s)] + dvk(l), writes=[("x", m, s)])
                        if acc_stats:
                            sqt, sqk = sq_tile()
                            P.op("act", lambda e, m=m, s=s, sqt=sqt: e.activation(out=sqt[:, :], in_=xs[:, m, cols(s)], func=AF.Square),
                                 reads=[("x", m, s)], writes=[sqk])
                            ab_ = acc_bank(s)
                            pend.append(lambda m=m, sqt=sqt, sqk=sqk, ab_=ab_: P.op(
                                "pe", lambda e: e.matmul(ps[ab_][:, :N], onesD[:, :], sqt[:, :], start=(m == 0), stop=(m == KC - 1)),
                                reads=[sqk, ("onesD",)], writes=[("ps", ab_)]))
                    wrelease(1)
                while pend:
                    pend.pop(0)()

            def mixer_even(l, blk, norm_call):
                P.mark("mixer_even")
                i = l // 2
                P.new_epoch()
                AR.reset()
                ybuf = AR.take(KC * TB, BF16).rearrange("p (a b) -> p a b", a=KC)
                cvb = [AR.take(TB + 2, BF16) for _ in range(2)]
                pbuf = AR.take(TB + 2 * PP, F32)
                T0 = AR.take(N + 2 * PP + 16, F32)
                T1 = AR.take(N + 2 * PP + 16, F32)
                pooled = [AR.take(N, BF16) for _ in range(2)]
                wplt = AR.take(4 * 128, BF16).rearrange("p (a b) -> p a b", a=4)
                dg3s = [sqr[2][:, 0:384].rearrange("p (a b) -> p a b", a=3), sqr[3][:, 0:384].rearrange("p (a b) -> p a b", a=3)]
                dg3k = [("sq", 2), ("sq", 3)]
                win = ab_w_in[i]
                eo_m = blk * EB
                for c_ in range(2):
                    P.op("dve", lambda e, c_=c_: e.memset(cvb[c_][:, 0:1], 0.0), writes=[("ar", "cvpadl", c_)])
                    P.op("dve", lambda e, c_=c_: e.memset(cvb[c_][:, TB + 1:TB + 2], 0.0), writes=[("ar", "cvpadr", c_)])
                P.op("dve", lambda e: e.memset(pbuf[:, 0:PP], 0.0), writes=[("ar", "ppadl")])
                P.op("dve", lambda e: e.memset(pbuf[:, PP + TB:PP + TB + PP], 0.0), writes=[("ar", "ppadr")])
                si = stg_i[0] % 2
                stg_i[0] += 1
                P.dma("sp", f"sg{si}", lambda e, si=si: e.dma_start(
                    out=stg[si][:, 0:4, :], in_=ab_w_pool[i].rearrange("g c e -> c g e")), writes=[("stg", si)])
                P.op("pool", lambda e, si=si: e.tensor_copy(out=wplt[:, :, :], in_=stg[si][:, 0:4, :]),
                     reads=[("stg", si)], writes=[("ar", "wpl")])

                def P_step(g, s, wp, wpk):
                    pP, pPk = bank()
                    mm_group(pP[:, :N], pPk, [(wp[:, k, :], hs[:, k, cols(s)]) for k in range(KC)],
                             [wpk] + [("h", k, s) for k in range(KC)])
                    P.op("act", lambda e, pP=pP, s=s: e.activation(out=pbuf[:, PP + s * N:PP + (s + 1) * N], in_=pP[:, :N], func=AF.Copy),
                         reads=[pPk], writes=[("ar", "p", s)])
                    if s == 0:
                        P.op("dve", lambda e: e.tensor_tensor(out=pbuf[:, PP:PP + EDGE], in0=pbuf[:, PP:PP + EDGE],
                                                             in1=edge[:, eo_m:eo_m + EDGE], op=ALU.mult),
                             reads=[("edge",), ("ar", "p", 0)], writes=[("ar", "p", 0)])
                    if s == NS - 1:
                        P.op("dve", lambda e: e.tensor_tensor(out=pbuf[:, PP + TB - EDGE:PP + TB], in0=pbuf[:, PP + TB - EDGE:PP + TB],
                                                             in1=edge[:, eo_m + EDGE:eo_m + 2 * EDGE], op=ALU.mult),
                             reads=[("edge",), ("ar", "p", NS - 1)], writes=[("ar", "p", NS - 1)])

                def chain_step(g, s):
                    w_ = 2 << g
                    base = s * N
                    L = N + 2 * PP
                    rkeys = [("ar", "p", s_) for s_ in (s - 1, s, s + 1) if 0 <= s_ < NS] + [("ar", "ppadl"), ("ar", "ppadr")]
                    src, srcoff, srckey = pbuf, base, rkeys
                    Ts = [T0, T1]
                    d = 1
                    for step in range(g + 1):
                        dst = Ts[step % 2]
                        P.op("dve", lambda e, dst=dst, src=src, srcoff=srcoff, d=d, L=L: e.tensor_tensor(
                            out=dst[:, d:L], in0=src[:, srcoff + d:srcoff + L], in1=src[:, srcoff:srcoff + L - d], op=ALU.add),
                            reads=srckey, writes=[("ar", "T", step % 2)])
                        src, srcoff, srckey = dst, 0, [("ar", "T", step % 2)]
                        d *= 2
                    o_ = PP + w_ // 2 - 1
                    pl = pooled[s % 2]
                    plk = ("ar", "pooled", s % 2)
                    P.op("dve", lambda e, src=src, o_=o_, pl=pl, s=s, w_=w_: e.scalar_tensor_tensor(
                        out=pl[:, :], in0=src[:, o_:o_ + N], scalar=1.0 / w_, in1=pbuf[:, PP + s * N:PP + (s + 1) * N],
                        op0=ALU.mult, op1=ALU.subtract), reads=srckey + [("ar", "p", s)], writes=[plk])
                    eo = blk * EB + 2 * EDGE + g * 2 * PEDGE
                    if s == 0 or s == NS - 1:
                        c0 = POFF if s == 0 else N - POFF - PEDGE
                        et = edge[:, eo:eo + PEDGE] if s == 0 else edge[:, eo + PEDGE:eo + 2 * PEDGE]
                        tt, tk = tmp_tile()
                        P.op("dve", lambda e, src=src, o_=o_, c0=c0, et=et, tt=tt: e.tensor_tensor(
                            out=tt[:, 0:PEDGE], in0=src[:, o_ + c0:o_ + c0 + PEDGE], in1=et, op=ALU.mult),
                            reads=srckey + [("edge",)], writes=[tk])
                        P.op("dve", lambda e, c0=c0, tt=tt, pl=pl, s=s: e.tensor_tensor(
                            out=pl[:, c0:c0 + PEDGE], in0=tt[:, 0:PEDGE],
                            in1=pbuf[:, PP + s * N + c0:PP + s * N + c0 + PEDGE], op=ALU.subtract),
                            reads=[tk, ("ar", "p", s)], writes=[plk])

                def poolmm_step(g, s):
                    pl = pooled[s % 2]
                    plk = ("ar", "pooled", s % 2)
                    pO, pOk = bank()
                    mm_group(pO[:, :N], pOk, [(wplt[:, g, :], pl[:, :])], [("ar", "wpl"), plk])
                    P.op("act", lambda e, pO=pO, g=g, s=s: e.activation(
                        out=ybuf[:, 4 + g, cols(s)], in_=pO[:, :N], func=AF.Identity, scale=cv_ap(("pscale", i), g, 1)),
                        reads=[pOk, ("cvec",)], writes=[("ar", "y", 4 + g, s)])

                def CV_step(a, s, wc, wck, wv, wvk):
                    cb = cvb[a % 2]
                    pC, pCk = bank()
                    pV, pVk = bank()
                    hk = [("h", k, s) for k in range(KC)]
                    mm_group(pC[:, :N], pCk, [(wc[:, k, :], hs[:, k, cols(s)]) for k in range(KC)], [wck] + hk)
                    mm_group(pV[:, :N], pVk, [(wv[:, k, :], hs[:, k, cols(s)]) for k in range(KC)], [wvk] + hk)
                    tt, tk = tmp_tile()
                    P.op("act", lambda e, tt=tt, pC=pC: e.activation(out=tt[:, :], in_=pC[:, :N], func=AF.Copy),
                         reads=[pCk], writes=[tk])
                    P.op("dve", lambda e, tt=tt, pV=pV, cb=cb, s=s: e.tensor_tensor(
                        out=cb[:, 1 + s * N:1 + (s + 1) * N], in0=tt[:, :], in1=pV[:, :N], op=ALU.mult),
                        reads=[tk, pVk], writes=[("ar", "cv", a % 2, s)])
                    if s == 0:
                        P.op("dve", lambda e, cb=cb: e.tensor_tensor(out=cb[:, 1:1 + EDGE], in0=cb[:, 1:1 + EDGE],
                                                                    in1=edge[:, eo_m:eo_m + EDGE], op=ALU.mult),
                             reads=[("edge",), ("ar", "cv", a % 2, 0)], writes=[("ar", "cv", a % 2, 0)])
                    if s == NS - 1:
                        P.op("dve", lambda e, cb=cb: e.tensor_tensor(out=cb[:, 1 + TB - EDGE:1 + TB], in0=cb[:, 1 + TB - EDGE:1 + TB],
                                                                    in1=edge[:, eo_m + EDGE:eo_m + 2 * EDGE], op=ALU.mult),
                             reads=[("edge",), ("ar", "cv", a % 2, NS - 1)], writes=[("ar", "cv", a % 2, NS - 1)])

                def diag_build(a):
                    for k in range(3):
                        P.op("dve", lambda e, k=k, a=a: e.tensor_scalar(
                            out=dg3s[a % 2][:, k, :], in0=identb[:, :], scalar1=cv_ap(("conv", i), k * 4 + a, 1), scalar2=None,
                            op0=ALU.mult), reads=[("identb",), ("cvec",)], writes=[dg3k[a % 2]])

                def conv_step(a, s, wb, wbk):
                    cb = cvb[a % 2]
                    dg3 = dg3s[a % 2]
                    pY, pYk = bank()
                    pB, pBk = bank()
                    rk = [("ar", "cv", a % 2, s_) for s_ in (s - 1, s, s + 1) if 0 <= s_ < NS]
                    rk += [("ar", "cvpadl", a % 2), ("ar", "cvpadr", a % 2), dg3k[a % 2]]
                    mm_group(pY[:, :N], pYk, [(dg3[:, k, :], cb[:, s * N + k:s * N + k + N]) for k in range(3)], rk)
                    mm_group(pB[:, :N], pBk, [(wb[:, k, :], hs[:, k, cols(s)]) for k in range(KC)],
                             [wbk] + [("h", k, s) for k in range(KC)])
                    tt, tk = tmp_tile()
                    P.op("act", lambda e, tt=tt, pY=pY: e.activation(out=tt[:, :], in_=pY[:, :N], func=AF.Copy),
                         reads=[pYk], writes=[tk])
                    P.op("dve", lambda e, tt=tt, pB=pB, a=a, s=s: e.tensor_tensor(
                        out=ybuf[:, a, cols(s)], in0=tt[:, :], in1=pB[:, :N], op=ALU.mult),
                        reads=[tk, pBk], writes=[("ar", "y", a, s)])

                def wsl(c0):
                    return win[:, c0:c0 + 128]

                (wc, wck), (wv, wvk), (wp, wpk) = get_tiles([(wsl(512), KC), (wsl(1024), KC), (wsl(1536), KC)])
                def first_pass(s):
                    CV_step(0, s, wc, wck, wv, wvk)
                    P_step(0, s, wp, wpk)
                norm_call(first_pass)
                wrelease(3)
                diag_build(0)
                for a in range(4):
                    specs = []
                    if a >= 1:
                        specs += [(wsl(512 + a * 128), KC), (wsl(1024 + a * 128), KC)]
                    specs += [(wsl((a) * 128), KC)]
                    if a + 1 < 4:
                        specs += [(wsl(1536 + (a + 1) * 128), KC)]
                    tl = get_tiles(specs)
                    if a >= 1:
                        (wc, wck), (wv, wvk) = tl[0], tl[1]
                        tl = tl[2:]
                    (wb, wbk) = tl[0]
                    wpn = tl[1] if a + 1 < 4 else None
                    for s in range(NS):
                        if a >= 1:
                            CV_step(a, s, wc, wck, wv, wvk)
                        chain_step(a, s)
                        if s >= 1:
                            poolmm_step(a, s - 1)
                        if a == 0:
                            if s >= 1:
                                conv_step(0, s - 1, wb, wbk)
                    poolmm_step(a, NS - 1)
                    if a == 0:
                        conv_step(0, NS - 1, wb, wbk)
                        wrelease(1)
                    else:
                        wrelease(2)
                        diag_build(a)
                        for s in range(NS):
                            conv_step(a, s, wb, wbk)
                        wrelease(1)
                    if wpn is not None:
                        for s in range(NS):
                            P_step(a + 1, s, wpn[0], wpn[1])
                proj_residual(l, lambda m: ab_w_out[i][:, m * 128:(m + 1) * 128], KC,
                              lambda k, s: ybuf[:, k, cols(s)], lambda k, s: ("ar", "y", k, s), 2, acc_stats=True)

            def mixer_odd(l, blk, norm_call):
                P.mark("mixer_odd")
                i = l // 2
                P.new_epoch()
                AR.reset()
                zb = AR.take(KC * (TB + 2 * ZP), BF16).rearrange("p (a b) -> p a b", a=KC)
                dgs = [AR.take(31 * 128, BF16).rearrange("p (a b) -> p a b", a=31) for _ in range(2)]
                for a in range(KC):
                    P.op("dve", lambda e, a=a: e.memset(zb[:, a, 0:ZP], 0.0), writes=[("ar", "zpadl", a)])
                    P.op("dve", lambda e, a=a: e.memset(zb[:, a, ZP + TB:ZP + TB + ZP], 0.0), writes=[("ar", "zpadr", a)])
                w1 = cf_w_pw1[i]
                for a in range(KC):
                    (wa, wak), (wg, wgk) = get_tiles([(w1[:, a * 128:(a + 1) * 128], KC),
                                                      (w1[:, D + a * 128:D + (a + 1) * 128], KC)])

                    def glu_step(s, a=a, wa=wa, wak=wak, wg=wg, wgk=wgk):
                        pA, pAk = bank()
                        pG, pGk = bank()
                        hk = [("h", k, s) for k in range(KC)]
                        mm_group(pA[:, :N], pAk, [(wa[:, k, :], hs[:, k, cols(s)]) for k in range(KC)], [wak] + hk)
                        mm_group(pG[:, :N], pGk, [(wg[:, k, :], hs[:, k, cols(s)]) for k in range(KC)], [wgk] + hk)
                        tt, tk = tmp_tile()
                        P.op("act", lambda e, tt=tt, pG=pG, a=a: e.activation(
                            out=tt[:, :], in_=pG[:, :N], func=AF.Sigmoid, bias=cv_ap(("bpw1", i), 8 + a, 1), scale=1.0),
                            reads=[pGk, ("cvec",)], writes=[tk])
                        P.op("dve", lambda e, tt=tt, pA=pA, a=a, s=s: e.scalar_tensor_tensor(
                            out=zb[:, a, ZP + s * N:ZP + (s + 1) * N], in0=pA[:, :N], scalar=cv_ap(("bpw1", i), a, 1),
                            in1=tt[:, :], op0=ALU.add, op1=ALU.mult),
                            reads=[pAk, tk, ("cvec",)], writes=[("ar", "z", a, s)])
                    if a == 0:
                        norm_call(glu_step)
                    else:
                        for s in range(NS):
                            glu_step(s)
                    wrelease(2)
                    edge_mask("dve", lambda c0, c1, a=a: zb[:, a, ZP + c0:ZP + c1], lambda s_, a=a: ("ar", "z", a, s_), blk)
                for a in range(KC):
                    dgt = dgs[a % 2]
                    for k in range(31):
                        P.op("dve", lambda e, k=k, a=a, dgt=dgt: e.tensor_scalar(
                            out=dgt[:, k, :], in0=identb[:, :], scalar1=cv_ap(("wdw", i), k * 8 + a, 1), scalar2=None,
                            op0=ALU.mult), reads=[("identb",), ("cvec",)], writes=[("ar", "dg", a % 2, k)])
                    for s in range(NS):
                        pZ, pZk = bank()
                        rk = [("ar", "z", a, s_) for s_ in (s - 1, s, s + 1) if 0 <= s_ < NS]
                        rk += [("ar", "zpadl", a), ("ar", "zpadr", a)] + [("ar", "dg", a % 2, k) for k in range(31)]
                        mm_group(pZ[:, :N], pZk, [(dgt[:, k, :], zb[:, a, s * N + k:s * N + k + N]) for k in range(31)], rk)
                        P.op("act", lambda e, pZ=pZ, a=a, s=s: e.activation(
                            out=hs[:, a, cols(s)], in_=pZ[:, :N], func=AF.Identity, bias=cv_ap(("bdw", i), a, 1), scale=1.0),
                            reads=[pZk, ("cvec",)], writes=[("h", a, s)])
                c1 = AR.take(N, F32)
                c2 = AR.take(N, F32)
                c3 = AR.take(N, F32)

                def rn(s):
                    return (2, 3) if s % 2 == 0 else (4, 5)

                def ln_stats(s):
                    for a in range(KC):
                        sqt, sqk = sq_tile()
                        P.op("dve", lambda e, a=a, s=s, sqt=sqt: e.tensor_tensor(
                            out=sqt[:, :], in0=hs[:, a, cols(s)], in1=hs[:, a, cols(s)], op=ALU.mult),
                            reads=[("h", a, s)], writes=[sqk])
                        P.op("pe", lambda e, a=a, s=s: e.matmul(ps[0][:, :N], onesD[:, :], hs[:, a, cols(s)], start=(a == 0), stop=(a == KC - 1)),
                             reads=[("h", a, s), ("onesD",)], writes=[("ps", 0)])
                        P.op("pe", lambda e, a=a, sqt=sqt: e.matmul(ps[1][:, :N], onesD[:, :], sqt[:, :], start=(a == 0), stop=(a == KC - 1)),
                             reads=[sqk, ("onesD",)], writes=[("ps", 1)])

                def chainA(s):
                    P.op("act", lambda e: e.activation(out=c1[:, :], in_=ps[0][:, :N], func=AF.Square),
                         reads=[("ps", 0)], writes=[("ar", "c1")])
                    P.op("act", lambda e: e.activation(out=c3[:, :], in_=ps[0][:, :N], func=AF.Copy),
                         reads=[("ps", 0)], writes=[("ar", "c3")])

                def chainB(s):
                    P.op("dve", lambda e: e.tensor_tensor(out=c2[:, :], in0=ps[1][:, :N], in1=c1[:, :], op=ALU.subtract),
                         reads=[("ps", 1), ("ar", "c1")], writes=[("ar", "c2")])

                def chainC(s):
                    rb, nb_ = rn(s)
                    P.op("act", lambda e: e.activation(out=c1[:, :], in_=c2[:, :], func=AF.Ln, bias=eps_l[:, 0:1], scale=1.0),
                         reads=[("ar", "c2"), ("eps",)], writes=[("ar", "c1")])
                    P.op("act", lambda e, rb=rb: e.activation(out=ps[rb][:, :N], in_=c1[:, :], func=AF.Exp, scale=-0.5),
                         reads=[("ar", "c1")], writes=[("ps", rb)])
                    P.op("act", lambda e: e.activation(out=c2[:, :], in_=c1[:, :], func=AF.Exp, scale=-0.5),
                         reads=[("ar", "c1")], writes=[("ar", "c2")])

                def chainD(s):
                    rb, nb_ = rn(s)
                    P.op("dve", lambda e, nb_=nb_: e.scalar_tensor_tensor(
                        out=ps[nb_][:, :N], in0=c3[:, :], scalar=-1.0, in1=c2[:, :], op0=ALU.mult, op1=ALU.mult),
                        reads=[("ar", "c3"), ("ar", "c2")], writes=[("ps", nb_)])

                def ln_norm_half(s, half):
                    rb, nb_ = rn(s)
                    for a in range(half * 4, half * 4 + 4):
                        ta, tak = tmp_tile()
                        P.op("dve", lambda e, a=a, s=s, ta=ta, rb=rb: e.tensor_tensor(
                            out=ta[:, :], in0=hs[:, a, cols(s)], in1=ps[rb][:, :N], op=ALU.mult),
                            reads=[("h", a, s), ("ps", rb)], writes=[tak])
                        P.op("dve", lambda e, ta=ta, nb_=nb_: e.tensor_tensor(out=ta[:, :], in0=ta[:, :], in1=ps[nb_][:, :N], op=ALU.add),
                             reads=[tak, ("ps", nb_)], writes=[tak])
                        P.op("act", lambda e, a=a, s=s, ta=ta: e.activation(
                            out=zb[:, a, ZP + s * N:ZP + (s + 1) * N], in_=ta[:, :], func=AF.Silu,
                            bias=cv_ap(("lnb", i), a, 1), scale=cv_ap(("lng", i), a, 1)),
                            reads=[tak, ("cvec",)], writes=[("ar", "z", a, s)])

                ln_stats(0)
                chainA(0)
                chainB(0)
                chainC(0)
                chainD(0)
                for s in range(NS):
                    nxt = s + 1 < NS
                    if nxt:
                        ln_stats(s + 1)
                        chainA(s + 1)
                    ln_norm_half(s, 0)
                    if nxt:
                        chainB(s + 1)
                        chainC(s + 1)
                    ln_norm_half(s, 1)
                    if nxt:
                        chainD(s + 1)
                P.op("dve", lambda e: e.tensor_tensor(out=gbt[:, :], in0=dv[:, l, 2, :], in1=cv_ap(("bpw2", i), 0, 8), op=ALU.mult),
                     reads=dvk(l) + [("cvec",)], writes=[("gbt",)])
                for m in range(KC):
                    P.op("pool", lambda e, m=m: e.tensor_scalar(
                        out=xs[:, m, :], in0=xs[:, m, :], scalar1=1.0, scalar2=gbt[:, m:m + 1], op0=ALU.mult, op1=ALU.add),
                        reads=[("gbt",)] + [("x", m, s_) for s_ in range(NS)], writes=[("x", m, s_) for s_ in range(NS)])
                proj_residual(l, lambda m: cf_w_pw2[i][:, m * 128:(m + 1) * 128], KC,
                              lambda k, s: zb[:, k, ZP + s * N:ZP + (s + 1) * N], lambda k, s: ("ar", "z", k, s), 2,
                              acc_stats=True)

            def ffn(l, norm_call):
                P.mark("ffn")
                P.new_epoch()
                AR.reset()
                ab = AR.take(JG * TB, BF16).rearrange("p (a b) -> p a b", a=JG)
                for hf in range(NG):
                    for jj in range(JG):
                        j = hf * JG + jj
                        (wgt, wgk), (wut, wuk) = get_tiles([(ffn_w_gate[l][:, j * 128:(j + 1) * 128], KC),
                                                            (ffn_w_up[l][:, j * 128:(j + 1) * 128], KC)])

                        def gu_step(s, jj=jj, wgt=wgt, wgk=wgk, wut=wut, wuk=wuk):
                            pG, pGk = bank()
                            pU, pUk = bank()
                            hk = [("h", k, s) for k in range(KC)]
                            mm_group(pG[:, :N], pGk, [(wgt[:, k, :], hs[:, k, cols(s)]) for k in range(KC)], [wgk] + hk)
                            mm_group(pU[:, :N], pUk, [(wut[:, k, :], hs[:, k, cols(s)]) for k in range(KC)], [wuk] + hk)
                            tt, tk = tmp_tile()
                            P.op("act", lambda e, tt=tt, pG=pG: e.activation(out=tt[:, :], in_=pG[:, :N], func=AF.Silu),
                                 reads=[pGk], writes=[tk])
                            P.op("dve", lambda e, tt=tt, pU=pU, jj=jj, s=s: e.tensor_tensor(
                                out=ab[:, jj, cols(s)], in0=tt[:, :], in1=pU[:, :N], op=ALU.mult),
                                reads=[tk, pUk], writes=[("ar", "a", jj, s)])
                        if j == 0:
                            norm_call(gu_step)
                        else:
                            for s in range(NS):
                                gu_step(s)
                        wrelease(2)
                    proj_residual(l, lambda m, hf=hf: ffn_w_down[l][hf * JG * 128:(hf + 1) * JG * 128, m * 128:(m + 1) * 128], JG,
                                  lambda k, s: ab[:, k, cols(s)], lambda k, s: ("ar", "a", k, s), 5,
                                  acc_stats=(hf == NG - 1))

            def final_out(blk):
                P.mark("final")
                P.new_epoch()
                AR.reset()
                ot = [AR.take(N, F32) for _ in range(4)]
                oi = 0
                for s in range(NS):
                    if len(layers) > 0 and stop is None:
                        rb = rstd_from_acc(s)
                    else:
                        stats_rstd(lambda m, s_: xs[:, m, cols(s_)], lambda m, s_: ("x", m, s_), s, eps_r)
                        rb = 2
                    lo = max(s * N, HALO)
                    hi = min((s + 1) * N, HALO + TOK)
                    for m in range(KC):
                        o = oi % 4
                        oi += 1
                        P.op("dve", lambda e, m=m, s=s, o=o, rb=rb: e.scalar_tensor_tensor(
                            out=ot[o][:, :], in0=xs[:, m, cols(s)], scalar=cv_ap("gfin", m, 1), in1=ps[rb][:, :N],
                            op0=ALU.mult, op1=ALU.mult),
                            reads=[("x", m, s), ("ps", rb), ("cvec",)], writes=[("ar", "ot", o)])
                        P.dma("sp", f"o{o}", lambda e, m=m, s=s, o=o, lo=lo, hi=hi: e.dma_start(
                            out=yT[blk, m * 128:(m + 1) * 128, lo - HALO:hi - HALO], in_=ot[o][:, lo - s * N:hi - s * N]),
                            reads=[("ar", "ot", o)])

            def raw_out(blk):
                P.new_epoch()
                for m in range(KC):
                    P.dma("sp", f"o{m % 4}", lambda e, m=m: e.dma_start(
                        out=yT[blk, m * 128:(m + 1) * 128, :], in_=xs[:, m, HALO:HALO + TOK]),
                        reads=[("x", m, s) for s in range(NS)])

            for blk in range(nblocks):
                for m in range(KC):
                    P.dma("sp", f"x{m}", lambda e, m=m, blk=blk: e.dma_start(out=xs[:, m, :], in_=xT[blk, m * 128:(m + 1) * 128, :]),
                          writes=[("x", m, s) for s in range(NS)])
                for li, l in enumerate(layers):
                    if stop == "load":
                        break
                    if blk == 0:
                        drain_bg()
                        if li + 1 < len(layers):
                            enqueue_mod(layers[li + 1])
                    nm1 = lambda cb, l=l, li=li: norm_mod(l, 0, 1, have_stats=(li > 0), after_sub=cb)
                    if stop == "norm":
                        nm1(None)
                    if stop == "norm":
                        for m in range(KC):
                            tt, tk = tmp_tile()
                            P.op("dve", lambda e, m=m, tt=tt: e.tensor_copy(out=tt[:, :], in_=hs[:, m, 0:N]), reads=[("h", m, 0)], writes=[tk])
                            P.dma("sp", "o0", lambda e, m=m, tt=tt: e.dma_start(out=dbg_d[:, m * N:(m + 1) * N], in_=tt[:, :]), reads=[tk])
                        P.dma("sp", "o1", lambda e: e.dma_start(out=dbg_d[:, 8 * N:8 * N + 192], in_=dv[:, :, :, :].rearrange("p a b c -> p (a b c)")), reads=dvk(l))
                        P.dma("sp", "o1", lambda e: e.dma_start(out=dbg_d[:, 8 * N + 192:8 * N + 384], in_=modv[:, :, :].rearrange("p a b -> p (a b)")), reads=[("modv", l)])
                        break
                    if l % 2 == 0:
                        mixer_even(l, blk, nm1)
                    else:
                        mixer_odd(l, blk, nm1)
                    if stop == "mixer":
                        break
                    ffn(l, lambda cb, l=l: norm_mod(l, 3, 4, have_stats=True, after_sub=cb))
                if final_norm:
                    final_out(blk)
                else:
                    raw_out(blk)
                excl4[0] = False
            return wstate["specs"]

        Pd = Prog()
        plan = run(Pd, None)
        P = Prog()
        plan2 = run(P, plan)
        assert len(plan2) == len(plan)

        final_waits = {s: c for s, c in P.dmacnt.items() if s.startswith("o")}

        engmap = {"pe": "tensor", "act": "scalar", "dve": "vector", "pool": "gpsimd", "sp": "sync"}
        with nc.Block() as block:
            def make(engname):
                def body(e):
                    for item in P.q[engname]:
                        if item[0] == "wait":
                            e.wait_ge(sems[item[1]], item[2])
                        elif item[0] == "op":
                            ins = item[1](e)
                            ins.then_inc(sems[engname], 1)
                        else:
                            ins = item[1](e)
                            ins.then_inc(sems[item[2]], 16)
                    if engname == "sp":
                        for s_, c_ in final_waits.items():
                            e.wait_ge(sems[s_], c_)
                        for en in ("pe", "act", "dve"):
                            if P.cnt[en] > 0:
                                e.wait_ge(sems[en], P.cnt[en])
                return body
            block.tensor(make("pe"))
            block.scalar(make("act"))
            block.vector(make("dve"))
            block.gpsimd(make("pool"))
            block.sync(make("sp"))
        stats = {e: len(P.q[e]) for e in Prog.ENG}
        PHASE_MARKS[:] = P.marks
    return nc, stats


def _fm(v):
    v = np.asarray(v, np.float32)
    lead = v.shape[:-1]
    n = v.shape[-1] // 128
    v = v.reshape(lead + (n, 128))
    return np.moveaxis(v, -1, 0)


def _build_cvec(inp, b):
    cv = np.zeros((128, NV), np.float32)

    def put(name, arr):
        arr = np.asarray(arr, np.float32).reshape(128, -1)
        cv[:, CV[name]:CV[name] + arr.shape[1]] = arr
    for l in range(DEPTH):
        put(("gmix", l), _fm(inp["norm_mix_g"][l]))
        put(("gffn", l), _fm(inp["norm_ffn_g"][l]))
        put(("bmod", l), _fm(inp["b_mod"][l]))
    for i in range(2):
        put(("conv", i), _fm(inp["ab_conv"][i]))
        put(("pscale", i), _fm(inp["ab_pool_scale"][i]))
        put(("bpw1", i), _fm(inp["cf_b_pw1"][i]))
        put(("wdw", i), _fm(inp["cf_w_dw"][i]))
        put(("bdw", i), _fm(inp["cf_b_dw"][i]))
        put(("lng", i), _fm(inp["cf_ln_g"][i]))
        put(("lnb", i), _fm(inp["cf_ln_b"][i]))
        put(("bpw2", i), _fm(inp["cf_b_pw2"][i]))
    put("gfin", _fm(inp["final_norm_g"]))
    put("c", _fm(inp["c"][b]))
    return cv


def _build_edge(half):
    e = np.zeros((NE,), np.float32)
    for blk in range(NB):
        s0 = half * (S // 2) + blk * TOK - HALO
        pos = s0 + np.arange(TB)
        valid = ((pos >= 0) & (pos < S)).astype(np.float32)
        o = blk * EB
        e[o:o + EDGE] = valid[:EDGE]
        e[o + EDGE:o + 2 * EDGE] = valid[TB - EDGE:]
        for g in range(4):
            w = 2 << g
            left = w // 2
            right = w - 1 - left
            cnt = (np.minimum(pos + right, S - 1) - np.maximum(pos - left, 0) + 1).astype(np.float32)
            inv = np.where((pos >= 0) & (pos < S), 1.0 / np.maximum(cnt, 1.0), 1.0 / w).astype(np.float32)
            oo = o + 2 * EDGE + g * 2 * PEDGE
            e[oo:oo + PEDGE] = inv[POFF:POFF + PEDGE]
            e[oo + PEDGE:oo + 2 * PEDGE] = inv[TB - POFF - PEDGE:TB - POFF]
    return np.ascontiguousarray(np.broadcast_to(e[None, :], (128, NE)))


_CACHE = {}


def _get_nc(layers, final_norm):
    key = (tuple(layers), final_norm)
    if key not in _CACHE:
        _CACHE[key] = build_program(layers, final_norm)
    return _CACHE[key][0]


def _make_in_maps(inp, x_full):
    f32 = lambda a: np.ascontiguousarray(np.asarray(a, np.float32))
    shared = {k: f32(inp[k]) for k in ("w_mod", "ab_w_in", "ab_w_pool", "ab_w_out", "cf_w_pw1", "cf_w_pw2",
                                       "ffn_w_gate", "ffn_w_up", "ffn_w_down")}
    ident = np.eye(128, dtype=np.float32)
    in_maps = []
    for cid in range(NCORES):
        b, half = cid // 2, cid % 2
        xt = np.zeros((NB, D, TB), np.float32)
        for blk in range(NB):
            s0 = half * (S // 2) + blk * TOK - HALO
            lo, hi = max(s0, 0), min(s0 + TB, S)
            xt[blk, :, lo - s0:hi - s0] = x_full[b, lo:hi, :].T
        m = dict(shared)
        m.update({"xT": xt, "cvec": _build_cvec(inp, b), "edge": _build_edge(half), "ident": ident})
        in_maps.append(m)
    return in_maps


def _gather(res):
    out = np.empty((BATCH, S, D), np.float32)
    for cid in range(NCORES):
        b, half = cid // 2, cid % 2
        y = res.results[cid]["yT"]
        for blk in range(NB):
            t0 = half * (S // 2) + blk * TOK
            out[b, t0:t0 + TOK, :] = y[blk].T
    return out


def kernel(**inputs):
    inp = {k: np.asarray(v) for k, v in inputs.items()}
    x = np.asarray(inp["x"], np.float32)
    nc = _get_nc((0, 1, 2, 3), True)
    in_maps = _make_in_maps(inp, x)
    res = run_bass_kernel_spmd(nc, in_maps, core_ids=list(range(NCORES)))
    return _gather(res)
```

```python
import numpy as np
import concourse.bass as bass
import concourse.mybir as mybir
from concourse.bass_utils import run_bass_kernel_spmd

F32 = mybir.dt.float32
BF16 = mybir.dt.bfloat16
AF = mybir.ActivationFunctionType
ALU = mybir.AluOpType

D = 1024
S = 8192
BATCH = 4
DEPTH = 4
DFF = 2816
NCORES = 8
NB = 2
TOK = 2048
HALO = 46
TB = TOK + 2 * HALO
NS = 5
N = TB // NS
KC = 8
NJ = DFF // 128
NG = 2
JG = NJ // NG
RING = 5
EDGE = 48
PEDGE = 32
POFF = 32
ZP = 15
PP = 16

CV = {}
_off = 0


def _cv(name, n):
    global _off
    CV[name] = _off
    _off += n


for _l in range(DEPTH):
    _cv(("gmix", _l), 8)
    _cv(("gffn", _l), 8)
    _cv(("bmod", _l), 48)
for _i in range(2):
    _cv(("conv", _i), 12)
    _cv(("pscale", _i), 4)
for _i in range(2):
    _cv(("bpw1", _i), 16)
    _cv(("wdw", _i), 31 * 8)
    _cv(("bdw", _i), 8)
    _cv(("lng", _i), 8)
    _cv(("lnb", _i), 8)
    _cv(("bpw2", _i), 8)
_cv("gfin", 8)
_cv("c", 8)
NV = _off
EB = 2 * EDGE + 4 * 2 * PEDGE
NE = NB * EB


PHASE_MARKS = []


class Prog:
    ENG = ("pe", "act", "dve", "pool", "sp")

    def __init__(self):
        self.q = {e: [] for e in self.ENG}
        self.cnt = {e: 0 for e in self.ENG}
        self.waited = {e: {} for e in self.ENG}
        self.res = {}
        self.dmacnt = {}
        self.epoch = {}
        self.arena_prefixes = set()
        self.nmm = 0
        self.marks = []

    def _res(self, k):
        r = self.res.get(k)
        if r is None:
            if k[0] in self.arena_prefixes:
                r = [dict(self.epoch), {}]
            else:
                r = [{}, {}]
            self.res[k] = r
        return r

    def new_epoch(self):
        ep = {e: c for e, c in self.cnt.items() if c > 0}
        for s, c in self.dmacnt.items():
            if c > 0 and not s.startswith("sg") and not s.startswith("x"):
                ep[s] = c
        self.epoch = ep
        for k in [k for k in self.res if k[0] in self.arena_prefixes]:
            del self.res[k]

    def _deps(self, reads, writes):
        deps = {}

        def add(d):
            for s, c in d.items():
                if deps.get(s, 0) < c:
                    deps[s] = c
        for k in reads:
            add(self._res(k)[0])
        for k in writes:
            r = self._res(k)
            add(r[0])
            add(r[1])
        return deps

    def _emit_waits(self, eng, deps):
        w = self.waited[eng]
        for s, c in deps.items():
            if w.get(s, 0) < c:
                self.q[eng].append(("wait", s, c))
                w[s] = c

    def _commit(self, tok, reads, writes):
        s, c = tok
        for k in reads:
            r = self._res(k)
            if r[1].get(s, 0) < c:
                r[1][s] = c
        for k in writes:
            r = self._res(k)
            r[0] = {s: c}
            r[1] = {}

    def mark(self, label):
        self.marks.append((label, self.nmm))

    def op(self, eng, fn, reads=(), writes=(), nmm=1):
        if eng == "pe":
            self.nmm += nmm
        deps = self._deps(reads, writes)
        if eng == "pe":
            deps.pop("pe", None)
        self._emit_waits(eng, deps)
        self.cnt[eng] += 1
        tok = (eng, self.cnt[eng])
        self.q[eng].append(("op", fn, eng))
        self._commit(tok, reads, writes)
        return tok

    def dma(self, eng, sem, fn, reads=(), writes=()):
        deps = self._deps(reads, writes)
        prev = self.dmacnt.get(sem, 0)
        if prev > 0 and deps.get(sem, 0) < prev:
            deps[sem] = prev
        self._emit_waits(eng, deps)
        self.dmacnt[sem] = self.dmacnt.get(sem, 0) + 16
        tok = (sem, self.dmacnt[sem])
        self.q[eng].append(("dma", fn, sem))
        self._commit(tok, reads, writes)
        return tok


def build_program(layers=(0, 1, 2, 3), final_norm=True, nblocks=NB, stop=None):
    nc = bass.Bass("TRN2", target_bir_lowering=False)
    dr = {}

    def din(name, shape):
        dr[name] = nc.dram_tensor(name, list(shape), F32, kind="ExternalInput").ap()
        return dr[name]

    xT = din("xT", [NB, D, TB])
    cvec_d = din("cvec", [128, NV])
    edge_d = din("edge", [128, NE])
    ident_d = din("ident", [128, 128])
    w_mod = din("w_mod", [DEPTH, D, 6 * D])
    ab_w_in = din("ab_w_in", [2, D, 2048])
    ab_w_pool = din("ab_w_pool", [2, 4, 128, 128])
    ab_w_out = din("ab_w_out", [2, D, D])
    cf_w_pw1 = din("cf_w_pw1", [2, D, 2 * D])
    cf_w_pw2 = din("cf_w_pw2", [2, D, D])
    ffn_w_gate = din("ffn_w_gate", [DEPTH, D, DFF])
    ffn_w_up = din("ffn_w_up", [DEPTH, D, DFF])
    ffn_w_down = din("ffn_w_down", [DEPTH, DFF, D])
    yT = nc.dram_tensor("yT", [NB, D, TOK], F32, kind="ExternalOutput").ap()
    dbg_d = nc.dram_tensor("dbg", [128, 8 * N + 192 + 192], F32, kind="ExternalOutput").ap() if stop else None

    import contextlib
    st = contextlib.ExitStack()
    with st:
        def sb(name, shape, dt):
            return st.enter_context(nc.sbuf_tensor(name, list(shape), dt))

        xs = sb("xs", [128, KC, TB], F32)
        hs = sb("hs", [128, KC, TB], BF16)
        ARENA_E = 29056
        arena = sb("arena", [128, ARENA_E], BF16)
        wring = [sb(f"wr{r}", [128, JG, 128], BF16) for r in range(RING)]
        sqr = [sb(f"sq{r}", [128, N], BF16) for r in range(4)]
        cvec = sb("cvecs", [128, NV], F32)
        edge = sb("edges", [128, NE], F32)
        identf = sb("identf", [128, 128], F32)
        identb = sb("identb", [128, 128], BF16)
        onesD = sb("onesD", [128, 128], BF16)
        stg = [sb(f"stg{r}", [128, KC, 128], F32) for r in range(2)]
        gbt = sb("gbt", [128, 8], F32)
        modv = sb("modv", [128, DEPTH, 48], F32)
        dv = sb("dv", [128, DEPTH, 6, 8], F32)
        cact = sb("cact", [128, KC, 2], F32)
        eps_r = sb("eps_r", [128, 1], F32)
        eps_l = sb("eps_l", [128, 1], F32)
        tmpf = [sb(f"tmpf{r}", [128, N], F32) for r in range(4)]
        wmst = [sb(f"wmst{r}", [128, KC, 128], F32) for r in range(2)]
        mtmp = sb("mtmp", [128, 128], F32)
        macc = [sb(f"macc{r}", [128, 128], F32) for r in range(2)]
        onescol = sb("onescol", [128, 2], F32)
        ps = [st.enter_context(nc.psum_tensor(f"ps{b}", [128, 512], F32)) for b in range(8)]

        sems = {}
        for e in Prog.ENG:
            sems[e] = st.enter_context(nc.semaphore(f"s_{e}"))
        dma_sem_names = ["sg0", "sg1"] + [f"x{m}" for m in range(KC)] + \
            ["o0", "o1", "o2", "o3", "wm0", "wm1", "cst"]
        for s_ in dma_sem_names:
            sems[s_] = st.enter_context(nc.semaphore(f"d_{s_}"))

        class Arena:
            def __init__(self):
                self.off = 0

            def reset(self):
                self.off = 0

            def take(self, nelem, dt):
                if dt == F32:
                    self.off = (self.off + 1) // 2 * 2
                    v = arena[:, self.off:self.off + 2 * nelem].bitcast(F32)
                    self.off += 2 * nelem
                else:
                    v = arena[:, self.off:self.off + nelem]
                    self.off += nelem
                assert self.off <= ARENA_E, (self.off, ARENA_E)
                return v
        AR = Arena()

        def run(P, wplan):
            P.arena_prefixes = {"ar"}
            wstate = {"next_load": 0, "next_use": 0, "specs": [], "consumed": 0}
            tf_i = [0]
            sq_i = [0]
            ring_i = [0]

            def tmp_tile():
                i = tf_i[0] % 4
                tf_i[0] += 1
                return tmpf[i], ("tmpf", i)

            def sq_tile():
                i = sq_i[0] % 4
                sq_i[0] += 1
                return sqr[i], ("sq", i)

            excl4 = [False]

            def bank():
                b = 4 + ring_i[0] % 4
                ring_i[0] += 1
                if excl4[0] and b == 4:
                    b = 4 + ring_i[0] % 4
                    ring_i[0] += 1
                return ps[b], ("ps", b)

            def cv_ap(name, j=0, n=1):
                o = CV[name] + j
                return cvec[:, o:o + n]

            import collections as _c
            bgq = _c.deque()
            BG_RATE = 2

            def drain_bg(n=None):
                k = 0
                while bgq and (n is None or k < n):
                    bgq.popleft()()
                    k += 1

            stg_i = [0]

            def issue_load(j):
                src, kcn = wplan[j]
                slot = j % RING
                k0 = 0
                while k0 < kcn:
                    n_ = min(KC, kcn - k0)
                    si = stg_i[0] % 2
                    stg_i[0] += 1
                    P.dma("sp", f"sg{si}",
                          lambda e, si=si, n_=n_, k0=k0, src=src: e.dma_start(out=stg[si][:, 0:n_, :], in_=src[:, k0:k0 + n_, :]),
                          writes=[("stg", si)])
                    P.op("pool", lambda e, si=si, n_=n_, k0=k0, slot=slot: e.tensor_copy(
                        out=wring[slot][:, k0:k0 + n_, :], in_=stg[si][:, 0:n_, :]),
                        reads=[("stg", si)], writes=[("w", slot)])
                    k0 += n_

            def get_tiles(specs):
                assert len(specs) <= RING
                j0 = wstate["next_use"]
                outs = []
                for (src2d, kcn) in specs:
                    j = wstate["next_use"]
                    wstate["next_use"] += 1
                    src = src2d.rearrange("(kc p) n -> p kc n", p=128)
                    wstate["specs"].append((src, kcn))
                    slot = j % RING
                    outs.append((wring[slot], ("w", slot)))
                wstate["consumed"] = j0
                wpump()
                return outs

            def wpump():
                if wplan is not None:
                    lim = min(len(wplan), wstate["consumed"] + RING)
                    while wstate["next_load"] < lim:
                        issue_load(wstate["next_load"])
                        wstate["next_load"] += 1

            def wrelease(k):
                wstate["consumed"] += k
                assert wstate["consumed"] <= wstate["next_use"]
                wpump()

            def get_tile(src2d, kcn):
                return get_tiles([(src2d, kcn)])[0]

            def mm_group(pst, pkey, pairs, reads, extra_first=None):
                def fn(e, pairs=pairs, pst=pst):
                    ins = None
                    n_ = len(pairs)
                    for i_, (l_, r_) in enumerate(pairs):
                        ins = e.matmul(pst, l_, r_, start=(i_ == 0), stop=(i_ == n_ - 1))
                    return ins
                tok = P.op("pe", fn, reads=reads, writes=[pkey], nmm=len(pairs))
                drain_bg(BG_RATE)
                return tok

            P.dma("sp", "cst", lambda e: e.dma_start(out=cvec[:, :], in_=cvec_d[:, :]), writes=[("cvec",)])
            P.dma("sp", "cst", lambda e: e.dma_start(out=edge[:, :], in_=edge_d[:, :]), writes=[("edge",)])
            P.dma("sp", "cst", lambda e: e.dma_start(out=identf[:, :], in_=ident_d[:, :]), writes=[("identf",)])
            for k in (("cvec",), ("edge",), ("identf",)):
                P.res[k][0] = {"cst": P.dmacnt["cst"]}
            P.op("dve", lambda e: e.tensor_copy(out=identb[:, :], in_=identf[:, :]), reads=[("identf",)], writes=[("identb",)])
            P.op("dve", lambda e: e.memset(onesD[:, :], 1.0 / D), writes=[("onesD",)])
            P.op("dve", lambda e: e.memset(eps_r[:, :], 1e-6), writes=[("eps",)])
            P.op("dve", lambda e: e.memset(eps_l[:, :], 1e-5), writes=[("eps",)])
            for d_ in range(2):
                P.op("act", lambda e, d_=d_: e.activation(out=cact[:, :, d_], in_=cv_ap("c", 0, 8), func=AF.Silu),
                     reads=[("cvec",)], writes=[("cact", d_)])

            P.op("dve", lambda e: e.memset(onescol[:, :], 1.0), writes=[("onescol",)])
            mod_i = [0]

            def enqueue_mod(l):
                pending_pe = []
                for q in range(48):
                    gi_ = mod_i[0]
                    mod_i[0] += 1
                    slot = gi_ % 2
                    src = w_mod[l, :, q * 128:(q + 1) * 128].rearrange("(kc p) n -> p kc n", p=128)
                    bgq.append(lambda slot=slot, src=src: P.dma(
                        "sp", f"wm{slot}", lambda e: e.dma_start(out=wmst[slot][:, :, :], in_=src), writes=[("wmst", slot)]))
                    for kc in range(KC):
                        if kc == 0:
                            bgq.append(lambda slot=slot: P.op("pool", lambda e: e.tensor_scalar(
                                out=macc[slot][:, :], in0=wmst[slot][:, 0, :], scalar1=cact[:, 0, 0:1], scalar2=1.0,
                                op0=ALU.mult, op1=ALU.mult), reads=[("wmst", slot), ("cact", 0)], writes=[("macc", slot)]))
                        else:
                            bgq.append(lambda slot=slot, kc=kc: P.op("pool", lambda e: e.tensor_scalar(
                                out=mtmp[:, :], in0=wmst[slot][:, kc, :], scalar1=cact[:, kc, 0:1], scalar2=1.0,
                                op0=ALU.mult, op1=ALU.mult), reads=[("wmst", slot), ("cact", 0)], writes=[("mtmp",)]))
                            bgq.append(lambda slot=slot: P.op("pool", lambda e: e.tensor_tensor(
                                out=macc[slot][:, :], in0=macc[slot][:, :], in1=mtmp[:, :], op=ALU.add),
                                reads=[("mtmp",), ("macc", slot)], writes=[("macc", slot)]))

                    def pe_thunk(slot=slot, q=q, l=l):
                        pst, pkey = bank()
                        P.op("pe", lambda e: e.matmul(pst[:, 0:2], macc[slot][:, :], onescol[:, :], start=True, stop=True),
                             reads=[("macc", slot), ("onescol",)], writes=[pkey])
                        P.op("dve", lambda e: e.tensor_tensor(out=modv[:, l, q:q + 1], in0=pst[:, 0:1],
                                                             in1=cv_ap(("bmod", l), q, 1), op=ALU.add),
                             reads=[pkey, ("cvec",)], writes=[("modv", l)])
                    pending_pe.append(pe_thunk)
                    if len(pending_pe) > 1:
                        bgq.append(pending_pe.pop(0))
                bgq.append(pending_pe.pop(0))

                def fin(l=l):
                    for (di, mo, gname) in ((0, 8, "gmix"), (3, 32, "gffn")):
                        P.op("dve", lambda e, di=di, mo=mo, gname=gname: e.scalar_tensor_tensor(
                            out=dv[:, l, di, :], in0=modv[:, l, mo:mo + 8], scalar=1.0,
                            in1=cv_ap((gname, l), 0, 8), op0=ALU.add, op1=ALU.mult),
                            reads=[("modv", l), ("cvec",)], writes=[("dv", l, di)])
                    for (di, mo) in ((1, 0), (2, 16), (4, 24), (5, 40)):
                        P.op("dve", lambda e, di=di, mo=mo: e.tensor_copy(out=dv[:, l, di, :], in_=modv[:, l, mo:mo + 8]),
                             reads=[("modv", l)], writes=[("dv", l, di)])
                bgq.append(fin)

            def mod_prologue(l):
                P.mark("prologue")
                P.new_epoch()
                AR.reset()
                NSL = 6
                Wst = [AR.take(KC * 128, F32).rearrange("p (a b) -> p a b", a=KC) for _ in range(NSL)]
                Acc = [AR.take(128, F32) for _ in range(NSL)]
                pat = ("dve", "act", "pool", "dve", "act", "dve", "act")
                Tq = [AR.take(128, F32) for _ in range(2)]
                tq_i = 0
                pend = []
                for q in range(48):
                    while pend and pend[0][0] <= q - 3:
                        pend.pop(0)[1]()
                    sl = q % NSL
                    eng = pat[q % len(pat)]
                    src = w_mod[l, :, q * 128:(q + 1) * 128].rearrange("(kc p) n -> p kc n", p=128)
                    P.dma("sp", ("wm0", "wm1", "o0", "o1", "o2", "o3")[sl], lambda e, sl=sl, src=src: e.dma_start(out=Wst[sl][:, :, :], in_=src),
                          writes=[("ar", "wst", sl)])
                    if eng == "act":
                        pst, pkey = bank()
                        for kc in range(KC):
                            tq = Tq[tq_i % 2]
                            tqk = ("ar", "tq", tq_i % 2)
                            tq_i += 1
                            P.op("act", lambda e, sl=sl, kc=kc, tq=tq: e.activation(
                                out=tq[:, :], in_=Wst[sl][:, kc, :], func=AF.Identity, scale=cact[:, kc, 0:1]),
                                reads=[("ar", "wst", sl), ("cact", 0)], writes=[tqk])
                            P.op("pe", lambda e, kc=kc, tq=tq, pst=pst: e.matmul(
                                pst[:, 0:2], tq[:, :], onescol[:, :], start=(kc == 0), stop=(kc == KC - 1)),
                                reads=[tqk, ("onescol",)], writes=[pkey])
                        P.op("dve", lambda e, pst=pst, q=q: e.tensor_tensor(out=modv[:, l, q:q + 1], in0=pst[:, 0:1],
                                                                          in1=cv_ap(("bmod", l), q, 1), op=ALU.add),
                             reads=[pkey, ("cvec",), ("ar", "wst", sl)], writes=[("modv", l)])
                        continue
                    for kc in range(KC):
                        if eng == "dve":
                            if kc == 0:
                                P.op("dve", lambda e, sl=sl: e.tensor_scalar(
                                    out=Acc[sl][:, :], in0=Wst[sl][:, 0, :], scalar1=cact[:, 0, 0:1], scalar2=None, op0=ALU.mult),
                                    reads=[("ar", "wst", sl), ("cact", 0)], writes=[("ar", "acc", sl)])
                            else:
                                P.op("dve", lambda e, sl=sl, kc=kc: e.scalar_tensor_tensor(
                                    out=Acc[sl][:, :], in0=Wst[sl][:, kc, :], scalar=cact[:, kc, 0:1], in1=Acc[sl][:, :],
                                    op0=ALU.mult, op1=ALU.add),
                                    reads=[("ar", "wst", sl), ("cact", 0), ("ar", "acc", sl)], writes=[("ar", "acc", sl)])
                        else:
                            if kc == 0:
                                P.op("pool", lambda e, sl=sl: e.tensor_scalar(
                                    out=Acc[sl][:, :], in0=Wst[sl][:, 0, :], scalar1=cact[:, 0, 0:1], scalar2=1.0,
                                    op0=ALU.mult, op1=ALU.mult),
                                    reads=[("ar", "wst", sl), ("cact", 0)], writes=[("ar", "acc", sl)])
                            else:
                                P.op("pool", lambda e, sl=sl, kc=kc: e.tensor_scalar(
                                    out=mtmp[:, :], in0=Wst[sl][:, kc, :], scalar1=cact[:, kc, 0:1], scalar2=1.0,
                                    op0=ALU.mult, op1=ALU.mult), reads=[("ar", "wst", sl), ("cact", 0)], writes=[("mtmp",)])
                                P.op("pool", lambda e, sl=sl: e.tensor_tensor(
                                    out=Acc[sl][:, :], in0=Acc[sl][:, :], in1=mtmp[:, :], op=ALU.add),
                                    reads=[("mtmp",), ("ar", "acc", sl)], writes=[("ar", "acc", sl)])

                    def pe_part(sl=sl, q=q):
                        pst, pkey = bank()
                        P.op("pe", lambda e: e.matmul(pst[:, 0:2], Acc[sl][:, :], onescol[:, :], start=True, stop=True),
                             reads=[("ar", "acc", sl), ("onescol",)], writes=[pkey])
                        P.op("dve", lambda e: e.tensor_tensor(out=modv[:, l, q:q + 1], in0=pst[:, 0:1],
                                                             in1=cv_ap(("bmod", l), q, 1), op=ALU.add),
                             reads=[pkey, ("cvec",)], writes=[("modv", l)])
                    pend.append((q, pe_part))
                while pend:
                    pend.pop(0)[1]()
                for (di, mo, gname) in ((0, 8, "gmix"), (3, 32, "gffn")):
                    P.op("dve", lambda e, di=di, mo=mo, gname=gname: e.scalar_tensor_tensor(
                        out=dv[:, l, di, :], in0=modv[:, l, mo:mo + 8], scalar=1.0,
                        in1=cv_ap((gname, l), 0, 8), op0=ALU.add, op1=ALU.mult),
                        reads=[("modv", l), ("cvec",)], writes=[("dv", l, di)])
                for (di, mo) in ((1, 0), (2, 16), (4, 24), (5, 40)):
                    P.op("dve", lambda e, di=di, mo=mo: e.tensor_copy(out=dv[:, l, di, :], in_=modv[:, l, mo:mo + 8]),
                         reads=[("modv", l)], writes=[("dv", l, di)])

            wpump()
            mod_prologue(layers[0])

            def dvk(l):
                return [("dv", l, i_) for i_ in range(6)]

            def cols(s):
                return slice(s * N, (s + 1) * N)

            def stats_rstd(src_fn, src_keys, s, eps_t):
                pairs = []
                rk = []
                for m in range(KC):
                    sqt, sqk = sq_tile()
                    P.op("act", lambda e, m=m, sqt=sqt: e.activation(out=sqt[:, :], in_=src_fn(m, s), func=AF.Square),
                         reads=[src_keys(m, s)], writes=[sqk])
                    P.op("pe", lambda e, m=m, sqt=sqt: e.matmul(ps[0][:, :N], onesD[:, :], sqt[:, :], start=(m == 0), stop=(m == KC - 1)),
                         reads=[sqk, ("onesD",)], writes=[("ps", 0)])
                tt, tk = tmp_tile()
                P.op("act", lambda e, tt=tt: e.activation(out=tt[:, :], in_=ps[0][:, :N], func=AF.Ln, bias=eps_t[:, 0:1], scale=1.0),
                     reads=[("ps", 0), ("eps",)], writes=[tk])
                P.op("act", lambda e, tt=tt: e.activation(out=ps[2][:, :N], in_=tt[:, :], func=AF.Exp, scale=-0.5),
                     reads=[tk], writes=[("ps", 2)])

            def acc_bank(s):
                return s if s < 4 else 4

            def modulate(l, di_a, di_b, s, rb):
                for m in range(KC):
                    tt, tk = tmp_tile()
                    P.op("dve", lambda e, m=m, s=s, tt=tt: e.scalar_tensor_tensor(
                        out=tt[:, :], in0=xs[:, m, cols(s)], scalar=dv[:, l, di_a, m:m + 1], in1=ps[rb][:, :N],
                        op0=ALU.mult, op1=ALU.mult),
                        reads=[("x", m, s), ("ps", rb)] + dvk(l), writes=[tk])
                    if m % 2 == 0:
                        P.op("pool", lambda e, m=m, s=s, tt=tt: e.tensor_scalar(
                            out=hs[:, m, cols(s)], in0=tt[:, :], scalar1=1.0, scalar2=dv[:, l, di_b, m:m + 1],
                            op0=ALU.mult, op1=ALU.add),
                            reads=[tk] + dvk(l), writes=[("h", m, s)])
                    else:
                        P.op("act", lambda e, m=m, s=s, tt=tt: e.activation(
                            out=hs[:, m, cols(s)], in_=tt[:, :], func=AF.Identity, bias=dv[:, l, di_b, m:m + 1], scale=1.0),
                            reads=[tk] + dvk(l), writes=[("h", m, s)])

            def rstd_from_acc(s):
                rb = acc_bank(s)
                tt, tk = tmp_tile()
                P.op("act", lambda e, tt=tt: e.activation(out=tt[:, :], in_=ps[rb][:, :N], func=AF.Ln, bias=eps_r[:, 0:1], scale=1.0),
                     reads=[("ps", rb), ("eps",)], writes=[tk])
                P.op("act", lambda e, tt=tt: e.activation(out=ps[rb][:, :N], in_=tt[:, :], func=AF.Exp, scale=-0.5),
                     reads=[tk], writes=[("ps", rb)])
                return rb

            def norm_mod(l, di_a, di_b, have_stats=False, after_sub=None):
                P.mark("norm")
                for s in range(NS):
                    if have_stats:
                        rb = rstd_from_acc(s)
                    else:
                        stats_rstd(lambda m, s_: xs[:, m, cols(s_)], lambda m, s_: ("x", m, s_), s, eps_r)
                        rb = 2
                    modulate(l, di_a, di_b, s, rb)
                    if s == NS - 1:
                        excl4[0] = False
                    if after_sub is not None and s >= 1:
                        after_sub(s - 1)
                excl4[0] = False
                if after_sub is not None:
                    after_sub(NS - 1)

            def edge_mask(eng, buf_fn, key_fn, blk):
                eo = blk * EB
                P.op(eng, lambda e: getattr(e, "tensor_tensor")(out=buf_fn(0, EDGE), in0=buf_fn(0, EDGE),
                                                               in1=edge[:, eo:eo + EDGE], op=ALU.mult),
                     reads=[("edge",), key_fn(0)], writes=[key_fn(0)])
                P.op(eng, lambda e: getattr(e, "tensor_tensor")(out=buf_fn(TB - EDGE, TB), in0=buf_fn(TB - EDGE, TB),
                                                               in1=edge[:, eo + EDGE:eo + 2 * EDGE], op=ALU.mult),
                     reads=[("edge",), key_fn(NS - 1)], writes=[key_fn(NS - 1)])

            def proj_residual(l, wsrc_fn, kcn, rhs_fn, rhs_keys, gi, bias_fn=None, acc_stats=False):
                pend = []
                if acc_stats:
                    excl4[0] = True
                for m in range(KC):
                    wt, wk = get_tile(wsrc_fn(m), kcn)
                    for s in range(NS):
                        if len(pend) > 2:
                            pend.pop(0)()
                        pst, pkey = bank()
                        pairs = [(wt[:, k, :], rhs_fn(k, s)) for k in range(kcn)]
                        reads = [wk] + [rhs_keys(k, s) for k in range(kcn)]
                        mm_group(pst[:, :N], pkey, pairs, reads)
                        P.op("dve", lambda e, m=m, s=s, pst=pst: e.scalar_tensor_tensor(
                            out=xs[:, m, cols(s)], in0=pst[:, :N], scalar=dv[:, l, gi, m:m + 1], in1=xs[:, m, cols(s)],
                            op0=ALU.mult, op1=ALU.add),
                            reads=[pkey, ("x", m, s)] + dvk(l), writes=[("x", m, s)])
                        if acc_stats:
                            sqt, sqk = sq_tile()
                            P.op("act", lambda e, m=m, s=s, sqt=sqt: e.activation(out=sqt[:, :], in_=xs[:, m, cols(s)], func=AF.Square),
                                 reads=[("x", m, s)], writes=[sqk])
                            ab_ = acc_bank(s)
                            pend.append(lambda m=m, sqt=sqt, sqk=sqk, ab_=ab_: P.op(
                                "pe", lambda e: e.matmul(ps[ab_][:, :N], onesD[:, :], sqt[:, :], start=(m == 0), stop=(m == KC - 1)),
                                reads=[sqk, ("onesD",)], writes=[("ps", ab_)]))
                    wrelease(1)
                while pend:
                    pend.pop(0)()

            def mixer_even(l, blk, norm_call):
                P.mark("mixer_even")
                i = l // 2
                P.new_epoch()
                AR.reset()
                ybuf = AR.take(KC * TB, BF16).rearrange("p (a b) -> p a b", a=KC)
                cvb = [AR.take(TB + 2, BF16) for _ in range(2)]
                pbuf = AR.take(TB + 2 * PP, F32)
                T0 = AR.take(N + 2 * PP + 16, F32)
                T1 = AR.take(N + 2 * PP + 16, F32)
                pooled = [AR.take(N, BF16) for _ in range(2)]
                wplt = AR.take(4 * 128, BF16).rearrange("p (a b) -> p a b", a=4)
                dg3s = [sqr[2][:, 0:384].rearrange("p (a b) -> p a b", a=3), sqr[3][:, 0:384].rearrange("p (a b) -> p a b", a=3)]
                dg3k = [("sq", 2), ("sq", 3)]
                win = ab_w_in[i]
                eo_m = blk * EB
                for c_ in range(2):
                    P.op("dve", lambda e, c_=c_: e.memset(cvb[c_][:, 0:1], 0.0), writes=[("ar", "cvpadl", c_)])
                    P.op("dve", lambda e, c_=c_: e.memset(cvb[c_][:, TB + 1:TB + 2], 0.0), writes=[("ar", "cvpadr", c_)])
                P.op("dve", lambda e: e.memset(pbuf[:, 0:PP], 0.0), writes=[("ar", "ppadl")])
                P.op("dve", lambda e: e.memset(pbuf[:, PP + TB:PP + TB + PP], 0.0), writes=[("ar", "ppadr")])
                si = stg_i[0] % 2
                stg_i[0] += 1
                P.dma("sp", f"sg{si}", lambda e, si=si: e.dma_start(
                    out=stg[si][:, 0:4, :], in_=ab_w_pool[i].rearrange("g c e -> c g e")), writes=[("stg", si)])
                P.op("pool", lambda e, si=si: e.tensor_copy(out=wplt[:, :, :], in_=stg[si][:, 0:4, :]),
                     reads=[("stg", si)], writes=[("ar", "wpl")])

                def P_step(g, s, wp, wpk):
                    pP, pPk = bank()
                    mm_group(pP[:, :N], pPk, [(wp[:, k, :], hs[:, k, cols(s)]) for k in range(KC)],
                             [wpk] + [("h", k, s) for k in range(KC)])
                    P.op("act", lambda e, pP=pP, s=s: e.activation(out=pbuf[:, PP + s * N:PP + (s + 1) * N], in_=pP[:, :N], func=AF.Copy),
                         reads=[pPk], writes=[("ar", "p", s)])
                    if s == 0:
                        P.op("dve", lambda e: e.tensor_tensor(out=pbuf[:, PP:PP + EDGE], in0=pbuf[:, PP:PP + EDGE],
                                                             in1=edge[:, eo_m:eo_m + EDGE], op=ALU.mult),
                             reads=[("edge",), ("ar", "p", 0)], writes=[("ar", "p", 0)])
                    if s == NS - 1:
                        P.op("dve", lambda e: e.tensor_tensor(out=pbuf[:, PP + TB - EDGE:PP + TB], in0=pbuf[:, PP + TB - EDGE:PP + TB],
                                                             in1=edge[:, eo_m + EDGE:eo_m + 2 * EDGE], op=ALU.mult),
                             reads=[("edge",), ("ar", "p", NS - 1)], writes=[("ar", "p", NS - 1)])

                def chain_step(g, s):
                    w_ = 2 << g
                    base = s * N
                    L = N + 2 * PP
                    rkeys = [("ar", "p", s_) for s_ in (s - 1, s, s + 1) if 0 <= s_ < NS] + [("ar", "ppadl"), ("ar", "ppadr")]
                    src, srcoff, srckey = pbuf, base, rkeys
                    Ts = [T0, T1]
                    d = 1
                    for step in range(g + 1):
                        dst = Ts[step % 2]
                        P.op("dve", lambda e, dst=dst, src=src, srcoff=srcoff, d=d, L=L: e.tensor_tensor(
                            out=dst[:, d:L], in0=src[:, srcoff + d:srcoff + L], in1=src[:, srcoff:srcoff + L - d], op=ALU.add),
                            reads=srckey, writes=[("ar", "T", step % 2)])
                        src, srcoff, srckey = dst, 0, [("ar", "T", step % 2)]
                        d *= 2
                    o_ = PP + w_ // 2 - 1
                    pl = pooled[s % 2]
                    plk = ("ar", "pooled", s % 2)
                    P.op("dve", lambda e, src=src, o_=o_, pl=pl, s=s, w_=w_: e.scalar_tensor_tensor(
                        out=pl[:, :], in0=src[:, o_:o_ + N], scalar=1.0 / w_, in1=pbuf[:, PP + s * N:PP + (s + 1) * N],
                        op0=ALU.mult, op1=ALU.subtract), reads=srckey + [("ar", "p", s)], writes=[plk])
                    eo = blk * EB + 2 * EDGE + g * 2 * PEDGE
                    if s == 0 or s == NS - 1:
                        c0 = POFF if s == 0 else N - POFF - PEDGE
                        et = edge[:, eo:eo + PEDGE] if s == 0 else edge[:, eo + PEDGE:eo + 2 * PEDGE]
                        tt, tk = tmp_tile()
                        P.op("dve", lambda e, src=src, o_=o_, c0=c0, et=et, tt=tt: e.tensor_tensor(
                            out=tt[:, 0:PEDGE], in0=src[:, o_ + c0:o_ + c0 + PEDGE], in1=et, op=ALU.mult),
                            reads=srckey + [("edge",)], writes=[tk])
                        P.op("dve", lambda e, c0=c0, tt=tt, pl=pl, s=s: e.tensor_tensor(
                            out=pl[:, c0:c0 + PEDGE], in0=tt[:, 0:PEDGE],
                            in1=pbuf[:, PP + s * N + c0:PP + s * N + c0 + PEDGE], op=ALU.subtract),
                            reads=[tk, ("ar", "p", s)], writes=[plk])

                def poolmm_step(g, s):
                    pl = pooled[s % 2]
                    plk = ("ar", "pooled", s % 2)
                    pO, pOk = bank()
                    mm_group(pO[:, :N], pOk, [(wplt[:, g, :], pl[:, :])], [("ar", "wpl"), plk])
                    P.op("act", lambda e, pO=pO, g=g, s=s: e.activation(
                        out=ybuf[:, 4 + g, cols(s)], in_=pO[:, :N], func=AF.Identity, scale=cv_ap(("pscale", i), g, 1)),
                        reads=[pOk, ("cvec",)], writes=[("ar", "y", 4 + g, s)])

                def CV_step(a, s, wc, wck, wv, wvk):
                    cb = cvb[a % 2]
                    pC, pCk = bank()
                    pV, pVk = bank()
                    hk = [("h", k, s) for k in range(KC)]
                    mm_group(pC[:, :N], pCk, [(wc[:, k, :], hs[:, k, cols(s)]) for k in range(KC)], [wck] + hk)
                    mm_group(pV[:, :N], pVk, [(wv[:, k, :], hs[:, k, cols(s)]) for k in range(KC)], [wvk] + hk)
                    tt, tk = tmp_tile()
                    P.op("act", lambda e, tt=tt, pC=pC: e.activation(out=tt[:, :], in_=pC[:, :N], func=AF.Copy),
                         reads=[pCk], writes=[tk])
                    P.op("dve", lambda e, tt=tt, pV=pV, cb=cb, s=s: e.tensor_tensor(
                        out=cb[:, 1 + s * N:1 + (s + 1) * N], in0=tt[:, :], in1=pV[:, :N], op=ALU.mult),
                        reads=[tk, pVk], writes=[("ar", "cv", a % 2, s)])
                    if s == 0:
                        P.op("dve", lambda e, cb=cb: e.tensor_tensor(out=cb[:, 1:1 + EDGE], in0=cb[:, 1:1 + EDGE],
                                                                    in1=edge[:, eo_m:eo_m + EDGE], op=ALU.mult),
                             reads=[("edge",), ("ar", "cv", a % 2, 0)], writes=[("ar", "cv", a % 2, 0)])
                    if s == NS - 1:
                        P.op("dve", lambda e, cb=cb: e.tensor_tensor(out=cb[:, 1 + TB - EDGE:1 + TB], in0=cb[:, 1 + TB - EDGE:1 + TB],
                                                                    in1=edge[:, eo_m + EDGE:eo_m + 2 * EDGE], op=ALU.mult),
                             reads=[("edge",), ("ar", "cv", a % 2, NS - 1)], writes=[("ar", "cv", a % 2, NS - 1)])

                def diag_build(a):
                    for k in range(3):
                        P.op("dve", lambda e, k=k, a=a: e.tensor_scalar(
                            out=dg3s[a % 2][:, k, :], in0=identb[:, :], scalar1=cv_ap(("conv", i), k * 4 + a, 1), scalar2=None,
                            op0=ALU.mult), reads=[("identb",), ("cvec",)], writes=[dg3k[a % 2]])

                def conv_step(a, s, wb, wbk):
                    cb = cvb[a % 2]
                    dg3 = dg3s[a % 2]
                    pY, pYk = bank()
                    pB, pBk = bank()
                    rk = [("ar", "cv", a % 2, s_) for s_ in (s - 1, s, s + 1) if 0 <= s_ < NS]
                    rk += [("ar", "cvpadl", a % 2), ("ar", "cvpadr", a % 2), dg3k[a % 2]]
                    mm_group(pY[:, :N], pYk, [(dg3[:, k, :], cb[:, s * N + k:s * N + k + N]) for k in range(3)], rk)
                    mm_group(pB[:, :N], pBk, [(wb[:, k, :], hs[:, k, cols(s)]) for k in range(KC)],
                             [wbk] + [("h", k, s) for k in range(KC)])
                    tt, tk = tmp_tile()
                    P.op("act", lambda e, tt=tt, pY=pY: e.activation(out=tt[:, :], in_=pY[:, :N], func=AF.Copy),
                         reads=[pYk], writes=[tk])
                    P.op("dve", lambda e, tt=tt, pB=pB, a=a, s=s: e.tensor_tensor(
                        out=ybuf[:, a, cols(s)], in0=tt[:, :], in1=pB[:, :N], op=ALU.mult),
                        reads=[tk, pBk], writes=[("ar", "y", a, s)])

                def wsl(c0):
                    return win[:, c0:c0 + 128]

                (wc, wck), (wv, wvk), (wp, wpk) = get_tiles([(wsl(512), KC), (wsl(1024), KC), (wsl(1536), KC)])
                def first_pass(s):
                    CV_step(0, s, wc, wck, wv, wvk)
                    P_step(0, s, wp, wpk)
                norm_call(first_pass)
                wrelease(3)
                diag_build(0)
                for a in range(4):
                    specs = []
                    if a >= 1:
                        specs += [(wsl(512 + a * 128), KC), (wsl(1024 + a * 128), KC)]
                    specs += [(wsl((a) * 128), KC)]
                    if a + 1 < 4:
                        specs += [(wsl(1536 + (a + 1) * 128), KC)]
                    tl = get_tiles(specs)
                    if a >= 1:
                        (wc, wck), (wv, wvk) = tl[0], tl[1]
                        tl = tl[2:]
                    (wb, wbk) = tl[0]
                    wpn = tl[1] if a + 1 < 4 else None
                    for s in range(NS):
                        if a >= 1:
                            CV_step(a, s, wc, wck, wv, wvk)
                        chain_step(a, s)
                        if s >= 1:
                            poolmm_step(a, s - 1)
                        if a == 0:
                            if s >= 1:
                                conv_step(0, s - 1, wb, wbk)
                    poolmm_step(a, NS - 1)
                    if a == 0:
                        conv_step(0, NS - 1, wb, wbk)
                        wrelease(1)
                    else:
                        wrelease(2)
                        diag_build(a)
                        for s in range(NS):
                            conv_step(a, s, wb, wbk)
                        wrelease(1)
                    if wpn is not None:
                        for s in range(NS):
                            P_step(a + 1, s, wpn[0], wpn[1])
                proj_residual(l, lambda m: ab_w_out[i][:, m * 128:(m + 1) * 128], KC,
                              lambda k, s: ybuf[:, k, cols(s)], lambda k, s: ("ar", "y", k, s), 2, acc_stats=True)

            def mixer_odd(l, blk, norm_call):
                P.mark("mixer_odd")
                i = l // 2
                P.new_epoch()
                AR.reset()
                zb = AR.take(KC * (TB + 2 * ZP), BF16).rearrange("p (a b) -> p a b", a=KC)
                dgs = [AR.take(31 * 128, BF16).rearrange("p (a b) -> p a b", a=31) for _ in range(2)]
                for a in range(KC):
                    P.op("dve", lambda e, a=a: e.memset(zb[:, a, 0:ZP], 0.0), writes=[("ar", "zpadl", a)])
                    P.op("dve", lambda e, a=a: e.memset(zb[:, a, ZP + TB:ZP + TB + ZP], 0.0), writes=[("ar", "zpadr", a)])
                w1 = cf_w_pw1[i]
                for a in range(KC):
                    (wa, wak), (wg, wgk) = get_tiles([(w1[:, a * 128:(a + 1) * 128], KC),
                                                      (w1[:, D + a * 128:D + (a + 1) * 128], KC)])

                    def glu_step(s, a=a, wa=wa, wak=wak, wg=wg, wgk=wgk):
                        pA, pAk = bank()
                        pG, pGk = bank()
                        hk = [("h", k, s) for k in range(KC)]
                        mm_group(pA[:, :N], pAk, [(wa[:, k, :], hs[:, k, cols(s)]) for k in range(KC)], [wak] + hk)
                        mm_group(pG[:, :N], pGk, [(wg[:, k, :], hs[:, k, cols(s)]) for k in range(KC)], [wgk] + hk)
                        tt, tk = tmp_tile()
                        P.op("act", lambda e, tt=tt, pG=pG, a=a: e.activation(
                            out=tt[:, :], in_=pG[:, :N], func=AF.Sigmoid, bias=cv_ap(("bpw1", i), 8 + a, 1), scale=1.0),
                            reads=[pGk, ("cvec",)], writes=[tk])
                        P.op("dve", lambda e, tt=tt, pA=pA, a=a, s=s: e.scalar_tensor_tensor(
                            out=zb[:, a, ZP + s * N:ZP + (s + 1) * N], in0=pA[:, :N], scalar=cv_ap(("bpw1", i), a, 1),
                            in1=tt[:, :], op0=ALU.add, op1=ALU.mult),
                            reads=[pAk, tk, ("cvec",)], writes=[("ar", "z", a, s)])
                    if a == 0:
                        norm_call(glu_step)
                    else:
                        for s in range(NS):
                            glu_step(s)
                    wrelease(2)
                    edge_mask("dve", lambda c0, c1, a=a: zb[:, a, ZP + c0:ZP + c1], lambda s_, a=a: ("ar", "z", a, s_), blk)
                for a in range(KC):
                    dgt = dgs[a % 2]
                    for k in range(31):
                        P.op("dve", lambda e, k=k, a=a, dgt=dgt: e.tensor_scalar(
                            out=dgt[:, k, :], in0=identb[:, :], scalar1=cv_ap(("wdw", i), k * 8 + a, 1), scalar2=None,
                            op0=ALU.mult), reads=[("identb",), ("cvec",)], writes=[("ar", "dg", a % 2, k)])
                    for s in range(NS):
                        pZ, pZk = bank()
                        rk = [("ar", "z", a, s_) for s_ in (s - 1, s, s + 1) if 0 <= s_ < NS]
                        rk += [("ar", "zpadl", a), ("ar", "zpadr", a)] + [("ar", "dg", a % 2, k) for k in range(31)]
                        mm_group(pZ[:, :N], pZk, [(dgt[:, k, :], zb[:, a, s * N + k:s * N + k + N]) for k in range(31)], rk)
                        P.op("act", lambda e, pZ=pZ, a=a, s=s: e.activation(
                            out=hs[:, a, cols(s)], in_=pZ[:, :N], func=AF.Identity, bias=cv_ap(("bdw", i), a, 1), scale=1.0),
                            reads=[pZk, ("cvec",)], writes=[("h", a, s)])
                c1 = AR.take(N, F32)
                c2 = AR.take(N, F32)
                c3 = AR.take(N, F32)

                def rn(s):
                    return (2, 3) if s % 2 == 0 else (4, 5)

                def ln_stats(s):
                    for a in range(KC):
                        sqt, sqk = sq_tile()
                        P.op("dve", lambda e, a=a, s=s, sqt=sqt: e.tensor_tensor(
                            out=sqt[:, :], in0=hs[:, a, cols(s)], in1=hs[:, a, cols(s)], op=ALU.mult),
                            reads=[("h", a, s)], writes=[sqk])
                        P.op("pe", lambda e, a=a, s=s: e.matmul(ps[0][:, :N], onesD[:, :], hs[:, a, cols(s)], start=(a == 0), stop=(a == KC - 1)),
                             reads=[("h", a, s), ("onesD",)], writes=[("ps", 0)])
                        P.op("pe", lambda e, a=a, sqt=sqt: e.matmul(ps[1][:, :N], onesD[:, :], sqt[:, :], start=(a == 0), stop=(a == KC - 1)),
                             reads=[sqk, ("onesD",)], writes=[("ps", 1)])

                def chainA(s):
                    P.op("act", lambda e: e.activation(out=c1[:, :], in_=ps[0][:, :N], func=AF.Square),
                         reads=[("ps", 0)], writes=[("ar", "c1")])
                    P.op("act", lambda e: e.activation(out=c3[:, :], in_=ps[0][:, :N], func=AF.Copy),
                         reads=[("ps", 0)], writes=[("ar", "c3")])

                def chainB(s):
                    P.op("dve", lambda e: e.tensor_tensor(out=c2[:, :], in0=ps[1][:, :N], in1=c1[:, :], op=ALU.subtract),
                         reads=[("ps", 1), ("ar", "c1")], writes=[("ar", "c2")])

                def chainC(s):
                    rb, nb_ = rn(s)
                    P.op("act", lambda e: e.activation(out=c1[:, :], in_=c2[:, :], func=AF.Ln, bias=eps_l[:, 0:1], scale=1.0),
                         reads=[("ar", "c2"), ("eps",)], writes=[("ar", "c1")])
                    P.op("act", lambda e, rb=rb: e.activation(out=ps[rb][:, :N], in_=c1[:, :], func=AF.Exp, scale=-0.5),
                         reads=[("ar", "c1")], writes=[("ps", rb)])
                    P.op("act", lambda e: e.activation(out=c2[:, :], in_=c1[:, :], func=AF.Exp, scale=-0.5),
                         reads=[("ar", "c1")], writes=[("ar", "c2")])

                def chainD(s):
                    rb, nb_ = rn(s)
                    P.op("dve", lambda e, nb_=nb_: e.scalar_tensor_tensor(
                        out=ps[nb_][:, :N], in0=c3[:, :], scalar=-1.0, in1=c2[:, :], op0=ALU.mult, op1=ALU.mult),
                        reads=[("ar", "c3"), ("ar", "c2")], writes=[("ps", nb_)])

                def ln_norm_half(s, half):
                    rb, nb_ = rn(s)
                    for a in range(half * 4, half * 4 + 4):
                        ta, tak = tmp_tile()
                        P.op("dve", lambda e, a=a, s=s, ta=ta, rb=rb: e.tensor_tensor(
                            out=ta[:, :], in0=hs[:, a, cols(s)], in1=ps[rb][:, :N], op=ALU.mult),
                            reads=[("h", a, s), ("ps", rb)], writes=[tak])
                        P.op("dve", lambda e, ta=ta, nb_=nb_: e.tensor_tensor(out=ta[:, :], in0=ta[:, :], in1=ps[nb_][:, :N], op=ALU.add),
                             reads=[tak, ("ps", nb_)], writes=[tak])
                        P.op("act", lambda e, a=a, s=s, ta=ta: e.activation(
                            out=zb[:, a, ZP + s * N:ZP + (s + 1) * N], in_=ta[:, :], func=AF.Silu,
                            bias=cv_ap(("lnb", i), a, 1), scale=cv_ap(("lng", i), a, 1)),
                            reads=[tak, ("cvec",)], writes=[("ar", "z", a, s)])

                ln_stats(0)
                chainA(0)
                chainB(0)
                chainC(0)
                chainD(0)
                for s in range(NS):
                    nxt = s + 1 < NS
                    if nxt:
                        ln_stats(s + 1)
                        chainA(s + 1)
                    ln_norm_half(s, 0)
                    if nxt:
                        chainB(s + 1)
                        chainC(s + 1)
                    ln_norm_half(s, 1)
                    if nxt:
                        chainD(s + 1)
                P.op("dve", lambda e: e.tensor_tensor(out=gbt[:, :], in0=dv[:, l, 2, :], in1=cv_ap(("bpw2", i), 0, 8), op=ALU.mult),
                     reads=dvk(l) + [("cvec",)], writes=[("gbt",)])
                for m in range(KC):
                    P.op("pool", lambda e, m=m: e.tensor_scalar(
                        out=xs[:, m, :], in0=xs[:, m, :], scalar1=1.0, scalar2=gbt[:, m:m + 1], op0=ALU.mult, op1=ALU.add),
                        reads=[("gbt",)] + [("x", m, s_) for s_ in range(NS)], writes=[("x", m, s_) for s_ in range(NS)])
                proj_residual(l, lambda m: cf_w_pw2[i][:, m * 128:(m + 1) * 128], KC,
                              lambda k, s: zb[:, k, ZP + s * N:ZP + (s + 1) * N], lambda k, s: ("ar", "z", k, s), 2,
                              acc_stats=True)

            def ffn(l, norm_call):
                P.mark("ffn")
                P.new_epoch()
                AR.reset()
                ab = AR.take(JG * TB, BF16).rearrange("p (a b) -> p a b", a=JG)
                for hf in range(NG):
                    for jj in range(JG):
                        j = hf * JG + jj
                        (wgt, wgk), (wut, wuk) = get_tiles([(ffn_w_gate[l][:, j * 128:(j + 1) * 128], KC),
                                                            (ffn_w_up[l][:, j * 128:(j + 1) * 128], KC)])

                        def gu_step(s, jj=jj, wgt=wgt, wgk=wgk, wut=wut, wuk=wuk):
                            pG, pGk = bank()
                            pU, pUk = bank()
                            hk = [("h", k, s) for k in range(KC)]
                            mm_group(pG[:, :N], pGk, [(wgt[:, k, :], hs[:, k, cols(s)]) for k in range(KC)], [wgk] + hk)
                            mm_group(pU[:, :N], pUk, [(wut[:, k, :], hs[:, k, cols(s)]) for k in range(KC)], [wuk] + hk)
                            tt, tk = tmp_tile()
                            P.op("act", lambda e, tt=tt, pG=pG: e.activation(out=tt[:, :], in_=pG[:, :N], func=AF.Silu),
                                 reads=[pGk], writes=[tk])
                            P.op("dve", lambda e, tt=tt, pU=pU, jj=jj, s=s: e.tensor_tensor(
                                out=ab[:, jj, cols(s)], in0=tt[:, :], in1=pU[:, :N], op=ALU.mult),
                                reads=[tk, pUk], writes=[("ar", "a", jj, s)])
                        if j == 0:
                            norm_call(gu_step)
                        else:
                            for s in range(NS):
                                gu_step(s)
                        wrelease(2)
                    proj_residual(l, lambda m, hf=hf: ffn_w_down[l][hf * JG * 128:(hf + 1) * JG * 128, m * 128:(m + 1) * 128], JG,
                                  lambda k, s: ab[:, k, cols(s)], lambda k, s: ("ar", "a", k, s), 5,
                                  acc_stats=(hf == NG - 1))

            def final_out(blk):
                P.mark("final")
                P.new_epoch()
                AR.reset()
                ot = [AR.take(N, F32) for _ in range(4)]
                oi = 0
                for s in range(NS):
                    if len(layers) > 0 and stop is None:
                        rb = rstd_from_acc(s)
                    else:
                        stats_rstd(lambda m, s_: xs[:, m, cols(s_)], lambda m, s_: ("x", m, s_), s, eps_r)
                        rb = 2
                    lo = max(s * N, HALO)
                    hi = min((s + 1) * N, HALO + TOK)
                    for m in range(KC):
                        o = oi % 4
                        oi += 1
                        P.op("dve", lambda e, m=m, s=s, o=o, rb=rb: e.scalar_tensor_tensor(
                            out=ot[o][:, :], in0=xs[:, m, cols(s)], scalar=cv_ap("gfin", m, 1), in1=ps[rb][:, :N],
                            op0=ALU.mult, op1=ALU.mult),
                            reads=[("x", m, s), ("ps", rb), ("cvec",)], writes=[("ar", "ot", o)])
                        P.dma("sp", f"o{o}", lambda e, m=m, s=s, o=o, lo=lo, hi=hi: e.dma_start(
                            out=yT[blk, m * 128:(m + 1) * 128, lo - HALO:hi - HALO], in_=ot[o][:, lo - s * N:hi - s * N]),
                            reads=[("ar", "ot", o)])

            def raw_out(blk):
                P.new_epoch()
                for m in range(KC):
                    P.dma("sp", f"o{m % 4}", lambda e, m=m: e.dma_start(
                        out=yT[blk, m * 128:(m + 1) * 128, :], in_=xs[:, m, HALO:HALO + TOK]),
                        reads=[("x", m, s) for s in range(NS)])

            for blk in range(nblocks):
                for m in range(KC):
                    P.dma("sp", f"x{m}", lambda e, m=m, blk=blk: e.dma_start(out=xs[:, m, :], in_=xT[blk, m * 128:(m + 1) * 128, :]),
                          writes=[("x", m, s) for s in range(NS)])
                for li, l in enumerate(layers):
                    if stop == "load":
                        break
                    if blk == 0:
                        drain_bg()
                        if li + 1 < len(layers):
                            enqueue_mod(layers[li + 1])
                    nm1 = lambda cb, l=l, li=li: norm_mod(l, 0, 1, have_stats=(li > 0), after_sub=cb)
                    if stop == "norm":
                        nm1(None)
                    if stop == "norm":
                        for m in range(KC):
                            tt, tk = tmp_tile()
                            P.op("dve", lambda e, m=m, tt=tt: e.tensor_copy(out=tt[:, :], in_=hs[:, m, 0:N]), reads=[("h", m, 0)], writes=[tk])
                            P.dma("sp", "o0", lambda e, m=m, tt=tt: e.dma_start(out=dbg_d[:, m * N:(m + 1) * N], in_=tt[:, :]), reads=[tk])
                        P.dma("sp", "o1", lambda e: e.dma_start(out=dbg_d[:, 8 * N:8 * N + 192], in_=dv[:, :, :, :].rearrange("p a b c -> p (a b c)")), reads=dvk(l))
                        P.dma("sp", "o1", lambda e: e.dma_start(out=dbg_d[:, 8 * N + 192:8 * N + 384], in_=modv[:, :, :].rearrange("p a b -> p (a b)")), reads=[("modv", l)])
                        break
                    if l % 2 == 0:
                        mixer_even(l, blk, nm1)
                    else:
                        mixer_odd(l, blk, nm1)
                    if stop == "mixer":
                        break
                    ffn(l, lambda cb, l=l: norm_mod(l, 3, 4, have_stats=True, after_sub=cb))
                if final_norm:
                    final_out(blk)
                else:
                    raw_out(blk)
                excl4[0] = False
            return wstate["specs"]

        Pd = Prog()
        plan = run(Pd, None)
        P = Prog()
        plan2 = run(P, plan)
        assert len(plan2) == len(plan)

        final_waits = {s: c for s, c in P.dmacnt.items() if s.startswith("o")}

        engmap = {"pe": "tensor", "act": "scalar", "dve": "vector", "pool": "gpsimd", "sp": "sync"}
        with nc.Block() as block:
            def make(engname):
                def body(e):
                    for item in P.q[engname]:
                        if item[0] == "wait":
                            e.wait_ge(sems[item[1]], item[2])
                        elif item[0] == "op":
                            ins = item[1](e)
                            ins.then_inc(sems[engname], 1)
                        else:
                            ins = item[1](e)
                            ins.then_inc(sems[item[2]], 16)
                    if engname == "sp":
                        for s_, c_ in final_waits.items():
                            e.wait_ge(sems[s_], c_)
                        for en in ("pe", "act", "dve"):
                            if P.cnt[en] > 0:
                                e.wait_ge(sems[en], P.cnt[en])
                return body
            block.tensor(make("pe"))
            block.scalar(make("act"))
            block.vector(make("dve"))
            block.gpsimd(make("pool"))
            block.sync(make("sp"))
        stats = {e: len(P.q[e]) for e in Prog.ENG}
        PHASE_MARKS[:] = P.marks
    return nc, stats


def _fm(v):
    v = np.asarray(v, np.float32)
    lead = v.shape[:-1]
    n = v.shape[-1] // 128
    v = v.reshape(lead + (n, 128))
    return np.moveaxis(v, -1, 0)


def _build_cvec(inp, b):
    cv = np.zeros((128, NV), np.float32)

    def put(name, arr):
        arr = np.asarray(arr, np.float32).reshape(128, -1)
        cv[:, CV[name]:CV[name] + arr.shape[1]] = arr
    for l in range(DEPTH):
        put(("gmix", l), _fm(inp["norm_mix_g"][l]))
        put(("gffn", l), _fm(inp["norm_ffn_g"][l]))
        put(("bmod", l), _fm(inp["b_mod"][l]))
    for i in range(2):
        put(("conv", i), _fm(inp["ab_conv"][i]))
        put(("pscale", i), _fm(inp["ab_pool_scale"][i]))
        put(("bpw1", i), _fm(inp["cf_b_pw1"][i]))
        put(("wdw", i), _fm(inp["cf_w_dw"][i]))
        put(("bdw", i), _fm(inp["cf_b_dw"][i]))
        put(("lng", i), _fm(inp["cf_ln_g"][i]))
        put(("lnb", i), _fm(inp["cf_ln_b"][i]))
        put(("bpw2", i), _fm(inp["cf_b_pw2"][i]))
    put("gfin", _fm(inp["final_norm_g"]))
    put("c", _fm(inp["c"][b]))
    return cv


def _build_edge(half):
    e = np.zeros((NE,), np.float32)
    for blk in range(NB):
        s0 = half * (S // 2) + blk * TOK - HALO
        pos = s0 + np.arange(TB)
        valid = ((pos >= 0) & (pos < S)).astype(np.float32)
        o = blk * EB
        e[o:o + EDGE] = valid[:EDGE]
        e[o + EDGE:o + 2 * EDGE] = valid[TB - EDGE:]
        for g in range(4):
            w = 2 << g
            left = w // 2
            right = w - 1 - left
            cnt = (np.minimum(pos + right, S - 1) - np.maximum(pos - left, 0) + 1).astype(np.float32)
            inv = np.where((pos >= 0) & (pos < S), 1.0 / np.maximum(cnt, 1.0), 1.0 / w).astype(np.float32)
            oo = o + 2 * EDGE + g * 2 * PEDGE
            e[oo:oo + PEDGE] = inv[POFF:POFF + PEDGE]
            e[oo + PEDGE:oo + 2 * PEDGE] = inv[TB - POFF - PEDGE:TB - POFF]
    return np.ascontiguousarray(np.broadcast_to(e[None, :], (128, NE)))


_CACHE = {}


def _get_nc(layers, final_norm):
    key = (tuple(layers), final_norm)
    if key not in _CACHE:
        _CACHE[key] = build_program(layers, final_norm)
    return _CACHE[key][0]


def _make_in_maps(inp, x_full):
    f32 = lambda a: np.ascontiguousarray(np.asarray(a, np.float32))
    shared = {k: f32(inp[k]) for k in ("w_mod", "ab_w_in", "ab_w_pool", "ab_w_out", "cf_w_pw1", "cf_w_pw2",
                                       "ffn_w_gate", "ffn_w_up", "ffn_w_down")}
    ident = np.eye(128, dtype=np.float32)
    in_maps = []
    for cid in range(NCORES):
        b, half = cid // 2, cid % 2
        xt = np.zeros((NB, D, TB), np.float32)
        for blk in range(NB):
            s0 = half * (S // 2) + blk * TOK - HALO
            lo, hi = max(s0, 0), min(s0 + TB, S)
            xt[blk, :, lo - s0:hi - s0] = x_full[b, lo:hi, :].T
        m = dict(shared)
        m.update({"xT": xt, "cvec": _build_cvec(inp, b), "edge": _build_edge(half), "ident": ident})
        in_maps.append(m)
    return in_maps


def _gather(res):
    out = np.empty((BATCH, S, D), np.float32)
    for cid in range(NCORES):
        b, half = cid // 2, cid % 2
        y = res.results[cid]["yT"]
        for blk in range(NB):
            t0 = half * (S // 2) + blk * TOK
            out[b, t0:t0 + TOK, :] = y[blk].T
    return out


def kernel(**inputs):
    inp = {k: np.asarray(v) for k, v in inputs.items()}
    x = np.asarray(inp["x"], np.float32)
    nc = _get_nc((0, 1, 2, 3), True)
    in_maps = _make_in_maps(inp, x)
    res = run_bass_kernel_spmd(nc, in_maps, core_ids=list(range(NCORES)))
    return _gather(res)
```
